# Optimizing a Trainium2 kernel written in Bass

```python
import jax
import jax.numpy as jnp
from jax import lax
import numpy as np

D_MODEL = 1024
BATCH = 2
SEQ = 16384
DEPTH = 4

CTX_LEN = 256
GRID_W = 64
EPS = 1e-6
NEG_INF = -1e30
ROPE_BASE = 10000.0
CHUNK = 64

GLA_HEADS = 4
GLA_DK = 64
GLA_DV = 128
GLA_RANK = 16
GLA_TEMP = 16.0
LRU_WIDTH = 512
LRU_BLOCKS = 4
LRU_CONV = 4
LRU_C = 8.0
HEAD_DIM = 64
SWA_QH = 8
SWA_KVH = 2
WINDOW = 128
BLOCK = 128
RET_HEADS = 4
RET_DK = 64
RET_DV = 128

EVEN_SIZES = (GLA_HEADS * GLA_DK, GLA_HEADS * GLA_DK, GLA_HEADS * GLA_DV, GLA_HEADS * GLA_DV,
              2 * GLA_RANK, LRU_WIDTH, LRU_WIDTH)
ODD_SIZES = (SWA_QH * HEAD_DIM, SWA_KVH * HEAD_DIM, SWA_KVH * HEAD_DIM, SWA_QH * HEAD_DIM,
             RET_HEADS * RET_DK, RET_HEADS * RET_DK, RET_HEADS * RET_DV, RET_HEADS * RET_DV)
EVEN_IN = sum(EVEN_SIZES)
ODD_IN = sum(ODD_SIZES)
EVEN_MIX = GLA_HEADS * GLA_DV + LRU_WIDTH
ODD_MIX = SWA_QH * HEAD_DIM + RET_HEADS * RET_DV

kernel_name = "hybrid_gla_rglru_swa_retention_dit"


def rms_norm(x, g):
    xf = x.astype(jnp.float32)
    y = xf * lax.rsqrt(jnp.mean(xf * xf, axis=-1, keepdims=True) + EPS)
    return (y * g.astype(jnp.float32)).astype(x.dtype)


def head_rms_norm(o, g):
    B, L = o.shape[:2]
    return rms_norm(o, g).reshape(B, L, -1)


def split_cols(z, sizes):
    return jnp.split(z, np.cumsum(sizes)[:-1].tolist(), axis=-1)


def axial_rope_tables(row, col):
    n_freq = HEAD_DIM // 4
    inv = ROPE_BASE ** (-jnp.arange(n_freq, dtype=jnp.float32) / n_freq)
    ang = jnp.concatenate([row.astype(jnp.float32)[:, None] * inv[None],
                           col.astype(jnp.float32)[:, None] * inv[None]], axis=-1)
    return jnp.cos(ang), jnp.sin(ang)


def apply_rope(x, cos, sin):
    half = x.shape[-1] // 2
    x1, x2 = x[..., :half], x[..., half:]
    c, s = cos[None, :, None, :], sin[None, :, None, :]
    return jnp.concatenate([x1 * c - x2 * s, x1 * s + x2 * c], axis=-1).astype(x.dtype)


def chunked_gla(q, k, v, log_a, s0):
    B, L, H, dk = q.shape
    dv = v.shape[-1]
    n = L // CHUNK

    def to_chunks(z):
        return z.reshape(B, n, CHUNK, H, z.shape[-1]).astype(jnp.float32)

    qc, kc, vc, gc = to_chunks(q), to_chunks(k), to_chunks(v), to_chunks(log_a)
    b = jnp.cumsum(gc, axis=2)
    b_last = b[:, :, -1:]
    q_in = qc * jnp.exp(b)
    k_in = kc * jnp.exp(-b)
    k_st = kc * jnp.exp(b_last - b)
    mask = jnp.tril(jnp.ones((CHUNK, CHUNK), dtype=bool))
    att = jnp.where(mask, jnp.einsum('bnthd,bnshd->bnhts', q_in, k_in), 0.0)
    o_intra = jnp.einsum('bnhts,bnshv->bnthv', att, vc)
    d_state = jnp.einsum('bnshd,bnshv->bnhdv', k_st, vc)
    decay = jnp.exp(b_last[:, :, 0])

    def step(S, inp):
        ds_n, dec_n = inp
        return dec_n[..., None] * S + ds_n, S

    s_final, s_prev = lax.scan(step, s0.astype(jnp.float32),
                               (jnp.moveaxis(d_state, 1, 0), jnp.moveaxis(decay, 1, 0)))
    s_prev = jnp.moveaxis(s_prev, 0, 1)
    o_inter = jnp.einsum('bnthd,bnhdv->bnthv', q_in, s_prev)
    o = (o_intra + o_inter).reshape(B, L, H, dv)
    return o.astype(v.dtype), s_final


def bidir_chunked(qc, kc, vc, gc_fw, gc_bw, ql, kl, vl, gl_fw, gl_bw):
    B, _, H, dk = qc.shape
    dv = vc.shape[-1]
    s0 = jnp.zeros((B, H, dk, dv), jnp.float32)
    flip = lambda z: jnp.flip(z, axis=1)
    oc_f, sc_f = chunked_gla(qc, kc, vc, gc_fw, s0)
    oc_b, sc_b = chunked_gla(flip(qc), flip(kc), flip(vc), flip(gc_bw), s0)
    ol_f, _ = chunked_gla(ql, kl, vl, gl_fw, sc_f)
    ol_b, _ = chunked_gla(flip(ql), flip(kl), flip(vl), flip(gl_bw), sc_b)
    return oc_f + flip(oc_b), ol_f + flip(ol_b)


def centred_depthwise_conv(x, w, b):
    K, C = w.shape
    left = K // 2
    y = lax.conv_general_dilated(x, w[:, None, :].astype(x.dtype), window_strides=(1,),
                                 padding=[(left, K - 1 - left)],
                                 dimension_numbers=('NWC', 'WIO', 'NWC'), feature_group_count=C)
    return y + b.astype(x.dtype)


def rg_lru_coeffs(xc, wa, ba, wx, bx, lam):
    B, L, W = xc.shape
    xb = xc.reshape(B, L, LRU_BLOCKS, W // LRU_BLOCKS)
    r = jax.nn.sigmoid(jnp.einsum('blni,nij->blnj', xb, wa).reshape(B, L, W) + ba)
    i_g = jax.nn.sigmoid(jnp.einsum('blni,nij->blnj', xb, wx).reshape(B, L, W) + bx)
    log_a = (LRU_C * r.astype(jnp.float32)) * jax.nn.log_sigmoid(lam.astype(jnp.float32))
    a = jnp.exp(log_a)
    u = jnp.sqrt(-jnp.expm1(2.0 * log_a)) * (i_g * xc).astype(jnp.float32)
    return a, u


def linear_recurrence(a, u, h0):
    def combine(l, r):
        al, ul = l
        ar, ur = r
        return al * ar, ar * ul + ur
    a_cum, h = lax.associative_scan(combine, (a, u), axis=1)
    h = h + a_cum * h0[:, None]
    return h, h[:, -1]


def bidir_lru(c_fw, c_bw, l_fw, l_bw):
    flip = lambda z: jnp.flip(z, axis=1)
    B, _, W = c_fw[0].shape
    h0 = jnp.zeros((B, W), jnp.float32)
    hc_f, sc_f = linear_recurrence(c_fw[0], c_fw[1], h0)
    hc_b, sc_b = linear_recurrence(flip(c_bw[0]), flip(c_bw[1]), h0)
    hl_f, _ = linear_recurrence(l_fw[0], l_fw[1], sc_f)
    hl_b, _ = linear_recurrence(flip(l_bw[0]), flip(l_bw[1]), sc_b)
    return hc_f + flip(hc_b), hl_f + flip(hl_b)


def gla_lru_mixer(hc, hl, w_in, gla_up_fw, gla_b_fw, gla_up_bw, gla_b_bw, gla_norm_g,
                  conv_w, conv_b, wa_fw, ba_fw, wx_fw, bx_fw, lam_fw,
                  wa_bw, ba_bw, wx_bw, bx_bw, lam_bw):
    def project(h):
        B, L, _ = h.shape
        q, k, v, g_gla, lr, xr, g_lru = split_cols(h @ w_in, EVEN_SIZES)
        q = q.reshape(B, L, GLA_HEADS, GLA_DK) * (GLA_DK ** -0.5)
        k = k.reshape(B, L, GLA_HEADS, GLA_DK)
        v = v.reshape(B, L, GLA_HEADS, GLA_DV)
        lr = lr.astype(jnp.float32)
        la_fw = (jax.nn.log_sigmoid(lr[..., :GLA_RANK] @ gla_up_fw + gla_b_fw) / GLA_TEMP
                 ).reshape(B, L, GLA_HEADS, GLA_DK)
        la_bw = (jax.nn.log_sigmoid(lr[..., GLA_RANK:] @ gla_up_bw + gla_b_bw) / GLA_TEMP
                 ).reshape(B, L, GLA_HEADS, GLA_DK)
        xc = centred_depthwise_conv(xr, conv_w, conv_b)
        co_fw = rg_lru_coeffs(xc, wa_fw, ba_fw, wx_fw, bx_fw, lam_fw)
        co_bw = rg_lru_coeffs(xc, wa_bw, ba_bw, wx_bw, bx_bw, lam_bw)
        return (q, k, v, la_fw, la_bw, g_gla), (co_fw, co_bw, g_lru)

    (qc, kc, vc, fc, bc, ggc), (lcf, lcb, glc) = project(hc)
    (ql, kl, vl, fl, bl, ggl), (llf, llb, gll) = project(hl)
    gla_c, gla_l = bidir_chunked(qc, kc, vc, fc, bc, ql, kl, vl, fl, bl)
    lru_c, lru_l = bidir_lru(lcf, lcb, llf, llb)
    y_c = jnp.concatenate([head_rms_norm(gla_c, gla_norm_g) * jax.nn.silu(ggc),
                           lru_c.astype(hc.dtype) * jax.nn.silu(glc)], axis=-1)
    y_l = jnp.concatenate([head_rms_norm(gla_l, gla_norm_g) * jax.nn.silu(ggl),
                           lru_l.astype(hl.dtype) * jax.nn.silu(gll)], axis=-1)
    return y_c, y_l


def context_attention(qc, kc, vc, sink):
    B, Lc, Hq, dh = qc.shape
    Hkv = kc.shape[2]
    G = Hq // Hkv
    qg = qc.reshape(B, Lc, Hkv, G, dh)
    s = jnp.einsum('bqhgd,bkhd->bhgqk', qg, kc).astype(jnp.float32) * (dh ** -0.5)
    sk = jnp.broadcast_to(sink.astype(jnp.float32).reshape(1, Hkv, G, 1, 1), (B, Hkv, G, Lc, 1))
    p = jax.nn.softmax(jnp.concatenate([s, sk], axis=-1), axis=-1)[..., :Lc].astype(vc.dtype)
    return jnp.einsum('bhgqk,bkhd->bqhgd', p, vc).reshape(B, Lc, Hq * dh)


def windowed_attention(q, k, v, kc, vc, sink):
    B, L, Hq, dh = q.shape
    Hkv = k.shape[2]
    G = Hq // Hkv
    Lc = kc.shape[1]
    nb = L // BLOCK
    pad = ((0, 0), (BLOCK, BLOCK), (0, 0), (0, 0))
    kp, vp = jnp.pad(k, pad), jnp.pad(v, pad)
    scale = dh ** -0.5
    sk = jnp.broadcast_to(sink.astype(jnp.float32).reshape(1, Hkv, G, 1, 1), (B, Hkv, G, BLOCK, 1))
    q_off = jnp.arange(BLOCK)[:, None]
    k_off = jnp.arange(3 * BLOCK)[None, :] - BLOCK
    near = jnp.abs(k_off - q_off) <= WINDOW

    def one_block(i):
        start = i * BLOCK
        qi = lax.dynamic_slice_in_dim(q, start, BLOCK, axis=1).reshape(B, BLOCK, Hkv, G, dh)
        ki = lax.dynamic_slice_in_dim(kp, start, 3 * BLOCK, axis=1)
        vi = lax.dynamic_slice_in_dim(vp, start, 3 * BLOCK, axis=1)
        kpos = start + k_off
        valid = near & (kpos >= 0) & (kpos < L)
        s_loc = jnp.einsum('bqhgd,bkhd->bhgqk', qi, ki).astype(jnp.float32) * scale
        s_loc = jnp.where(valid, s_loc, NEG_INF)
        s_ctx = jnp.einsum('bqhgd,bkhd->bhgqk', qi, kc).astype(jnp.float32) * scale
        p = jax.nn.softmax(jnp.concatenate([s_loc, s_ctx, sk], axis=-1), axis=-1).astype(v.dtype)
        o = (jnp.einsum('bhgqk,bkhd->bqhgd', p[..., :3 * BLOCK], vi)
             + jnp.einsum('bhgqk,bkhd->bqhgd', p[..., 3 * BLOCK:3 * BLOCK + Lc], vc))
        return o.reshape(B, BLOCK, Hq * dh)

    o = lax.map(one_block, jnp.arange(nb))
    return jnp.moveaxis(o, 0, 1).reshape(B, L, Hq * dh)


def swa_ret_mixer(hc, hl, cos, sin, w_in, sink, dec_fw, dec_bw, ret_norm_g):
    def project(h, rope):
        B, L, _ = h.shape
        q, k, v, g_swa, rq, rk, rv, g_ret = split_cols(h @ w_in, ODD_SIZES)
        q = q.reshape(B, L, SWA_QH, HEAD_DIM)
        k = k.reshape(B, L, SWA_KVH, HEAD_DIM)
        v = v.reshape(B, L, SWA_KVH, HEAD_DIM)
        rq = rq.reshape(B, L, RET_HEADS, RET_DK)
        rk = rk.reshape(B, L, RET_HEADS, RET_DK)
        rv = rv.reshape(B, L, RET_HEADS, RET_DV)
        if rope:
            q, k, rq, rk = (apply_rope(t, cos, sin) for t in (q, k, rq, rk))
        rk = rk * (RET_DK ** -0.5)
        return q, k, v, g_swa, rq, rk, rv, g_ret

    qc, kc, vc, gsc, rqc, rkc, rvc, grc = project(hc, False)
    ql, kl, vl, gsl, rql, rkl, rvl, grl = project(hl, True)
    att_c = context_attention(qc, kc, vc, sink)
    att_l = windowed_attention(ql, kl, vl, kc, vc, sink)

    def log_decay(logit, ref):
        return jnp.broadcast_to(jax.nn.log_sigmoid(logit.astype(jnp.float32))[:, None], ref.shape)

    ret_c, ret_l = bidir_chunked(rqc, rkc, rvc, log_decay(dec_fw, rkc), log_decay(dec_bw, rkc),
                                 rql, rkl, rvl, log_decay(dec_fw, rkl), log_decay(dec_bw, rkl))
    y_c = jnp.concatenate([att_c * jax.nn.silu(gsc),
                           head_rms_norm(ret_c, ret_norm_g) * jax.nn.silu(grc)], axis=-1)
    y_l = jnp.concatenate([att_l * jax.nn.silu(gsl),
                           head_rms_norm(ret_l, ret_norm_g) * jax.nn.silu(grl)], axis=-1)
    return y_c, y_l


def setup_inputs(seed: int = 0) -> dict:
    key = jax.random.key(seed)
    ks = iter(jax.random.split(key, 48))
    D = D_MODEL
    NE = (DEPTH + 1) // 2
    NO = DEPTH // 2
    lb = LRU_WIDTH // LRU_BLOCKS

    def nrm(shape, s):
        return jax.random.normal(next(ks), shape, jnp.float32) * s

    def lru_lambda():
        a0 = jax.random.uniform(next(ks), (NE, LRU_WIDTH), jnp.float32, 0.9, 0.999)
        u = a0 ** (1.0 / LRU_C)
        return jnp.log(u) - jnp.log1p(-u)

    m = 5.0 + jnp.arange(RET_HEADS, dtype=jnp.float32)
    ret_logit = jnp.log(2.0 ** m - 1.0)
    return {
        "x": nrm((BATCH, SEQ, D), 1.0),
        "c": nrm((BATCH, D), 1.0),
        "ctx": nrm((BATCH, CTX_LEN, D), 1.0),
        "c_ctx": nrm((D,), 1.0),
        "ada_w": nrm((DEPTH, D, 3 * D), 0.5 * D ** -0.5),
        "ada_b": nrm((DEPTH, 3 * D), 0.02),
        "norm_g": 1.0 + nrm((DEPTH, D), 0.02),
        "e_w_in": nrm((NE, D, EVEN_IN), D ** -0.5),
        "gla_up_fw": nrm((NE, GLA_RANK, GLA_HEADS * GLA_DK), GLA_RANK ** -0.5),
        "gla_b_fw": nrm((NE, GLA_HEADS * GLA_DK), 0.1),
        "gla_up_bw": nrm((NE, GLA_RANK, GLA_HEADS * GLA_DK), GLA_RANK ** -0.5),
        "gla_b_bw": nrm((NE, GLA_HEADS * GLA_DK), 0.1),
        "gla_norm_g": 1.0 + nrm((NE, GLA_DV), 0.02),
        "lru_conv_w": nrm((NE, LRU_CONV, LRU_WIDTH), LRU_CONV ** -0.5),
        "lru_conv_b": nrm((NE, LRU_WIDTH), 0.02),
        "lru_wa_fw": nrm((NE, LRU_BLOCKS, lb, lb), lb ** -0.5),
        "lru_ba_fw": nrm((NE, LRU_WIDTH), 0.02),
        "lru_wx_fw": nrm((NE, LRU_BLOCKS, lb, lb), lb ** -0.5),
        "lru_bx_fw": nrm((NE, LRU_WIDTH), 0.02),
        "lru_lam_fw": lru_lambda(),
        "lru_wa_bw": nrm((NE, LRU_BLOCKS, lb, lb), lb ** -0.5),
        "lru_ba_bw": nrm((NE, LRU_WIDTH), 0.02),
        "lru_wx_bw": nrm((NE, LRU_BLOCKS, lb, lb), lb ** -0.5),
        "lru_bx_bw": nrm((NE, LRU_WIDTH), 0.02),
        "lru_lam_bw": lru_lambda(),
        "e_w_out": nrm((NE, EVEN_MIX, D), EVEN_MIX ** -0.5),
        "o_w_in": nrm((NO, D, ODD_IN), D ** -0.5),
        "swa_sink": nrm((NO, SWA_QH), 0.5),
        "ret_dec_fw": ret_logit[None] + nrm((NO, RET_HEADS), 0.05),
        "ret_dec_bw": ret_logit[None] + nrm((NO, RET_HEADS), 0.05),
        "ret_norm_g": 1.0 + nrm((NO, RET_DV), 0.02),
        "o_w_out": nrm((NO, ODD_MIX, D), ODD_MIX ** -0.5),
        "final_g": 1.0 + nrm((D,), 0.02),
    }


def reference(x, c, ctx, c_ctx, ada_w, ada_b, norm_g, e_w_in, gla_up_fw, gla_b_fw, gla_up_bw, gla_b_bw,
              gla_norm_g, lru_conv_w, lru_conv_b, lru_wa_fw, lru_ba_fw, lru_wx_fw, lru_bx_fw, lru_lam_fw,
              lru_wa_bw, lru_ba_bw, lru_wx_bw, lru_bx_bw, lru_lam_bw, e_w_out, o_w_in, swa_sink,
              ret_dec_fw, ret_dec_bw, ret_norm_g, o_w_out, final_g):
    n_tok = x.shape[1]
    rows = n_tok // GRID_W
    row = jnp.repeat(jnp.arange(rows), GRID_W)
    col = jnp.tile(jnp.arange(GRID_W), rows)
    cos, sin = axial_rope_tables(row, col)
    D = x.shape[-1]
    ctx_h = ctx
    for i in range(DEPTH):
        mod_l = jax.nn.silu(c) @ ada_w[i] + ada_b[i]
        mod_c = jax.nn.silu(c_ctx) @ ada_w[i] + ada_b[i]
        sh_l, sc_l, g_l = mod_l[:, :D], mod_l[:, D:2 * D], mod_l[:, 2 * D:]
        sh_c, sc_c, g_c = mod_c[:D], mod_c[D:2 * D], mod_c[2 * D:]
        hl = rms_norm(x, norm_g[i]) * (1.0 + sc_l[:, None]) + sh_l[:, None]
        hc = rms_norm(ctx_h, norm_g[i]) * (1.0 + sc_c) + sh_c
        j = i // 2
        if i % 2 == 0:
            y_c, y_l = gla_lru_mixer(hc, hl, e_w_in[j], gla_up_fw[j], gla_b_fw[j], gla_up_bw[j], gla_b_bw[j],
                                     gla_norm_g[j], lru_conv_w[j], lru_conv_b[j],
                                     lru_wa_fw[j], lru_ba_fw[j], lru_wx_fw[j], lru_bx_fw[j], lru_lam_fw[j],
                                     lru_wa_bw[j], lru_ba_bw[j], lru_wx_bw[j], lru_bx_bw[j], lru_lam_bw[j])
            w_out = e_w_out[j]
        else:
            y_c, y_l = swa_ret_mixer(hc, hl, cos, sin, o_w_in[j], swa_sink[j], ret_dec_fw[j], ret_dec_bw[j],
                                     ret_norm_g[j])
            w_out = o_w_out[j]
        x = x + g_l[:, None] * (y_l @ w_out)
        if i < DEPTH - 1:
            ctx_h = ctx_h + g_c * (y_c @ w_out)
    return rms_norm(x, final_g)
```

```python
import numpy as np
import ml_dtypes
from contextlib import ExitStack
import concourse.bass as bass
import concourse.mybir as mybir
from concourse.bass_utils import run_bass_kernel_spmd

F32 = mybir.dt.float32
BF16 = mybir.dt.bfloat16
AF = mybir.ActivationFunctionType
ALU = mybir.AluOpType
AX = mybir.AxisListType

D = 1024
CTX = 256
EPS = 1e-6
SAME_ENG_SYNC = True


class Res:
    __slots__ = ("w", "r")

    def __init__(self):
        self.w = None
        self.r = {}


class V:
    __slots__ = ("ap", "res")

    def __init__(self, ap, res):
        self.ap = ap
        self.res = res

    def __getitem__(self, key):
        return V(self.ap[key], self.res)

    def re(self, s, **kw):
        return V(self.ap.rearrange(s, **kw), self.res)

    def bc(self, shape):
        return V(self.ap.to_broadcast(list(shape)), self.res)


class Prog:
    def __init__(self, nc, es):
        self.nc = nc
        self.es = es
        self.ops = {e: [] for e in ("pe", "act", "dve", "pool", "sp")}
        self.sem = {}
        self.cnt = {}
        for e in ("pe", "act", "dve", "pool"):
            self.sem[e] = es.enter_context(nc.semaphore("s_" + e))
            self.cnt[e] = 0
        self.dsem = {}
        self.dcnt = {}
        self.drr = {}
        for q, n in (("sp", 16), ("pool", 8)):
            self.dsem[q] = [es.enter_context(nc.semaphore("d_%s%d" % (q, i))) for i in range(n)]
            self.dcnt[q] = [0] * n
            self.drr[q] = 0
        self.waited = {e: {} for e in self.ops}
        self.dram_res = {}
        self.n_sb = 0

    def sb(self, shape, dt=F32, name=None):
        self.n_sb += 1
        t = self.es.enter_context(self.nc.sbuf_tensor("sb_" + (name or ("t%d" % self.n_sb)), list(shape), dt))
        return V(t[:] if len(shape) == 2 else t.ap(), [Res()])

    def ps(self, shape, dt=F32, name=None):
        self.n_sb += 1
        t = self.es.enter_context(self.nc.psum_tensor("ps_" + (name or ("t%d" % self.n_sb)), list(shape), dt))
        return V(t.ap(), [Res()])

    def dres(self, name, rowkey, c0, n):
        out = []
        for b in range(c0 // 128, (c0 + n - 1) // 128 + 1):
            k = (name, rowkey, b)
            if k not in self.dram_res:
                self.dram_res[k] = Res()
            out.append(self.dram_res[k])
        return out

    def emit(self, eng, fn, reads, writes, dma=False):
        toks = []
        for v in reads:
            for r in v.res:
                if r.w is not None:
                    toks.append(r.w)
        for v in writes:
            for r in v.res:
                if r.w is not None:
                    toks.append(r.w)
                toks.extend(r.r.values())
        waits = []
        wd = self.waited[eng]
        for (s, val, te) in toks:
            if te == eng and not dma and (eng == "pe" or not SAME_ENG_SYNC):
                continue
            if wd.get(id(s), 0) >= val:
                continue
            wd[id(s)] = val
            waits.append((s, val))
        if dma:
            pool = self.dsem[eng]
            j = self.drr[eng]
            self.drr[eng] = (j + 1) % len(pool)
            s = pool[j]
            k = self.dcnt[eng][j]
            if k > 0 and wd.get(id(s), 0) < 16 * k:
                wd[id(s)] = 16 * k
                waits.append((s, 16 * k))
            self.dcnt[eng][j] = k + 1
            tok = (s, 16 * (k + 1), eng + "q")
            inc = 16
        else:
            self.cnt[eng] += 1
            tok = (self.sem[eng], self.cnt[eng], eng)
            inc = 1
        self.ops[eng].append((waits, fn, tok[0], inc))
        for v in reads:
            for r in v.res:
                r.r[id(tok[0])] = tok
        for v in writes:
            for r in v.res:
                r.w = tok
                r.r = {}
        return tok

    @staticmethod
    def _a(x):
        return x.ap if isinstance(x, V) else x

    @staticmethod
    def _vs(*xs):
        return [x for x in xs if isinstance(x, V)]

    def mm(self, out, lhsT, rhs, start=True, stop=True):
        self.emit("pe", lambda e: e.matmul(out.ap, lhsT.ap, rhs.ap, start=start, stop=stop),
                  [lhsT, rhs], [out])

    def tr(self, out, in_, ident):
        self.emit("pe", lambda e: e.transpose(out.ap, in_.ap, ident.ap), [in_, ident], [out])

    def act(self, out, in_, func, bias=None, scale=None, accum=None, eng="act"):
        kw = {}
        if bias is not None:
            kw["bias"] = self._a(bias)
        if scale is not None:
            kw["scale"] = self._a(scale)
        if accum is not None:
            kw["accum_out"] = accum.ap
        self.emit("act", lambda e: e.activation(out.ap, in_.ap, func, **kw),
                  self._vs(in_, bias, scale), self._vs(out, accum))

    def ts(self, eng, out, in0, s1, s2, op0, op1=None):
        a1, a2 = self._a(s1), self._a(s2)
        if op1 is None:
            f = lambda e: e.tensor_scalar(out.ap, in0.ap, a1, None, op0)
        else:
            f = lambda e: e.tensor_scalar(out.ap, in0.ap, a1, a2, op0, op1)
        self.emit(eng, f, self._vs(in0, s1, s2), [out])

    def tt(self, eng, out, in0, in1, op):
        self.emit(eng, lambda e: e.tensor_tensor(out.ap, in0.ap, in1.ap, op), [in0, in1], [out])

    def stt(self, eng, out, in0, scalar, in1, op0, op1):
        sc = self._a(scalar)
        self.emit(eng, lambda e: e.scalar_tensor_tensor(out.ap, in0.ap, sc, in1.ap, op0, op1),
                  self._vs(in0, scalar, in1), [out])

    def copy(self, eng, out, in_):
        if eng == "act":
            self.emit("act", lambda e: e.copy(out.ap, in_.ap), [in_], [out])
        else:
            self.emit(eng, lambda e: e.tensor_copy(out.ap, in_.ap), [in_], [out])

    def scan(self, out, d0, d1, init, op0=ALU.mult, op1=ALU.add):
        ia = self._a(init)
        self.emit("dve", lambda e: e.tensor_tensor_scan(out.ap, d0.ap, d1.ap, ia, op0, op1),
                  self._vs(d0, d1, init), [out])

    def memset(self, eng, out, val):
        self.emit(eng, lambda e: e.memset(out.ap, val), [], [out])

    def recip(self, out, in_):
        self.emit("dve", lambda e: e.reciprocal(out.ap, in_.ap), [in_], [out])

    def reduce(self, out, in_, op, axis=AX.X):
        self.emit("dve", lambda e: e.tensor_reduce(out.ap, in_.ap, axis, op), [in_], [out])

    def dma(self, q, out, in_, slow=False):
        if slow:
            self.emit(q, lambda e: e.dma_start(out=out.ap, in_=in_.ap, allow_slow_non_contiguous=True), [in_], [out], dma=True)
        else:
            self.emit(q, lambda e: e.dma_start(out=out.ap, in_=in_.ap), [in_], [out], dma=True)

    def allgather(self, out, in_, groups):
        self.n_cc = getattr(self, "n_cc", 0) + 1
        sem = self.es.enter_context(self.nc.semaphore("cc%d" % self.n_cc))
        eng = "pool"
        toks = []
        for r in in_.res:
            if r.w is not None:
                toks.append(r.w)
        for r in out.res:
            if r.w is not None:
                toks.append(r.w)
            toks.extend(r.r.values())
        waits = []
        wd = self.waited[eng]
        for (sm, val, te) in toks:
            if wd.get(id(sm), 0) >= val:
                continue
            wd[id(sm)] = val
            waits.append((sm, val))
        oa, ia = out.ap, in_.ap
        fn = lambda e: e.collective_compute("AllGather", ALU.bypass, replica_groups=groups, ins=[ia], outs=[oa])
        tok = (sem, 1, "ccq")
        self.ops[eng].append((waits, fn, sem, None))
        self.cc_sems = getattr(self, "cc_sems", []) + [sem]
        for r in in_.res:
            r.r[id(sem)] = tok
        for r in out.res:
            r.w = tok
            r.r = {}

    def barrier(self):
        allw = []
        for q in self.dsem:
            for sm, k in zip(self.dsem[q], self.dcnt[q]):
                if k > 0:
                    allw.append((sm, 16 * k))
        for e in ("pe", "act", "dve", "pool"):
            if self.cnt[e] > 0:
                allw.append((self.sem[e], self.cnt[e]))
        for sm in getattr(self, "cc_sems", []):
            allw.append((sm, 1))
        for e in self.ops:
            wd = self.waited[e]
            waits = []
            for (sm, val) in allw:
                if sm is self.sem.get(e):
                    continue
                if wd.get(id(sm), 0) >= val:
                    continue
                wd[id(sm)] = val
                waits.append((sm, val))
            if waits:
                self.ops[e].append((waits, None, None, 0))

    def finish(self):
        nc = self.nc
        fin = []
        for q in self.dsem:
            for s, k in zip(self.dsem[q], self.dcnt[q]):
                if k > 0:
                    fin.append((s, 16 * k))
        for e in ("pe", "act", "dve", "pool"):
            if self.cnt[e] > 0:
                fin.append((self.sem[e], self.cnt[e]))
        ops = self.ops

        def run(e, lst, extra=()):
            for waits, fn, s, inc in lst:
                for (ws, wv) in waits:
                    e.wait_ge(ws, wv)
                if fn is not None:
                    if inc is None:
                        fn(e).then_inc(s)
                    else:
                        fn(e).then_inc(s, inc)
            for (ws, wv) in extra:
                e.wait_ge(ws, wv)

        with nc.Block() as block:
            @block.sync
            def _(e):
                run(e, ops["sp"], fin)

            @block.tensor
            def _(e):
                run(e, ops["pe"])

            @block.scalar
            def _(e):
                run(e, ops["act"])

            @block.vector
            def _(e):
                run(e, ops["dve"])

            @block.gpsimd
            def _(e):
                run(e, ops["pool"])


EVEN_COLS = dict(q=(0, 256), k=(256, 512), v=(512, 1024), gg=(1024, 1536), lrf=(1536, 1552),
                 lrb=(1552, 1568), xr=(1568, 2080), gl=(2080, 2592))
ODD_COLS = dict(q=(0, 512), k=(512, 640), v=(640, 768), gs=(768, 1280), rq=(1280, 1536),
                rk=(1536, 1792), rv=(1792, 2304), gr=(2304, 2816),
                qr=(2816, 3328), kr=(3328, 3456), rqr=(3456, 3712), rkr=(3712, 3968))


class Builder:
    def __init__(self, SEQ, DEPTH, dbg=(), NSEG=1):
        self.NSEG = NSEG
        self.SEQ = SEQ
        self.DEPTH = DEPTH
        self.TALL = CTX + SEQ
        self.dbg = dbg
        self.nc = bass.Bass("TRN2", target_bir_lowering=False)
        self.es = ExitStack()
        self.tiles = [(0, CTX, True)] + [(CTX + i * 512, 512, False) for i in range(SEQ // 512)]

    def din(self, name, shape, dt=F32):
        return self.nc.dram_tensor(name, list(shape), dt, kind="ExternalInput")

    def dscr(self, name, shape, dt=F32):
        kind = "ExternalOutput" if name in self.dbg else "Internal"
        return self.nc.dram_tensor(name, list(shape), dt, kind=kind)

    def declare(self):
        T = self.TALL
        S = self.SEQ
        i = {}
        i["xin"] = self.din("xin", [D, T])
        i["cvec"] = self.din("cvec", [128, 8, 2])
        i["ada_w"] = self.din("ada_w", [4, D, 3 * D])
        i["ada_b"] = self.din("ada_b", [4, 128, 24])
        i["norm_g"] = self.din("norm_g", [4, 128, 8])
        i["final_g"] = self.din("final_g", [128, 8])
        i["e_w_in"] = self.din("e_w_in", [2, D, 2592])
        i["e_w_out"] = self.din("e_w_out", [2, D, D])
        i["o_w_in"] = self.din("o_w_in", [2, D, 3968])
        i["o_w_out"] = self.din("o_w_out", [2, D, D])
        i["gla_up"] = self.din("gla_up", [2, 2, 16, 256])
        i["gla_b"] = self.din("gla_b", [2, 2, 128, 2])
        i["gla_ng"] = self.din("gla_ng", [2, 128, 1])
        i["ret_ng"] = self.din("ret_ng", [2, 128, 1])
        i["conv_w"] = self.din("conv_w", [2, 128, 4, 4])
        i["conv_b"] = self.din("conv_b", [2, 128, 4])
        i["lru_w"] = self.din("lru_w", [2, 4, 4, 128, 128])
        i["lru_b"] = self.din("lru_b", [2, 4, 128, 4])
        i["lru_lam"] = self.din("lru_lam", [2, 2, 128, 4])
        i["sink"] = self.din("sink", [2, 128, 8])
        i["ret_dec"] = self.din("ret_dec", [2, 2, 128, 2])
        i["ident"] = self.din("ident", [128, 128], BF16)
        i["masks"] = self.din("masks", [4, 128, 128])
        i["rmask"] = self.din("rmask", [2, 128, 512])
        i["ropec"] = self.din("ropec", [128, S])
        i["ropes"] = self.din("ropes", [128, S])
        i["segw"] = self.din("segw", [128, 16])
        self.i = i
        s = {}
        s["cc_h_in"] = self.dscr("cc_h_in", [128, 12])
        s["cc_h_out"] = self.dscr("cc_h_out", [512, 12])
        s["cc_kv_in"] = self.dscr("cc_kv_in", [256, 384], BF16)
        s["cc_kv_out"] = self.dscr("cc_kv_out", [1024, 384], BF16)
        s["cc_s_in"] = self.dscr("cc_s_in", [512, 129])
        s["cc_s_out"] = self.dscr("cc_s_out", [2048, 129])
        s["cc_l_in"] = self.dscr("cc_l_in", [128, 16])
        s["cc_l_out"] = self.dscr("cc_l_out", [512, 16])
        s["AF"] = self.dscr("AF", [512, T])
        s["HB"] = self.dscr("HB", [512, T])
        s["AB"] = self.dscr("AB", [512, T])
        s["X"] = self.dscr("X", [D, T])
        s["YT"] = self.dscr("YT", [D, T], BF16)
        s["QT"] = self.dscr("QT", [256, T])
        s["KT"] = self.dscr("KT", [256, T])
        s["VT"] = self.dscr("VT", [T, 512], BF16)
        s["GA"] = self.dscr("GA", [512, T])
        s["LR"] = self.dscr("LR", [2, 16, T])
        s["XR"] = self.dscr("XR", [512, T])
        s["GB"] = self.dscr("GB", [512, T])
        s["OF"] = self.dscr("OF", [512, T])
        s["HF"] = self.dscr("HF", [512, T])
        s["AQ"] = self.dscr("AQ", [512, T], BF16)
        s["AK"] = self.dscr("AK", [2, 128, T], BF16)
        s["AV"] = self.dscr("AV", [T, 128], BF16)
        self.s = s
        self.out = self.nc.dram_tensor("out", [D, S], F32, kind="ExternalOutput")

    def dv(self, handle, name, rowkey, ap, c0, n):
        return V(ap, self.p.dres(name, rowkey, c0, n))

    def fm(self, name, r0, nrows, c0, n, src=None):
        h = (src or self.s)[name]
        if nrows <= 128:
            ap = h[r0:r0 + nrows, c0:c0 + n]
            keys = [r0 // 128]
        else:
            ap = h[r0:r0 + nrows, c0:c0 + n].rearrange("(c p) t -> p c t", p=128)
            keys = list(range(r0 // 128, (r0 + nrows) // 128))
        res = []
        for k in keys:
            res += self.p.dres(name, k, c0, n)
        return V(ap, res)

    def tm(self, name, c0, n, col0, ncol):
        h = self.s[name]
        if n <= 128:
            ap = h[c0:c0 + n, col0:col0 + ncol]
        else:
            ap = h[c0:c0 + n, col0:col0 + ncol].rearrange("(b p) v -> p b v", p=128)
        return V(ap, self.p.dres(name, "all", c0, n))

    def cst(self, ap):
        return V(ap, [])

    def build(self):
        nc, es = self.nc, self.es
        self.declare()
        p = self.p = Prog(nc, es)
        i = self.i
        self.Fp = [es.enter_context(nc.sbuf_tensor("poolF%d" % k, [128, 4, 512], F32)) for k in range(10)]
        self.Hp = [es.enter_context(nc.sbuf_tensor("poolH%d" % k, [128, 4, 512], BF16)) for k in range(6)]
        self.ones_bf = p.sb([128, 128], BF16, "ones_bf")
        p.memset("pool", self.ones_bf, 1.0)
        self.ones_f = p.sb([128, 512], F32, "ones_f")
        p.memset("pool", self.ones_f, 1.0)
        self.ident = p.sb([128, 128], BF16, "ident")
        p.dma("sp", self.ident, self.cst(i["ident"].ap()))
        self.masks = p.sb([128, 4, 128], F32, "masks")
        p.dma("sp", self.masks, self.cst(i["masks"].ap().rearrange("m p t -> p m t")))
        self.rmask = p.sb([128, 2, 512], F32, "rmask")
        p.dma("sp", self.rmask, self.cst(i["rmask"].ap().rearrange("m p t -> p m t")))
        self.segw = p.sb([128, 4, 4], F32, "segw")
        p.dma("sp", self.segw, self.cst(i["segw"].ap().rearrange("p (k r) -> p k r", r=4)))
        self.groups = [[0, 1, 2, 3], [4, 5, 6, 7]]
        self.Wb = p.sb([128, 8, 3968], BF16, "Wb")
        self.Wo = p.sb([128, 8, 1024], BF16, "Wo")
        self.wstg = [p.sb([128, 512], F32, "wstg%d" % k) for k in range(2)]
        self.wrr = 0
        self.banks = [p.ps([128, 512], F32, "bank%d" % b) for b in range(7)]
        self.bank_bf = p.ps([128, 1024], BF16, "bankbf")
        self.brr = 0
        self.prologue()
        p.barrier()
        stg = [self.Fv(k) for k in range(4)]
        cnt = 0
        for (c0, n, isctx) in self.tiles:
            for half in range(2):
                st = stg[cnt % 4]
                cnt += 1
                p.dma("sp", st[:, :, :n], self.cst(i["xin"][half * 512:(half + 1) * 512, c0:c0 + n].rearrange("(c p) t -> p c t", p=128)))
                p.dma("pool", self.fm("X", half * 512, 512, c0, n), st[:, :, :n])
        for l in range(self.DEPTH):
            j = l // 2
            if l % 2 == 0:
                self.load_w_in(i["e_w_in"][j], 2592)
                self.load_w_out(i["e_w_out"][j])
                self.phase1_even(l)
                self.linattn(l, "gla")
                self.lru(l)
            else:
                self.load_w_in(i["o_w_in"][j], 3968)
                self.load_w_out(i["o_w_out"][j])
                self.phase1_odd(l)
                import os
                if "ret" not in os.environ.get("KSKIP", ""):
                    self.linattn(l, "ret")
                if "swa" not in os.environ.get("KSKIP", ""):
                    self.swa(l)
            self.phase3(l)
        self.final_norm()
        p.finish()
        return nc

    def Fv(self, k, pl=None, npl=1, rows=128):
        t = self.Fp[k]
        if pl is None:
            return V(t.ap(), [Res()])
        if npl == 1:
            return V(t[0:rows, pl, :], [Res()])
        return V(t[0:rows, pl:pl + npl, :], [Res()])

    def Hv(self, k, pl=None, npl=1):
        t = self.Hp[k]
        if pl is None:
            return V(t.ap(), [Res()])
        if npl == 1:
            return V(t[:, pl, :], [Res()])
        return V(t[:, pl:pl + npl, :], [Res()])

    def bank(self):
        b = self.banks[self.brr]
        self.brr = (self.brr + 1) % len(self.banks)
        return b

    def prologue(self):
        p, i = self.p, self.i
        cv = p.sb([128, 8, 2], F32, "cvec")
        p.dma("sp", cv, self.cst(i["cvec"].ap()))
        sc = p.sb([128, 8, 2], F32, "silu_c")
        p.act(sc, cv, AF.Silu)
        self.GS, self.SH, self.GT = [], [], []
        wb = [[self.Fv(0), self.Fv(1)], [self.Fv(2), self.Fv(3)]]
        cnt = 0
        for l in range(self.DEPTH):
            ab = p.sb([128, 24], F32, "ada_b%d" % l)
            p.dma("sp", ab, self.cst(i["ada_b"][l]))
            ng = p.sb([128, 8], F32, "norm_g%d" % l)
            p.dma("sp", ng, self.cst(i["norm_g"][l]))
            mod = p.sb([128, 24, 2], F32, "mod%d" % l)
            bk = self.bank()
            for jg in range(6):
                w2 = wb[cnt % 2]
                cnt += 1
                for half in range(2):
                    p.dma("sp", w2[half], self.cst(i["ada_w"][l, half * 512:(half + 1) * 512, jg * 512:(jg + 1) * 512].rearrange("(k p) c -> p k c", p=128)))
                for j4 in range(4):
                    jj = jg * 4 + j4
                    for k in range(8):
                        p.mm(bk[:, jj * 2:jj * 2 + 2], w2[k // 4][:, k % 4, j4 * 128:(j4 + 1) * 128], sc[:, k, :],
                             start=(k == 0), stop=(k == 7))
            p.tt("dve", mod, bk[:, 0:48].re("p (j c) -> p j c", c=2), ab.re("p (j o) -> p j o", o=1).bc([128, 24, 2]), ALU.add)
            gs = p.sb([128, 8, 2], F32, "gs%d" % l)
            p.ts("dve", gs, mod[:, 8:16, :], 1.0, None, ALU.add)
            p.tt("dve", gs, gs, ng.re("p (j o) -> p j o", o=1).bc([128, 8, 2]), ALU.mult)
            self.GS.append(gs)
            self.SH.append(mod[:, 0:8, :])
            self.GT.append(mod[:, 16:24, :])
        self.fg = p.sb([128, 8], F32, "final_g")
        p.dma("sp", self.fg, self.cst(i["final_g"].ap()))

    def load_w_in(self, wap, ncol):
        p = self.p
        for k in range(8):
            for c0 in range(0, ncol, 512):
                w = min(512, ncol - c0)
                st = self.wstg[self.wrr % 2]
                self.wrr += 1
                p.dma("sp", st[:, :w], self.cst(wap[k * 128:(k + 1) * 128, c0:c0 + w]))
                p.copy("pool", self.Wb[:, k, c0:c0 + w], st[:, :w])

    def load_w_out(self, wap):
        p = self.p
        for k in range(8):
            for c0 in range(0, 1024, 512):
                st = self.wstg[self.wrr % 2]
                self.wrr += 1
                p.dma("sp", st[:, :512], self.cst(wap[k * 128:(k + 1) * 128, c0:c0 + 512]))
                p.copy("pool", self.Wo[:, k, c0:c0 + 512], st[:, :512])

    def norm_alloc(self):
        self.xt = [self.Fv(0), self.Fv(1)]
        self.sq = [self.Hv(0), self.Hv(1)]
        self.hT = [self.Hv(2), self.Hv(3)]
        self.rstd = self.Fv(2, 0)

    def norm_tile(self, l, c0, n, isctx, src="X"):
        p = self.p
        w = 1 if isctx else 0
        xt, sq, hT, rstd = self.xt, self.sq, self.hT, self.rstd
        bk = self.bank()
        for half in range(2):
            p.dma("sp", xt[half][:, :, :n], self.fm(src, half * 512, 512, c0, n))
            p.act(sq[half][:, :, :n], xt[half][:, :, :n], AF.Square)
        for k in range(8):
            p.mm(bk[:, :n], self.ones_bf, sq[k // 4][:, k % 4, :n], start=(k == 0), stop=(k == 7))
        p.ts("dve", rstd[:, :n], bk[:, :n], 1.0 / D, EPS, ALU.mult, ALU.add)
        p.act(rstd[:, :n], rstd[:, :n], AF.Sqrt)
        p.recip(rstd[:, :n], rstd[:, :n])
        for k in range(8):
            eng = "dve" if k % 2 == 0 else "pool"
            xk = xt[k // 4][:, k % 4, :n]
            p.tt(eng, xk, xk, rstd[:, :n], ALU.mult)
            if l is None:
                p.act(xk, xk, AF.Identity, scale=self.fg[:, k:k + 1])
            else:
                p.act(hT[k // 4][:, k % 4, :n], xk, AF.Identity, bias=self.SH[l][:, k, w:w + 1],
                      scale=self.GS[l][:, k, w:w + 1])
        return lambda k: hT[k // 4][:, k % 4, :]

    def proj_fm(self, hT, n, col0, m):
        p = self.p
        bk = self.bank()
        for k in range(8):
            p.mm(bk[:m, :n], self.Wb[:, k, col0:col0 + m], hT(k)[:, :n], start=(k == 0), stop=(k == 7))
        return bk

    def proj_tm(self, hT, blk, col0, m):
        p = self.p
        bk = self.bank()
        for k in range(8):
            p.mm(bk[:, :m], hT(k)[:, blk * 128:(blk + 1) * 128], self.Wb[:, k, col0:col0 + m],
                 start=(k == 0), stop=(k == 7))
        return bk

    def phase1_even(self, l):
        p = self.p
        C = EVEN_COLS
        p.barrier()
        self.norm_alloc()
        stg = [self.Fv(k) for k in (3, 4, 5, 6)]
        vstg = [self.Hv(4), self.Hv(5)]
        lrstg = [self.Fv(7, k, rows=16) for k in range(4)]
        srr = 0
        ti = 0
        for (c0, n, isctx) in self.tiles:
            hT = self.norm_tile(l, c0, n, isctx)
            for (nm, dst, scale) in (("q", "QT", 0.125), ("k", "KT", None)):
                st = stg[srr % 4]
                srr += 1
                for c in range(2):
                    bk = self.proj_fm(hT, n, C[nm][0] + c * 128, 128)
                    if scale is not None:
                        p.act(st[:, c, :n], bk[:, :n], AF.Copy, scale=scale)
                    else:
                        p.copy("dve", st[:, c, :n], bk[:, :n])
                p.dma("pool", self.fm(dst, 0, 256, c0, n), st[:, 0:2, :n])
            for (nm, dst, silu) in (("gg", "GA", True), ("xr", "XR", False), ("gl", "GB", True)):
                st = stg[srr % 4]
                srr += 1
                for c in range(4):
                    bk = self.proj_fm(hT, n, C[nm][0] + c * 128, 128)
                    if silu:
                        p.act(st[:, c, :n], bk[:, :n], AF.Silu)
                    else:
                        p.copy("dve", st[:, c, :n], bk[:, :n])
                p.dma("pool", self.fm(dst, 0, 512, c0, n), st[:, :, :n])
            for d, nm in enumerate(("lrf", "lrb")):
                bk = self.proj_fm(hT, n, C[nm][0], 16)
                ls = lrstg[(2 * ti + d) % 4]
                p.copy("dve", ls[:, :n], bk[:16, :n])
                p.dma("pool", V(self.s["LR"][d, :, c0:c0 + n], p.dres("LR", d, c0, n)), ls[:, :n])
            vs = vstg[ti % 2]
            nb = n // 128
            for b in range(nb):
                bk = self.proj_tm(hT, b, C["v"][0], 512)
                p.copy("act" if b % 2 else "dve", vs[:, b, :], bk[:, :])
            p.dma("pool", self.tm("VT", c0, n, 0, 512), vs[:, :nb, :])
            ti += 1

    def linattn(self, l, kind):
        p, i = self.p, self.i
        j = l // 2
        NSEG = self.NSEG
        p.barrier()
        if not hasattr(self, "la_S"):
            self.la_dec = [p.sb([128, 8], F32, "la_dec%d" % k) for k in range(2)]
            self.la_S = [p.sb([128, 128], F32, "la_S%d" % k) for k in range(2)]
            self.la_Sb = [[p.sb([128, 128], BF16, "la_Sb%d_%d" % (k, m)) for m in range(2)] for k in range(2)]
            self.la_att = [p.sb([128, 128], BF16, "la_att%d" % k) for k in range(3)]
            self.la_up = p.sb([16, 2, 256], F32, "la_up")
            self.la_nb = p.sb([128, 2, 2], F32, "la_nb")
            self.la_ng = p.sb([128, 1], F32, "la_ng")
            self.la_Lr = p.sb([128, 2, 2], F32, "la_Lr")
            self.la_asum = p.sb([128, 4], F32, "la_asum")
            self.la_ared = p.sb([128, 1], F32, "la_ared")
            self.la_tmpS = p.sb([128, 128], F32, "la_tmpS")
        la_q = [self.Fv(0, 0), self.Fv(0, 1)]
        la_k = [self.Fv(0, 2), self.Fv(0, 3)]
        la_L = [self.Fv(1, 0), self.Fv(1, 1)]
        la_b = [self.Fv(1, 2), self.Fv(1, 3)]
        la_e = [self.Fv(2, 0), self.Fv(2, 1)]
        la_lr = [self.Fv(2, 2, rows=16), self.Fv(2, 3, rows=16)]
        la_v = [self.Hv(0), self.Hv(1)]
        la_qi = [self.Hv(2, 0), self.Hv(2, 1)]
        la_ki = [self.Hv(2, 2), self.Hv(2, 3)]
        la_ks = [self.Hv(3, 0), self.Hv(3, 1)]
        la_kst = [self.Hv(3, 2).re("p (b c) -> p b c", c=128), self.Hv(3, 3).re("p (b c) -> p b c", c=128)]
        la_o = [self.Fv(3, 0, 2), self.Fv(3, 2, 2)]
        la_of = [self.Fv(4, 0, 2), self.Fv(4, 2, 2)]
        la_g = [self.Fv(5, 0, 2), self.Fv(5, 2, 2)]
        la_rs = self.Fv(6, 0)
        la_o2 = self.Hv(4, 0)
        la_y = [self.Hv(5, 0, 2), self.Hv(5, 2, 2)]
        la_sst = self.Fv(7, 0, 2).re("p a b -> p (a b)")
        la_G = [V(self.Fp[8 + d].ap().rearrange("p a b -> p (a b)")[:, 0:1032].rearrange("p (r q c) -> p r q c", r=4, q=2), [Res()])
                for d in range(2)]
        if kind == "gla":
            p.dma("sp", self.la_up, self.cst(i["gla_up"][j].rearrange("d r c -> r d c")))
            p.dma("sp", self.la_nb, self.cst(i["gla_b"][j].rearrange("d p c -> p d c")))
            p.ts("dve", self.la_nb, self.la_nb, -1.0, None, ALU.mult)
            p.dma("sp", self.la_ng, self.cst(i["gla_ng"][j]))
            yrow0 = 0
        else:
            p.dma("sp", self.la_ng, self.cst(i["ret_ng"][j]))
            p.dma("sp", self.la_Lr, self.cst(i["ret_dec"][j].rearrange("d p c -> p d c")))
            p.act(self.la_Lr, self.la_Lr, AF.Exp, scale=-1.0)
            p.act(self.la_Lr, self.la_Lr, AF.Ln, bias=1.0)
            p.ts("dve", self.la_Lr, self.la_Lr, 16.0, None, ALU.mult)
            yrow0 = 512
        cnt = [0]
        sbi = [0, 0]

        def do_tile(d, c0, n, state_only):
            mask = self.masks[:, d, :]
            nblk = n // 128
            nch = n // 64
            v = la_v[cnt[0] % 2]
            p.dma("sp", v[:, :nblk, :], self.tm("VT", c0, n, 0, 512))
            for hp in range(2):
                x = hp
                ot = la_o[x]
                qt, kt, L, bc, e = la_q[x], la_k[x], la_L[x], la_b[x], la_e[x]
                qi, ki, ks, kst, dec = la_qi[x], la_ki[x], la_ks[x], la_kst[x], self.la_dec[x]
                if not state_only:
                    p.dma("sp", qt[:, :n], self.fm("QT", hp * 128, 128, c0, n))
                p.dma("sp", kt[:, :n], self.fm("KT", hp * 128, 128, c0, n))
                if kind == "gla":
                    lr = la_lr[x]
                    p.dma("sp", lr[:, :n], V(self.s["LR"][d, :, c0:c0 + n], p.dres("LR", d, c0, n)))
                    bk = self.bank()
                    p.mm(bk[:, :n], self.la_up[:, d, hp * 128:(hp + 1) * 128], lr[:, :n])
                    p.act(L[:, :n], bk[:, :n], AF.Exp, scale=-1.0, bias=self.la_nb[:, d, hp:hp + 1])
                    p.act(L[:, :n], L[:, :n], AF.Ln, bias=1.0)
                else:
                    p.ts("dve", L[:, :n], self.ones_f[:, :n], self.la_Lr[:, d, hp:hp + 1], None, ALU.mult)
                if d == 0:
                    p.scan(bc[:, :n], self.rmask[:, 0, :n], L[:, :n], 0.0)
                else:
                    p.scan(bc[:, :n][:, ::-1], self.rmask[:, 1, :n][:, ::-1], L[:, :n][:, ::-1], 0.0)
                b3 = bc[:, :n].re("p (c i) -> p c i", i=64)
                bend = b3[:, :, 63:64] if d == 0 else b3[:, :, 0:1]
                if not state_only:
                    p.act(e[:, :n], bc[:, :n], AF.Exp, scale=-1.0 / 16)
                    p.tt("dve", qi[:, :n], qt[:, :n], e[:, :n], ALU.mult)
                    p.act(e[:, :n], bc[:, :n], AF.Exp, scale=1.0 / 16)
                    p.tt("pool", ki[:, :n], kt[:, :n], e[:, :n], ALU.mult)
                else:
                    p.reduce(self.la_ared, bend.re("p c o -> p (c o)"), ALU.add)
                    aq = self.la_asum[:, 2 * d + hp:2 * d + hp + 1]
                    p.tt("dve", aq, aq, self.la_ared, ALU.add)
                p.act(dec[:, :nch], bend.re("p c o -> p (c o)"), AF.Exp, scale=-1.0 / 16)
                p.tt("dve", e[:, :n].re("p (c i) -> p c i", i=64), bend.bc([128, nch, 64]), b3, ALU.subtract)
                p.act(e[:, :n], e[:, :n], AF.Exp, scale=-1.0 / 16)
                p.tt("pool", ks[:, :n], kt[:, :n], e[:, :n], ALU.mult)
                for b in range(nblk):
                    p.tr(self.bank_bf[:, b * 128:(b + 1) * 128], ks[:, b * 128:(b + 1) * 128], self.ident)
                p.copy("act", kst[:, :nblk, :], self.bank_bf[:, :nblk * 128].re("p (b c) -> p b c", c=128))
                blks = list(range(nblk)) if d == 0 else list(reversed(range(nblk)))
                S = self.la_S[hp]
                for b in blks:
                    cs = slice(b * 128, (b + 1) * 128)
                    chunks = [0, 1] if d == 0 else [1, 0]
                    psO = [None, None]
                    if not state_only:
                        for h in range(2):
                            r = slice(h * 64, (h + 1) * 64)
                            head = 2 * hp + h
                            vc = slice(head * 128, (head + 1) * 128)
                            psA = self.bank()
                            p.mm(psA[:, :128], ki[r, cs], qi[r, cs])
                            am = self.la_att[(h + 2 * b) % 3]
                            p.tt("dve", am, psA[:, :128], mask, ALU.mult)
                            po = self.bank()
                            psO[h] = po
                            p.mm(po[:, :128], v[:, b, vc], am, start=True, stop=False)
                    for ci, cc in enumerate(chunks):
                        tk = slice(b * 128 + cc * 64, b * 128 + cc * 64 + 64)
                        pk = slice(cc * 64, cc * 64 + 64)
                        Sb = self.la_Sb[hp][sbi[hp] % 2]
                        psD = self.bank()
                        for h in range(2):
                            r = slice(h * 64, (h + 1) * 64)
                            head = 2 * hp + h
                            vc = slice(head * 128, (head + 1) * 128)
                            if not state_only:
                                p.mm(psO[h][:, cc * 64:cc * 64 + 64], Sb[r, :], qi[r, tk], start=False, stop=(ci == 1))
                            p.mm(psD[r, :128], kst[pk, b, r], v[pk, b, vc])
                        chn = b * 2 + cc
                        p.stt("dve", S, S, dec[:, chn:chn + 1], psD[:, :128], ALU.mult, ALU.add)
                        if not state_only:
                            sbi[hp] += 1
                            p.copy("act", self.la_Sb[hp][sbi[hp] % 2], S)
                    if not state_only:
                        for h in range(2):
                            p.copy("act" if h else "dve", ot[:, h, cs], psO[h][:, :128])
                if state_only:
                    continue
                if d == 0:
                    p.dma("pool", self.fm("OF", hp * 256, 256, c0, n), ot[:, :, :n])
                else:
                    oft, gt, yt = la_of[x], la_g[x], la_y[x]
                    p.dma("sp", oft[:, :, :n], self.fm("OF", hp * 256, 256, c0, n))
                    p.dma("sp", gt[:, :, :n], self.fm("GA", hp * 256, 256, c0, n))
                    for h in range(2):
                        o = ot[:, h, :n]
                        p.tt("dve", o, o, oft[:, h, :n], ALU.add)
                        p.act(la_o2[:, :n], o, AF.Square)
                        bk = self.bank()
                        p.mm(bk[:, :n], self.ones_bf, la_o2[:, :n])
                        p.ts("dve", la_rs[:, :n], bk[:, :n], 1.0 / 128, EPS, ALU.mult, ALU.add)
                        p.act(la_rs[:, :n], la_rs[:, :n], AF.Sqrt)
                        p.recip(la_rs[:, :n], la_rs[:, :n])
                        p.stt("dve", o, o, self.la_ng[:, 0:1], la_rs[:, :n], ALU.mult, ALU.mult)
                        p.tt("pool", yt[:, h, :n], o, gt[:, h, :n], ALU.mult)
                    p.dma("pool", self.fm("YT", yrow0 + hp * 256, 256, c0, n), yt[:, :, :n])
            cnt[0] += 1

        def zero_state(bf=True):
            for hp in range(2):
                p.memset("dve", self.la_S[hp], 0.0)
                if bf:
                    p.memset("pool", self.la_Sb[hp][sbi[hp] % 2], 0.0)

        lat = self.tiles[1:]
        ctxt = self.tiles[0]
        if NSEG > 1:
            p.memset("dve", self.la_asum, 0.0)
            for d in range(2):
                zero_state(bf=False)
                for (c0, n, isctx) in (lat if d == 0 else list(reversed(lat))):
                    do_tile(d, c0, n, True)
                for hp in range(2):
                    q = 2 * d + hp
                    p.copy("dve", la_sst[:, q * 129:q * 129 + 128], self.la_S[hp])
                    p.act(la_sst[:, q * 129 + 128:q * 129 + 129], self.la_asum[:, q:q + 1], AF.Exp, scale=-1.0 / 16)
            cin = V(self.s["cc_s_in"].ap().rearrange("(q p) c -> p q c", p=128), p.dres("cc_s_in", 0, 0, 128))
            p.dma("pool", cin, la_sst[:, 0:516].re("p (q c) -> p q c", c=129))
            cout = V(self.s["cc_s_out"].ap(), p.dres("cc_s_out", 0, 0, 128))
            p.allgather(cout, V(self.s["cc_s_in"].ap(), cin.res), self.groups)
            for d in range(2):
                for r in range(4):
                    src = self.s["cc_s_out"][r * 512 + d * 256:r * 512 + d * 256 + 256, :].rearrange("(q p) c -> p q c", p=128)
                    p.dma("sp", la_G[d][:, r, :, :], V(src, cout.res))
        for d in range(2):
            zero_state()
            do_tile(d, ctxt[0], ctxt[1], False)
            if NSEG > 1:
                for hp in range(2):
                    S = self.la_S[hp]
                    for r in (range(4) if d == 0 else reversed(range(4))):
                        A_r = la_G[d][:, r, hp, 128:129]
                        B_r = la_G[d][:, r, hp, 0:128]
                        p.stt("dve", self.la_tmpS, S, A_r, B_r, ALU.mult, ALU.add)
                        p.tt("dve", self.la_tmpS, self.la_tmpS, S, ALU.subtract)
                        p.stt("dve", S, self.la_tmpS, self.segw[:, 2 + d, r:r + 1], S, ALU.mult, ALU.add)
                    sbi[hp] += 1
                    p.copy("act", self.la_Sb[hp][sbi[hp] % 2], S)
            for (c0, n, isctx) in (lat if d == 0 else list(reversed(lat))):
                do_tile(d, c0, n, False)


    def lru(self, l):
        p, i = self.p, self.i
        j = l // 2
        NSEG = self.NSEG
        p.barrier()
        if not hasattr(self, "lw"):
            self.lw = p.sb([128, 4, 4, 128], BF16, "lru_w")
            self.lb = p.sb([128, 4, 4], F32, "lru_b")
            self.lc8 = p.sb([128, 2, 4], F32, "lru_c8")
            self.lc16 = p.sb([128, 2, 4], F32, "lru_c16")
            self.lcw = p.sb([128, 4, 4], F32, "lru_cw")
            self.lcb = p.sb([128, 4], F32, "lru_cb")
            self.lr_carry = p.sb([128, 4], F32, "lru_carry")
            self.lr_acar = p.sb([128, 4], F32, "lru_acar")
            self.lr_hal = p.sb([128, 4, 3], F32, "lru_hal")
            self.lr_halG = p.sb([128, 4, 12], F32, "lru_halG")
            self.lr_halL = p.sb([128, 4, 2], F32, "lru_halL")
            self.lr_halR = p.sb([128, 4, 1], F32, "lru_halR")
            self.lr_sum = p.sb([128, 16], F32, "lru_sum")
            self.lr_sumG = p.sb([128, 4, 16], F32, "lru_sumG")
            self.lr_hin = p.sb([128, 2, 4], F32, "lru_hin")
            self.lr_t4 = p.sb([128, 4], F32, "lru_t4")
        lxh = [self.Fv(0, 0, 2).re("p a b -> p (a b)"), self.Fv(0, 2, 2).re("p a b -> p (a b)")]
        lxc = [self.Fv(1, 0), self.Fv(1, 1)]
        lr_r = [self.Fv(1, 2), self.Fv(1, 3)]
        lr_i = [self.Fv(2, 0), self.Fv(2, 1)]
        lr_a = [self.Fv(2, 2), self.Fv(2, 3)]
        lr_s = [self.Fv(3, 0), self.Fv(3, 1)]
        lxb = [self.Hv(0, 0), self.Hv(0, 1)]
        lr_h = [self.Fv(4), self.Fv(5)]
        lr_hf = [self.Fv(6), self.Fv(7)]
        lr_g = [self.Fv(8), self.Fv(9)]
        lr_y = [self.Hv(1), self.Hv(2)]
        for kd in range(4):
            for n_ in range(4):
                st = self.wstg[self.wrr % 2]
                self.wrr += 1
                p.dma("sp", st[:, :128], self.cst(i["lru_w"][j, kd, n_]))
                p.copy("pool", self.lw[:, kd, n_, :], st[:, :128])
        p.dma("sp", self.lb, self.cst(i["lru_b"][j].rearrange("k p c -> p k c")))
        p.dma("sp", self.lcw, self.cst(i["conv_w"][j]))
        p.dma("sp", self.lcb, self.cst(i["conv_b"][j]))
        p.dma("sp", self.lc8, self.cst(i["lru_lam"][j].rearrange("d p c -> p d c")))
        p.act(self.lc8, self.lc8, AF.Exp, scale=-1.0)
        p.act(self.lc8, self.lc8, AF.Ln, bias=1.0)
        p.ts("dve", self.lc16, self.lc8, -16.0, None, ALU.mult)
        p.ts("dve", self.lc8, self.lc8, -8.0, None, ALU.mult)
        T = self.TALL
        if NSEG > 1:
            hal = self.lr_hal
            p.dma("sp", hal[:, :, 0:1], self.fm("XR", 0, 512, CTX, 1), slow=True)
            p.dma("sp", hal[:, :, 1:3], self.fm("XR", 0, 512, T - 2, 2), slow=True)
            cin = V(self.s["cc_h_in"].ap(), p.dres("cc_h_in", 0, 0, 128))
            p.dma("pool", cin, hal.re("p c k -> p (c k)"))
            cout = V(self.s["cc_h_out"].ap(), p.dres("cc_h_out", 0, 0, 128))
            p.allgather(cout, cin, self.groups)
            p.dma("sp", self.lr_halG, V(self.s["cc_h_out"].ap().rearrange("(r p) k -> p r k", p=128), cout.res))
            G4 = self.lr_halG.re("p r (c k) -> p r c k", k=3)
            for r in range(4):
                wl = self.segw[:, 0, r:r + 1]
                wr = self.segw[:, 1, r:r + 1]
                if r == 0:
                    p.ts("dve", self.lr_halL, G4[:, r, :, 1:3], wl, None, ALU.mult)
                    p.ts("dve", self.lr_halR, G4[:, r, :, 0:1], wr, None, ALU.mult)
                else:
                    p.stt("dve", self.lr_halL, G4[:, r, :, 1:3], wl, self.lr_halL, ALU.mult, ALU.add)
                    p.stt("dve", self.lr_halR, G4[:, r, :, 0:1], wr, self.lr_halR, ALU.mult, ALU.add)
        cnt = 0
        lat = self.tiles[1:]
        for d in range(2):
            order = [self.tiles[0]] + (list(lat) if d == 0 else list(reversed(lat)))
            first = True
            for ti_, (c0, n, isctx) in enumerate(order):
                seg0, seg1 = (0, CTX) if isctx else (CTX, T)
                h = lr_h[cnt % 2]
                ac = lr_hf[cnt % 2] if (NSEG > 1 and not isctx) else None
                seg_first = (NSEG > 1 and ti_ == 1)
                for ch in range(4):
                    x = ch % 2
                    xh, xc, xb = lxh[x], lxc[x], lxb[x]
                    lo = max(seg0, c0 - 2)
                    hi = min(seg1, c0 + n + 1)
                    if lo > c0 - 2 or hi < c0 + n + 1:
                        p.memset("pool", xh[:, 0:516], 0.0)
                    p.dma("sp", xh[:, lo - (c0 - 2):hi - (c0 - 2)], self.fm("XR", ch * 128, 128, lo, hi - lo))
                    if NSEG > 1 and not isctx:
                        if lo > c0 - 2:
                            p.copy("dve", xh[:, 0:2], self.lr_halL[:, ch, :])
                        if hi < c0 + n + 1:
                            p.copy("dve", xh[:, n + 2:n + 3], self.lr_halR[:, ch, :])
                    p.ts("dve", xc[:, :n], xh[:, 0:n], self.lcw[:, ch, 0:1], self.lcb[:, ch:ch + 1], ALU.mult, ALU.add)
                    for k in range(1, 4):
                        p.stt("dve", xc[:, :n], xh[:, k:k + n], self.lcw[:, ch, k:k + 1], xc[:, :n], ALU.mult, ALU.add)
                    p.copy("act", xb[:, :n], xc[:, :n])
                    r, ig, a, s = lr_r[x], lr_i[x], lr_a[x], lr_s[x]
                    bk = self.bank()
                    p.mm(bk[:, :n], self.lw[:, 2 * d, ch, :], xb[:, :n])
                    p.act(r[:, :n], bk[:, :n], AF.Sigmoid, bias=self.lb[:, 2 * d, ch:ch + 1])
                    bk2 = self.bank()
                    p.mm(bk2[:, :n], self.lw[:, 2 * d + 1, ch, :], xb[:, :n])
                    p.act(ig[:, :n], bk2[:, :n], AF.Sigmoid, bias=self.lb[:, 2 * d + 1, ch:ch + 1])
                    p.act(a[:, :n], r[:, :n], AF.Exp, scale=self.lc8[:, d, ch:ch + 1])
                    p.act(s[:, :n], r[:, :n], AF.Exp, scale=self.lc16[:, d, ch:ch + 1])
                    p.act(s[:, :n], s[:, :n], AF.Sqrt, scale=-1.0, bias=1.0)
                    p.tt("pool", ig[:, :n], ig[:, :n], xc[:, :n], ALU.mult)
                    p.tt("dve", s[:, :n], s[:, :n], ig[:, :n], ALU.mult)
                    init = 0.0 if (first or seg_first) else self.lr_carry[:, ch:ch + 1]
                    ainit = 1.0 if seg_first else self.lr_acar[:, ch:ch + 1]
                    if NSEG > 1 and seg_first:
                        p.copy("dve", self.lr_hin[:, d, ch:ch + 1], self.lr_carry[:, ch:ch + 1])
                    if d == 0:
                        p.scan(h[:, ch, :n], a[:, :n], s[:, :n], init)
                        p.copy("dve", self.lr_carry[:, ch:ch + 1], h[:, ch, n - 1:n])
                        if ac is not None:
                            p.scan(ac[:, ch, :n], a[:, :n], self.ones_f[:, :n], ainit, ALU.mult, ALU.mult)
                            p.copy("dve", self.lr_acar[:, ch:ch + 1], ac[:, ch, n - 1:n])
                    else:
                        p.scan(h[:, ch, :n][:, ::-1], a[:, :n][:, ::-1], s[:, :n][:, ::-1], init)
                        p.copy("dve", self.lr_carry[:, ch:ch + 1], h[:, ch, 0:1])
                        if ac is not None:
                            p.scan(ac[:, ch, :n][:, ::-1], a[:, :n][:, ::-1], self.ones_f[:, :n], ainit, ALU.mult, ALU.mult)
                            p.copy("dve", self.lr_acar[:, ch:ch + 1], ac[:, ch, 0:1])
                first = False
                if NSEG > 1 and not isctx:
                    p.dma("pool", self.fm("HF" if d == 0 else "HB", 0, 512, c0, n), h[:, :, :n])
                    p.dma("pool", self.fm("AF" if d == 0 else "AB", 0, 512, c0, n), ac[:, :, :n])
                elif d == 0:
                    p.dma("pool", self.fm("HF", 0, 512, c0, n), h[:, :, :n])
                else:
                    hf, g, y = lr_hf[cnt % 2], lr_g[cnt % 2], lr_y[cnt % 2]
                    p.dma("sp", hf[:, :, :n], self.fm("HF", 0, 512, c0, n))
                    p.dma("sp", g[:, :, :n], self.fm("GB", 0, 512, c0, n))
                    p.tt("dve", h[:, :, :n], h[:, :, :n], hf[:, :, :n], ALU.add)
                    p.tt("pool", y[:, :, :n], h[:, :, :n], g[:, :, :n], ALU.mult)
                    p.dma("pool", self.fm("YT", 512, 512, c0, n), y[:, :, :n])
                cnt += 1
            if NSEG > 1:
                p.copy("dve", self.lr_sum[:, d * 8:d * 8 + 4], self.lr_acar)
                p.copy("dve", self.lr_sum[:, d * 8 + 4:d * 8 + 8], self.lr_carry)
        if NSEG > 1:
            cin = V(self.s["cc_l_in"].ap(), p.dres("cc_l_in", 0, 0, 128))
            p.dma("pool", cin, self.lr_sum)
            cout = V(self.s["cc_l_out"].ap(), p.dres("cc_l_out", 0, 0, 128))
            p.allgather(cout, cin, self.groups)
            p.dma("sp", self.lr_sumG, V(self.s["cc_l_out"].ap().rearrange("(r p) k -> p r k", p=128), cout.res))
            for d in range(2):
                hin = self.lr_hin[:, d, :]
                for r in (range(4) if d == 0 else reversed(range(4))):
                    A_r = self.lr_sumG[:, r, d * 8:d * 8 + 4]
                    B_r = self.lr_sumG[:, r, d * 8 + 4:d * 8 + 8]
                    p.tt("dve", self.lr_t4, A_r, hin, ALU.mult)
                    p.tt("dve", self.lr_t4, self.lr_t4, B_r, ALU.add)
                    p.tt("dve", self.lr_t4, self.lr_t4, hin, ALU.subtract)
                    p.stt("dve", hin, self.lr_t4, self.segw[:, 2 + d, r:r + 1], hin, ALU.mult, ALU.add)
            bufs = [[self.Fv(k) for k in (0, 1, 2, 3, 4)], [self.Fv(k) for k in (5, 6, 7, 8, 9)]]
            ys = [self.Hv(1), self.Hv(2)]
            p.barrier()
            for ti_, (c0, n, isctx) in enumerate(lat):
                hf, af, hb, ab, g = bufs[ti_ % 2]
                y = ys[ti_ % 2]
                for nm, t in (("HF", hf), ("AF", af), ("HB", hb), ("AB", ab), ("GB", g)):
                    p.dma("sp", t[:, :, :n], self.fm(nm, 0, 512, c0, n))
                for ch in range(4):
                    p.stt("dve", hf[:, ch, :n], af[:, ch, :n], self.lr_hin[:, 0, ch:ch + 1], hf[:, ch, :n], ALU.mult, ALU.add)
                    p.stt("dve", hb[:, ch, :n], ab[:, ch, :n], self.lr_hin[:, 1, ch:ch + 1], hb[:, ch, :n], ALU.mult, ALU.add)
                p.tt("pool", hf[:, :, :n], hf[:, :, :n], hb[:, :, :n], ALU.add)
                p.tt("pool", y[:, :, :n], hf[:, :, :n], g[:, :, :n], ALU.mult)
                p.dma("pool", self.fm("YT", 512, 512, c0, n), y[:, :, :n])


    def phase3(self, l):
        p = self.p
        p.barrier()
        p3y = [[self.Hv(0), self.Hv(1)], [self.Hv(2), self.Hv(3)]]
        p3x = [[self.Fv(0), self.Fv(1)], [self.Fv(2), self.Fv(3)]]
        cnt = 0
        for (c0, n, isctx) in self.tiles:
            if isctx and l == self.DEPTH - 1:
                continue
            w = 1 if isctx else 0
            y, x = p3y[cnt % 2], p3x[cnt % 2]
            for half in range(2):
                p.dma("sp", y[half][:, :, :n], self.fm("YT", half * 512, 512, c0, n))
                p.dma("sp", x[half][:, :, :n], self.fm("X", half * 512, 512, c0, n))
            for f in range(8):
                bk = self.bank()
                for k in range(8):
                    p.mm(bk[:, :n], self.Wo[:, k, f * 128:(f + 1) * 128], y[k // 4][:, k % 4, :n], start=(k == 0), stop=(k == 7))
                xf = x[f // 4][:, f % 4, :n]
                p.stt("dve", xf, bk[:, :n], self.GT[l][:, f, w:w + 1], xf, ALU.mult, ALU.add)
            for half in range(2):
                p.dma("pool", self.fm("X", half * 512, 512, c0, n), x[half][:, :, :n])
            cnt += 1

    def final_norm(self):
        p = self.p
        p.barrier()
        self.norm_alloc()
        for (c0, n, isctx) in self.tiles:
            if isctx:
                continue
            self.norm_tile(None, c0, n, False)
            o0 = c0 - CTX
            for half in range(2):
                ov = V(self.out[half * 512:(half + 1) * 512, o0:o0 + n].rearrange("(c p) t -> p c t", p=128), p.dres("out", half, o0, n))
                p.dma("pool", ov, self.xt[half][:, :, :n])

    def phase1_odd(self, l):
        p, i = self.p, self.i
        C = ODD_COLS
        p.barrier()
        self.norm_alloc()
        stg = [self.Fv(k) for k in (3, 4, 5, 6)]
        Ct = [self.Fv(7, 0), self.Fv(7, 1)]
        St = [self.Fv(7, 2), self.Fv(7, 3)]
        t1 = [self.Fv(8, 0), self.Fv(8, 1)]
        t2 = [self.Fv(8, 2), self.Fv(8, 3)]
        aq = self.Hv(4)
        ak = self.Hv(5, 0)
        k2 = self.Hv(5, 1)
        av = self.Hv(5, 2).re("p (b c) -> p b c", c=128)
        if not hasattr(self, "kmax"):
            self.kmax = p.sb([128, 2], F32, "kmax")
            self.kred = p.sb([128, 1], F32, "kred")
            self.kmx = p.sb([128, 2], F32, "kmx")
        p.memset("dve", self.kmax, 0.0)
        rr = [0]
        srr = 0
        ti = 0
        for (c0, n, isctx) in self.tiles:
            hT = self.norm_tile(l, c0, n, isctx)
            ct, st_ = Ct[ti % 2], St[ti % 2]
            if not isctx:
                p.dma("sp", ct[:, :n], self.cst(i["ropec"][:, c0 - CTX:c0 - CTX + n]))
                p.dma("sp", st_[:, :n], self.cst(i["ropes"][:, c0 - CTX:c0 - CTX + n]))

            def rope_evac(out, col, colr, scale):
                bk = self.proj_fm(hT, n, col, 128)
                if isctx:
                    p.act(out, bk[:, :n], AF.Copy, scale=scale)
                else:
                    bk2 = self.proj_fm(hT, n, colr, 128)
                    a, b = t1[rr[0] % 2], t2[rr[0] % 2]
                    rr[0] += 1
                    p.stt("dve", a[:, :n], bk[:, :n], scale, ct[:, :n], ALU.mult, ALU.mult)
                    p.stt("dve", b[:, :n], bk2[:, :n], scale, st_[:, :n], ALU.mult, ALU.mult)
                    p.tt("pool", out, a[:, :n], b[:, :n], ALU.add)

            for c in range(4):
                rope_evac(aq[:, c, :n], C["q"][0] + c * 128, C["qr"][0] + c * 128, 0.125)
            p.dma("pool", self.fm("AQ", 0, 512, c0, n), aq[:, :, :n])
            rope_evac(ak[:, :n], C["k"][0], C["kr"][0], 1.0)
            for g in range(2):
                for dup in range(2):
                    p.dma("pool", V(self.s["AK"][g, dup * 64:(dup + 1) * 64, c0:c0 + n], p.dres("AK", g, c0, n)),
                          ak[g * 64:(g + 1) * 64, :n])
            p.tt("pool", k2[:, :n], ak[:, :n], ak[:, :n], ALU.mult)
            for g in range(2):
                bk = self.bank()
                p.mm(bk[:, :n], self.ones_bf[g * 64:(g + 1) * 64, :], k2[g * 64:(g + 1) * 64, :n])
                p.reduce(self.kred, bk[:, :n], ALU.max)
                p.tt("dve", self.kmax[:, g:g + 1], self.kmax[:, g:g + 1], self.kred, ALU.max)
            nb = n // 128
            for b in range(nb):
                bk = self.proj_tm(hT, b, C["v"][0], 128)
                p.copy("act" if b % 2 else "dve", av[:, b, :], bk[:, :128])
            p.dma("pool", self.tm("AV", c0, n, 0, 128), av[:, :nb, :])
            for (nm, dst) in (("gs", "GB"), ("gr", "GA")):
                st = stg[srr % 4]
                srr += 1
                for c in range(4):
                    bk = self.proj_fm(hT, n, C[nm][0] + c * 128, 128)
                    p.act(st[:, c, :n], bk[:, :n], AF.Silu)
                p.dma("pool", self.fm(dst, 0, 512, c0, n), st[:, :, :n])
            for (nm, nmr, dst, scale) in (("rq", "rqr", "QT", 1.0), ("rk", "rkr", "KT", 0.125)):
                st = stg[srr % 4]
                srr += 1
                for c in range(2):
                    rope_evac(st[:, c, :n], C[nm][0] + c * 128, C[nmr][0] + c * 128, scale)
                p.dma("pool", self.fm(dst, 0, 256, c0, n), st[:, 0:2, :n])
            vs = self.sq[0]
            for b in range(nb):
                bk = self.proj_tm(hT, b, C["rv"][0], 512)
                p.copy("act" if b % 2 else "dve", vs[:, b, :], bk[:, :])
            p.dma("pool", self.tm("VT", c0, n, 0, 512), vs[:, :nb, :])
            ti += 1
        p.act(self.kmx, self.kmax, AF.Sqrt)
        p.ts("dve", self.kmx, self.kmx, 1.01, None, ALU.mult)

    def swa(self, l):
        p, i = self.p, self.i
        j = l // 2
        T = self.TALL
        p.barrier()
        if not hasattr(self, "sink"):
            self.sink = p.sb([128, 8], F32, "sink")
        p.dma("sp", self.sink, self.cst(i["sink"][j]))
        Kc = [self.Hv(0, 0), self.Hv(0, 1)]
        Vc = self.Hv(0, 2).re("p (b c) -> p b c", c=128)
        q2 = self.Hv(0, 3)
        for g in range(2):
            p.dma("sp", Kc[g][:, :256], V(self.s["AK"][g, :, 0:256], p.dres("AK", g, 0, 256)))
        p.dma("sp", Vc[:, :2, :], self.tm("AV", 0, 256, 0, 128))
        NSEG = self.NSEG
        if NSEG > 1:
            if not hasattr(self, "sw_vlr"):
                self.sw_vlr = p.sb([128, 2], F32, "sw_vlr")
            fb4 = self.Fp[4].bitcast(BF16)
            self.sw_KL = [V(fb4[:, 0, g * 128:(g + 1) * 128], [Res()]) for g in range(2)]
            self.sw_KR = [V(fb4[:, 0, (2 + g) * 128:(3 + g) * 128], [Res()]) for g in range(2)]
            self.sw_VL = V(fb4[:, 0, 512:640], [Res()])
            self.sw_VR = V(fb4[:, 0, 640:768], [Res()])
            self.sw_mL = V(self.Fp[3][:, 0, 0:128], [Res()])
            self.sw_mR = V(self.Fp[3][:, 1, 0:128], [Res()])
            cres = p.dres("cc_kv_in", 0, 0, 128)
            for g in range(2):
                p.dma("sp", V(self.s["cc_kv_in"][g * 128:(g + 1) * 128, 0:128], cres),
                      V(self.s["AK"][g, :, CTX:CTX + 128], p.dres("AK", g, CTX, 128)))
                p.dma("sp", V(self.s["cc_kv_in"][g * 128:(g + 1) * 128, 128:256], cres),
                      V(self.s["AK"][g, :, T - 128:T], p.dres("AK", g, T - 128, 128)))
            p.dma("sp", V(self.s["cc_kv_in"][0:128, 256:384], cres), V(self.s["AV"][CTX:CTX + 128, :], p.dres("AV", "all", CTX, 128)))
            p.dma("sp", V(self.s["cc_kv_in"][128:256, 256:384], cres), V(self.s["AV"][T - 128:T, :], p.dres("AV", "all", T - 128, 128)))
            cout = V(self.s["cc_kv_out"].ap(), p.dres("cc_kv_out", 0, 0, 128))
            p.allgather(cout, V(self.s["cc_kv_in"].ap(), cres), self.groups)
            GK = V(self.Hp[3][:, 0:4, :].rearrange("p a b -> p (a b)").rearrange("p (r g c) -> p r g c", r=4, g=2), [Res()])
            GV = V(self.Hp[2][:, 0:2, :].rearrange("p a b -> p (a b)").rearrange("p (r f c) -> p r f c", r=4, f=2), [Res()])
            for r in range(4):
                p.dma("sp", GK[:, r, :, :], V(self.s["cc_kv_out"][r * 256:(r + 1) * 256, 0:256].rearrange("(g p) c -> p g c", p=128), cout.res))
                p.dma("sp", GV[:, r, :, :], V(self.s["cc_kv_out"][r * 256:(r + 1) * 256, 256:384].rearrange("(f p) c -> p f c", p=128), cout.res))
            for r in range(4):
                wl = self.segw[:, 0, r:r + 1]
                wr = self.segw[:, 1, r:r + 1]
                sel = [(self.sw_VL, GV[:, r, 1, :], wl), (self.sw_VR, GV[:, r, 0, :], wr)]
                for g in range(2):
                    sel.append((self.sw_KL[g], GK[:, r, g, 128:256], wl))
                    sel.append((self.sw_KR[g], GK[:, r, g, 0:128], wr))
                for (dst, src, w) in sel:
                    if r == 0:
                        p.ts("dve", dst, src, w, None, ALU.mult)
                    else:
                        p.stt("dve", dst, src, w, dst, ALU.mult, ALU.add)
            p.reduce(self.sw_vlr[:, 0:1], self.segw[:, 0, :], ALU.add)
            p.reduce(self.sw_vlr[:, 1:2], self.segw[:, 1, :], ALU.add)
            p.ts("dve", self.sw_mL, self.masks[:, 2, :], self.sw_vlr[:, 0:1], None, ALU.mult)
            p.ts("dve", self.sw_mR, self.masks[:, 3, :], self.sw_vlr[:, 1:2], None, ALU.mult)
            p.barrier()
        Kw = [[V(self.Hp[1 + x][:, 0:2, :].rearrange("p a b -> p (a b)"), [Res()]),
               V(self.Hp[1 + x][:, 2:4, :].rearrange("p a b -> p (a b)"), [Res()])] for x in range(2)]
        Vw = [V(self.Hp[3][:, 2 * x:2 * x + 2, :].rearrange("p a b -> p (a b)").rearrange("p (b c) -> p b c", c=128), [Res()])
              for x in range(2)]
        Q = [self.Hv(4), self.Hv(5)]
        Pt = []
        for k in (8, 9):
            fb = self.Fp[k].bitcast(BF16)
            for pl in range(4):
                for h in range(2):
                    Pt.append(V(fb[:, pl, h * 512:(h + 1) * 512], [Res()]))
        fb7 = self.Fp[7].bitcast(BF16)
        ybf = [V(fb7[0:64, 0, 0:512], [Res()]), V(fb7[0:64, 0, 512:1024], [Res()])]
        MB = [self.Fv(0, 0), self.Fv(0, 1)]
        tmp = [self.Fv(0, 2), self.Fv(0, 3), self.Fv(1, 0)]
        esk = self.Fv(1, 1, rows=64)
        den = self.Fv(1, 2, rows=64)
        o = self.Fv(1, 3, rows=64)
        gate = [self.Fv(2, 0, rows=64), self.Fv(2, 1, rows=64)]
        ti = 0
        it = 0
        pi = 0
        import os
        STG = int(os.environ.get("KSWA_STAGE", "9"))
        rot = [self.banks[k] for k in (2, 3, 4, 5, 6)]
        rri = [0]

        def rb():
            bnk = rot[rri[0] % 5]
            rri[0] += 1
            return bnk
        psO = self.banks[0]
        psD = self.banks[1]
        for (c0, n, isctx) in self.tiles:
            x = ti % 2
            Qx = Q[x]
            p.dma("sp", Qx[:, :, :n], self.fm("AQ", 0, 512, c0, n))
            if not isctx:
                lo = max(CTX, c0 - 128)
                hi = min(T, c0 + n + 128)
                off = lo - (c0 - 128)
                for g in range(2):
                    p.dma("sp", Kw[x][g][:, off:off + hi - lo], V(self.s["AK"][g, :, lo:hi], p.dres("AK", g, lo, hi - lo)))
                p.dma("sp", Vw[x][:, off // 128:off // 128 + (hi - lo) // 128, :], self.tm("AV", lo, hi - lo, 0, 128))
                if NSEG > 1 and c0 == CTX:
                    for g in range(2):
                        p.copy("pool", Kw[x][g][:, 0:128], self.sw_KL[g])
                    p.copy("pool", Vw[x][:, 0, :], self.sw_VL)
                if NSEG > 1 and c0 + n == T:
                    for g in range(2):
                        p.copy("pool", Kw[x][g][:, n + 128:n + 256], self.sw_KR[g])
                    p.copy("pool", Vw[x][:, (n + 128) // 128, :], self.sw_VR)
            for qb in range(n // 128):
                qs = slice(qb * 128, (qb + 1) * 128)
                tpos = c0 + qb * 128
                if STG < 0:
                    continue
                for c in range(4):
                    p.tt("pool", q2[:, c * 128:(c + 1) * 128], Qx[:, c, qs], Qx[:, c, qs], ALU.mult)
                for g in range(2):
                    hl = [(par, hs, 4 * g + 2 * hs + par) for par in range(2) for hs in range(2)]
                    bkq = [rb(), rb()]
                    for (par, hs, hq) in hl:
                        r = slice(par * 64, par * 64 + 64)
                        ch = hq // 2
                        p.mm(bkq[par][:, hs * 128:(hs + 1) * 128], self.ones_bf[r, :], q2[r, ch * 128:(ch + 1) * 128])
                    mb = MB[it % 2]
                    if STG < 1:
                        continue
                    for par in range(2):
                        p.act(mb[:, par * 256:(par + 1) * 256], bkq[par][:, :256], AF.Sqrt)
                    p.ts("dve", mb, mb, self.kmx[:, g:g + 1], None, ALU.mult)
                    gt = gate[it % 2]
                    for par in range(2):
                        p.dma("sp", gt[:, par * 256:(par + 1) * 256].re("p (h t) -> p h t", h=2),
                              V(self.s["GB"][g * 256:(g + 1) * 256, tpos:tpos + 128].rearrange("(hs par d) t -> par d hs t", par=2, d=64)[par],
                                p.dres("GB", 2 * g, tpos, 128) + p.dres("GB", 2 * g + 1, tpos, 128)))
                    kb = []
                    for b in range(2):
                        kb.append((Kc[g][:, b * 128:(b + 1) * 128], Vc[:, b, g * 64:(g + 1) * 64], None))
                    if not isctx:
                        for rel in (0, -1, 1):
                            kpos = tpos + rel * 128
                            halo = kpos < CTX or kpos >= T
                            if halo and NSEG == 1:
                                continue
                            wc = kpos - (c0 - 128)
                            m = None if rel == 0 else (self.masks[:, 2, :] if rel == -1 else self.masks[:, 3, :])
                            if halo:
                                m = self.sw_mL if rel == -1 else self.sw_mR
                            kb.append((Kw[x][g][:, wc:wc + 128], Vw[x][:, wc // 128, g * 64:(g + 1) * 64], m))
                    if STG < 2:
                        continue
                    for bi, (Kb, Vb, m) in enumerate(kb):
                        psS = [rb(), rb()]
                        for (par, hs, hq) in hl:
                            r = slice(par * 64, par * 64 + 64)
                            p.mm(psS[par][:, hs * 128:(hs + 1) * 128], Kb[r, :], Qx[r, hq // 2, qs])
                        t = tmp[pi % 3]
                        P = Pt[pi % 16]
                        pi += 1
                        for par in range(2):
                            cs_ = slice(par * 256, (par + 1) * 256)
                            p.tt("dve", t[:, cs_], psS[par][:, :256], mb[:, cs_], ALU.subtract)
                        p.act(P, t, AF.Exp)
                        if m is not None:
                            P3 = P.re("p (h t) -> p h t", h=4)
                            p.tt("dve", P3, P3, V(m.ap.unsqueeze(1).to_broadcast([128, 4, 128]), m.res), ALU.mult)
                        last = bi == len(kb) - 1
                        if STG < 3:
                            continue
                        p.mm(psO[0:64, :], Vb, P, start=(bi == 0), stop=last)
                        p.mm(psD[0:64, :], self.ones_bf[:, 0:64], P, start=(bi == 0), stop=last)
                    if STG < 4:
                        continue
                    for cb, (par, hs, hq) in enumerate(hl):
                        p.act(esk[:, cb * 128:(cb + 1) * 128], mb[0:64, cb * 128:(cb + 1) * 128], AF.Exp, scale=-1.0,
                              bias=self.sink[0:64, hq:hq + 1])
                    p.tt("dve", den, psD[0:64, :], esk, ALU.add)
                    p.recip(den, den)
                    p.tt("dve", o, psO[0:64, :], den, ALU.mult)
                    if STG < 5:
                        continue
                    yb = ybf[it % 2]
                    p.tt("pool", yb, o, gt, ALU.mult)
                    for par in range(2):
                        p.dma("pool", V(self.s["YT"][g * 256:(g + 1) * 256, tpos:tpos + 128].rearrange("(hs par d) t -> par d hs t", par=2, d=64)[par],
                                        p.dres("YT", 2 * g, tpos, 128) + p.dres("YT", 2 * g + 1, tpos, 128)),
                              yb[:, par * 256:(par + 1) * 256].re("p (h t) -> p h t", h=2))
                    it += 1
            ti += 1


def _pc(v, c):
    return np.ascontiguousarray(np.asarray(v, np.float32).reshape(c, 128).T)


def make_consts(SEQ):
    s = np.arange(128)[:, None]
    t = np.arange(128)[None, :]
    same = (s // 64) == (t // 64)
    masks = np.stack([(s <= t) & same, (s >= t) & same, s >= t, s <= t]).astype(np.float32)
    col = np.arange(512)
    rmask = np.stack([np.broadcast_to((col % 64 != 0), (128, 512)),
                      np.broadcast_to((col % 64 != 63), (128, 512))]).astype(np.float32)
    tok = np.arange(SEQ)
    row, cl = tok // 64, tok % 64
    inv = (10000.0 ** (-np.arange(16, dtype=np.float32) / 16)).astype(np.float32)
    ang = np.concatenate([row[:, None].astype(np.float32) * inv[None], cl[:, None].astype(np.float32) * inv[None]], axis=-1)
    cos, sin = np.cos(ang).astype(np.float32), np.sin(ang).astype(np.float32)
    C64 = np.concatenate([cos, cos], axis=1).T
    S64 = np.concatenate([-sin, sin], axis=1).T
    ropec = np.ascontiguousarray(np.concatenate([C64, C64], axis=0))
    ropes = np.ascontiguousarray(np.concatenate([S64, S64], axis=0))
    ident = np.eye(128, dtype=np.float32).astype(ml_dtypes.bfloat16)
    return dict(masks=masks, rmask=rmask, ropec=ropec, ropes=ropes, ident=ident)


def rot_cols(w, nheads, dh=64):
    K = w.shape[0]
    w4 = w.reshape(K, nheads, 2, dh // 2)
    return w4[:, :, ::-1, :].reshape(K, nheads * dh)


def prep_inputs(inp, b, SEQ):
    f = lambda a: np.ascontiguousarray(np.asarray(a, np.float32))
    m = {}
    m["xin"] = np.ascontiguousarray(np.concatenate([f(inp["ctx"][b]).T, f(inp["x"][b])[:SEQ].T], axis=1))
    m["cvec"] = np.ascontiguousarray(np.stack([_pc(inp["c"][b], 8), _pc(inp["c_ctx"], 8)], axis=-1))
    m["ada_w"] = f(inp["ada_w"])
    m["ada_b"] = np.stack([_pc(inp["ada_b"][l], 24) for l in range(4)])
    m["norm_g"] = np.stack([_pc(inp["norm_g"][l], 8) for l in range(4)])
    m["final_g"] = _pc(inp["final_g"], 8)
    m["e_w_in"] = f(inp["e_w_in"])
    m["e_w_out"] = f(inp["e_w_out"])
    ow = f(inp["o_w_in"])
    C = ODD_COLS
    rot = [np.concatenate([rot_cols(ow[j][:, C["q"][0]:C["q"][1]], 8), rot_cols(ow[j][:, C["k"][0]:C["k"][1]], 2),
                           rot_cols(ow[j][:, C["rq"][0]:C["rq"][1]], 4), rot_cols(ow[j][:, C["rk"][0]:C["rk"][1]], 4)], axis=1)
           for j in range(2)]
    m["o_w_in"] = np.ascontiguousarray(np.concatenate([ow, np.stack(rot)], axis=2))
    m["o_w_out"] = f(inp["o_w_out"])
    m["gla_up"] = np.ascontiguousarray(np.stack([f(inp["gla_up_fw"]), f(inp["gla_up_bw"])], axis=1))
    m["gla_b"] = np.stack([np.stack([_pc(inp["gla_b_fw"][j], 2), _pc(inp["gla_b_bw"][j], 2)]) for j in range(2)])
    m["gla_ng"] = np.stack([_pc(inp["gla_norm_g"][j], 1) for j in range(2)])
    m["ret_ng"] = np.stack([_pc(inp["ret_norm_g"][j], 1) for j in range(2)])
    cw = f(inp["lru_conv_w"])
    m["conv_w"] = np.ascontiguousarray(cw.reshape(2, 4, 4, 128).transpose(0, 3, 2, 1))
    m["conv_b"] = np.stack([_pc(inp["lru_conv_b"][j], 4) for j in range(2)])
    m["lru_w"] = np.ascontiguousarray(np.stack([f(inp["lru_wa_fw"]), f(inp["lru_wx_fw"]), f(inp["lru_wa_bw"]), f(inp["lru_wx_bw"])], axis=1))
    m["lru_b"] = np.stack([np.stack([_pc(inp[k][j], 4) for k in ("lru_ba_fw", "lru_bx_fw", "lru_ba_bw", "lru_bx_bw")]) for j in range(2)])
    m["lru_lam"] = np.stack([np.stack([_pc(inp["lru_lam_fw"][j], 4), _pc(inp["lru_lam_bw"][j], 4)]) for j in range(2)])
    m["sink"] = np.ascontiguousarray(np.broadcast_to(f(inp["swa_sink"])[:, None, :], (2, 128, 8)))
    rd = np.stack([f(inp["ret_dec_fw"]), f(inp["ret_dec_bw"])], axis=1)
    m["ret_dec"] = np.ascontiguousarray(np.repeat(rd, 64, axis=2).reshape(2, 2, 2, 128).transpose(0, 1, 3, 2))
    m.update(make_consts(SEQ))
    return m


_CACHE = {}


def seg_weights(s, nseg):
    w = np.zeros((4, 4), np.float32)
    if nseg > 1:
        if s > 0:
            w[0, s - 1] = 1.0
        if s < nseg - 1:
            w[1, s + 1] = 1.0
        w[2, :s] = 1.0
        w[3, s + 1:] = 1.0
    return np.ascontiguousarray(np.broadcast_to(w.reshape(1, 16), (128, 16)))


def make_in_maps(inp, SEQ_TOTAL, NSEG):
    maps = []
    L = SEQ_TOTAL // NSEG
    base = [prep_inputs(inp, b, SEQ_TOTAL) for b in range(2)]
    for b in range(2):
        for s in range(NSEG):
            m = dict(base[b])
            if NSEG > 1:
                xin = base[b]["xin"]
                m["xin"] = np.ascontiguousarray(np.concatenate([xin[:, :CTX], xin[:, CTX + s * L:CTX + (s + 1) * L]], axis=1))
                m["ropec"] = np.ascontiguousarray(base[b]["ropec"][:, s * L:(s + 1) * L])
                m["ropes"] = np.ascontiguousarray(base[b]["ropes"][:, s * L:(s + 1) * L])
            m["segw"] = seg_weights(s, NSEG)
            maps.append(m)
    return maps


def run(inp, SEQ_TOTAL, DEPTH, dbg=(), NSEG=4):
    key = (SEQ_TOTAL, DEPTH, tuple(dbg), NSEG)
    if key not in _CACHE:
        _CACHE[key] = Builder(SEQ_TOTAL // NSEG, DEPTH, dbg, NSEG).build()
    nc = _CACHE[key]
    in_maps = make_in_maps(inp, SEQ_TOTAL, NSEG)
    res = run_bass_kernel_spmd(nc, in_maps, core_ids=list(range(2 * NSEG)))
    return res


def gather_out(res, NSEG):
    outs = []
    for b in range(2):
        outs.append(np.concatenate([res.results[b * NSEG + s]["out"].T for s in range(NSEG)], axis=0))
    return np.stack(outs).astype(np.float32)


def kernel(**inputs):
    SEQ = inputs["x"].shape[1]
    res = run(inputs, SEQ, 4, NSEG=4)
    return gather_out(res, 4)
```

```python
import numpy as np
import ml_dtypes
from contextlib import ExitStack
import concourse.bass as bass
import concourse.mybir as mybir
from concourse.bass_utils import run_bass_kernel_spmd

F32 = mybir.dt.float32
BF16 = mybir.dt.bfloat16
AF = mybir.ActivationFunctionType
ALU = mybir.AluOpType
AX = mybir.AxisListType

D = 1024
CTX = 256
EPS = 1e-6
import os as _os
SAME_ENG_SYNC = _os.environ.get("KSYNC", "1") == "1"


class Res:
    __slots__ = ("w", "r")

    def __init__(self):
        self.w = None
        self.r = {}


class V:
    __slots__ = ("ap", "res")

    def __init__(self, ap, res):
        self.ap = ap
        self.res = res

    def __getitem__(self, key):
        return V(self.ap[key], self.res)

    def re(self, s, **kw):
        return V(self.ap.rearrange(s, **kw), self.res)

    def bc(self, shape):
        return V(self.ap.to_broadcast(list(shape)), self.res)


class Prog:
    def __init__(self, nc, es):
        self.nc = nc
        self.es = es
        self.ops = {e: [] for e in ("pe", "act", "dve", "pool", "sp")}
        self.sem = {}
        self.cnt = {}
        for e in ("pe", "act", "dve", "pool"):
            self.sem[e] = es.enter_context(nc.semaphore("s_" + e))
            self.cnt[e] = 0
        self.dsem = {}
        self.dcnt = {}
        self.drr = {}
        for q, n in (("sp", 16), ("pool", 8)):
            self.dsem[q] = [es.enter_context(nc.semaphore("d_%s%d" % (q, i))) for i in range(n)]
            self.dcnt[q] = [0] * n
            self.drr[q] = 0
        self.waited = {e: {} for e in self.ops}
        self.dram_res = {}
        self.n_sb = 0

    def sb(self, shape, dt=F32, name=None):
        self.n_sb += 1
        t = self.es.enter_context(self.nc.sbuf_tensor("sb_" + (name or ("t%d" % self.n_sb)), list(shape), dt))
        return V(t[:] if len(shape) == 2 else t.ap(), [Res()])

    def ps(self, shape, dt=F32, name=None):
        self.n_sb += 1
        t = self.es.enter_context(self.nc.psum_tensor("ps_" + (name or ("t%d" % self.n_sb)), list(shape), dt))
        return V(t.ap(), [Res()])

    def dres(self, name, rowkey, c0, n):
        out = []
        for b in range(c0 // 128, (c0 + n - 1) // 128 + 1):
            k = (name, rowkey, b)
            if k not in self.dram_res:
                self.dram_res[k] = Res()
            out.append(self.dram_res[k])
        return out

    def emit(self, eng, fn, reads, writes, dma=False):
        toks = []
        for v in reads:
            for r in v.res:
                if r.w is not None:
                    toks.append(r.w)
        for v in writes:
            for r in v.res:
                if r.w is not None:
                    toks.append(r.w)
                toks.extend(r.r.values())
        waits = []
        wd = self.waited[eng]
        for (s, val, te) in toks:
            if te == eng and not dma and (eng == "pe" or not SAME_ENG_SYNC):
                continue
            if wd.get(id(s), 0) >= val:
                continue
            wd[id(s)] = val
            waits.append((s, val))
        if dma:
            pool = self.dsem[eng]
            j = self.drr[eng]
            self.drr[eng] = (j + 1) % len(pool)
            s = pool[j]
            k = self.dcnt[eng][j]
            if k > 0 and wd.get(id(s), 0) < 16 * k:
                wd[id(s)] = 16 * k
                waits.append((s, 16 * k))
            self.dcnt[eng][j] = k + 1
            tok = (s, 16 * (k + 1), eng + "q")
            inc = 16
        else:
            self.cnt[eng] += 1
            tok = (self.sem[eng], self.cnt[eng], eng)
            inc = 1
        self.ops[eng].append((waits, fn, tok[0], inc))
        for v in reads:
            for r in v.res:
                r.r[id(tok[0])] = tok
        for v in writes:
            for r in v.res:
                r.w = tok
                r.r = {}
        return tok

    @staticmethod
    def _a(x):
        return x.ap if isinstance(x, V) else x

    @staticmethod
    def _vs(*xs):
        return [x for x in xs if isinstance(x, V)]

    def mm(self, out, lhsT, rhs, start=True, stop=True):
        self.emit("pe", lambda e: e.matmul(out.ap, lhsT.ap, rhs.ap, start=start, stop=stop),
                  [lhsT, rhs], [out])

    def tr(self, out, in_, ident):
        self.emit("pe", lambda e: e.transpose(out.ap, in_.ap, ident.ap), [in_, ident], [out])

    def act(self, out, in_, func, bias=None, scale=None, accum=None, eng="act"):
        kw = {}
        if bias is not None:
            kw["bias"] = self._a(bias)
        if scale is not None:
            kw["scale"] = self._a(scale)
        if accum is not None:
            kw["accum_out"] = accum.ap
        self.emit("act", lambda e: e.activation(out.ap, in_.ap, func, **kw),
                  self._vs(in_, bias, scale), self._vs(out, accum))

    def ts(self, eng, out, in0, s1, s2, op0, op1=None):
        a1, a2 = self._a(s1), self._a(s2)
        if op1 is None:
            f = lambda e: e.tensor_scalar(out.ap, in0.ap, a1, None, op0)
        else:
            f = lambda e: e.tensor_scalar(out.ap, in0.ap, a1, a2, op0, op1)
        self.emit(eng, f, self._vs(in0, s1, s2), [out])

    def tt(self, eng, out, in0, in1, op):
        self.emit(eng, lambda e: e.tensor_tensor(out.ap, in0.ap, in1.ap, op), [in0, in1], [out])

    def stt(self, eng, out, in0, scalar, in1, op0, op1):
        sc = self._a(scalar)
        self.emit(eng, lambda e: e.scalar_tensor_tensor(out.ap, in0.ap, sc, in1.ap, op0, op1),
                  self._vs(in0, scalar, in1), [out])

    def copy(self, eng, out, in_):
        if eng == "act":
            self.emit("act", lambda e: e.copy(out.ap, in_.ap), [in_], [out])
        else:
            self.emit(eng, lambda e: e.tensor_copy(out.ap, in_.ap), [in_], [out])

    def scan(self, out, d0, d1, init, op0=ALU.mult, op1=ALU.add):
        ia = self._a(init)
        self.emit("dve", lambda e: e.tensor_tensor_scan(out.ap, d0.ap, d1.ap, ia, op0, op1),
                  self._vs(d0, d1, init), [out])

    def memset(self, eng, out, val):
        self.emit(eng, lambda e: e.memset(out.ap, val), [], [out])

    def recip(self, out, in_):
        self.emit("dve", lambda e: e.reciprocal(out.ap, in_.ap), [in_], [out])

    def reduce(self, out, in_, op, axis=AX.X):
        self.emit("dve", lambda e: e.tensor_reduce(out.ap, in_.ap, axis, op), [in_], [out])

    def dma(self, q, out, in_, slow=False):
        if slow:
            self.emit(q, lambda e: e.dma_start(out=out.ap, in_=in_.ap, allow_slow_non_contiguous=True), [in_], [out], dma=True)
        else:
            self.emit(q, lambda e: e.dma_start(out=out.ap, in_=in_.ap), [in_], [out], dma=True)

    def allgather(self, out, in_, groups):
        self.n_cc = getattr(self, "n_cc", 0) + 1
        sem = self.es.enter_context(self.nc.semaphore("cc%d" % self.n_cc))
        eng = "pool"
        toks = []
        for r in in_.res:
            if r.w is not None:
                toks.append(r.w)
        for r in out.res:
            if r.w is not None:
                toks.append(r.w)
            toks.extend(r.r.values())
        waits = []
        wd = self.waited[eng]
        for (sm, val, te) in toks:
            if wd.get(id(sm), 0) >= val:
                continue
            wd[id(sm)] = val
            waits.append((sm, val))
        oa, ia = out.ap, in_.ap
        fn = lambda e: e.collective_compute("AllGather", ALU.bypass, replica_groups=groups, ins=[ia], outs=[oa])
        tok = (sem, 1, "ccq")
        self.ops[eng].append((waits, fn, sem, None))
        self.cc_sems = getattr(self, "cc_sems", []) + [sem]
        for r in in_.res:
            r.r[id(sem)] = tok
        for r in out.res:
            r.w = tok
            r.r = {}

    def barrier(self):
        allw = []
        for q in self.dsem:
            for sm, k in zip(self.dsem[q], self.dcnt[q]):
                if k > 0:
                    allw.append((sm, 16 * k))
        for e in ("pe", "act", "dve", "pool"):
            if self.cnt[e] > 0:
                allw.append((self.sem[e], self.cnt[e]))
        for sm in getattr(self, "cc_sems", []):
            allw.append((sm, 1))
        for e in self.ops:
            wd = self.waited[e]
            waits = []
            for (sm, val) in allw:
                if sm is self.sem.get(e):
                    continue
                if wd.get(id(sm), 0) >= val:
                    continue
                wd[id(sm)] = val
                waits.append((sm, val))
            if waits:
                self.ops[e].append((waits, None, None, 0))

    def finish(self):
        nc = self.nc
        fin = []
        for q in self.dsem:
            for s, k in zip(self.dsem[q], self.dcnt[q]):
                if k > 0:
                    fin.append((s, 16 * k))
        for e in ("pe", "act", "dve", "pool"):
            if self.cnt[e] > 0:
                fin.append((self.sem[e], self.cnt[e]))
        ops = self.ops

        def run(e, lst, extra=()):
            for waits, fn, s, inc in lst:
                for (ws, wv) in waits:
                    e.wait_ge(ws, wv)
                if fn is not None:
                    if inc is None:
                        fn(e).then_inc(s)
                    else:
                        fn(e).then_inc(s, inc)
            for (ws, wv) in extra:
                e.wait_ge(ws, wv)

        with nc.Block() as block:
            @block.sync
            def _(e):
                run(e, ops["sp"], fin)

            @block.tensor
            def _(e):
                run(e, ops["pe"])

            @block.scalar
            def _(e):
                run(e, ops["act"])

            @block.vector
            def _(e):
                run(e, ops["dve"])

            @block.gpsimd
            def _(e):
                run(e, ops["pool"])


EVEN_COLS = dict(q=(0, 256), k=(256, 512), v=(512, 1024), gg=(1024, 1536), lrf=(1536, 1552),
                 lrb=(1552, 1568), xr=(1568, 2080), gl=(2080, 2592))
ODD_COLS = dict(q=(0, 512), k=(512, 640), v=(640, 768), gs=(768, 1280), rq=(1280, 1536),
                rk=(1536, 1792), rv=(1792, 2304), gr=(2304, 2816),
                qr=(2816, 3328), kr=(3328, 3456), rqr=(3456, 3712), rkr=(3712, 3968))


class Builder:
    def __init__(self, SEQ, DEPTH, dbg=(), NSEG=1):
        self.NSEG = NSEG
        self.SEQ = SEQ
        self.DEPTH = DEPTH
        self.TALL = CTX + SEQ
        self.dbg = dbg
        self.nc = bass.Bass("TRN2", target_bir_lowering=False)
        self.es = ExitStack()
        self.tiles = [(0, CTX, True)] + [(CTX + i * 512, 512, False) for i in range(SEQ // 512)]

    def din(self, name, shape, dt=F32):
        return self.nc.dram_tensor(name, list(shape), dt, kind="ExternalInput")

    def dscr(self, name, shape, dt=F32):
        kind = "ExternalOutput" if name in self.dbg else "Internal"
        return self.nc.dram_tensor(name, list(shape), dt, kind=kind)

    def declare(self):
        T = self.TALL
        S = self.SEQ
        i = {}
        i["xin"] = self.din("xin", [D, T])
        i["cvec"] = self.din("cvec", [128, 8, 2])
        i["ada_w"] = self.din("ada_w", [4, D, 3 * D])
        i["ada_b"] = self.din("ada_b", [4, 128, 24])
        i["norm_g"] = self.din("norm_g", [4, 128, 8])
        i["final_g"] = self.din("final_g", [128, 8])
        i["e_w_in"] = self.din("e_w_in", [2, D, 2592])
        i["e_w_out"] = self.din("e_w_out", [2, D, D])
        i["o_w_in"] = self.din("o_w_in", [2, D, 3968])
        i["o_w_out"] = self.din("o_w_out", [2, D, D])
        i["gla_up"] = self.din("gla_up", [2, 2, 16, 256])
        i["gla_b"] = self.din("gla_b", [2, 2, 128, 2])
        i["gla_ng"] = self.din("gla_ng", [2, 128, 1])
        i["ret_ng"] = self.din("ret_ng", [2, 128, 1])
        i["conv_w"] = self.din("conv_w", [2, 128, 4, 4])
        i["conv_b"] = self.din("conv_b", [2, 128, 4])
        i["lru_w"] = self.din("lru_w", [2, 4, 4, 128, 128])
        i["lru_b"] = self.din("lru_b", [2, 4, 128, 4])
        i["lru_lam"] = self.din("lru_lam", [2, 2, 128, 4])
        i["sink"] = self.din("sink", [2, 128, 8])
        i["ret_dec"] = self.din("ret_dec", [2, 2, 128, 2])
        i["ident"] = self.din("ident", [128, 128], BF16)
        i["masks"] = self.din("masks", [4, 128, 128])
        i["rmask"] = self.din("rmask", [2, 128, 512])
        i["ropec"] = self.din("ropec", [128, S])
        i["ropes"] = self.din("ropes", [128, S])
        i["segw"] = self.din("segw", [128, 16])
        self.i = i
        s = {}
        s["cc_h_in"] = self.dscr("cc_h_in", [128, 12])
        s["cc_h_out"] = self.dscr("cc_h_out", [512, 12])
        s["cc_kv_in"] = self.dscr("cc_kv_in", [256, 384], BF16)
        s["cc_kv_out"] = self.dscr("cc_kv_out", [1024, 384], BF16)
        s["cc_s_in"] = self.dscr("cc_s_in", [512, 129])
        s["cc_s_out"] = self.dscr("cc_s_out", [2048, 129])
        s["cc_l_in"] = self.dscr("cc_l_in", [128, 16])
        s["cc_l_out"] = self.dscr("cc_l_out", [512, 16])
        s["AF"] = self.dscr("AF", [512, T])
        s["HB"] = self.dscr("HB", [512, T])
        s["AB"] = self.dscr("AB", [512, T])
        s["X"] = self.dscr("X", [D, T])
        s["YT"] = self.dscr("YT", [D, T], BF16)
        s["QT"] = self.dscr("QT", [256, T])
        s["KT"] = self.dscr("KT", [256, T])
        s["VT"] = self.dscr("VT", [T, 512], BF16)
        s["GA"] = self.dscr("GA", [512, T])
        s["LR"] = self.dscr("LR", [2, 16, T])
        s["XR"] = self.dscr("XR", [512, T])
        s["GB"] = self.dscr("GB", [512, T])
        s["OF"] = self.dscr("OF", [512, T])
        s["HF"] = self.dscr("HF", [512, T])
        s["AQ"] = self.dscr("AQ", [512, T], BF16)
        s["AK"] = self.dscr("AK", [2, 128, T], BF16)
        s["AV"] = self.dscr("AV", [T, 128], BF16)
        self.s = s
        self.out = self.nc.dram_tensor("out", [D, S], F32, kind="ExternalOutput")

    def dv(self, handle, name, rowkey, ap, c0, n):
        return V(ap, self.p.dres(name, rowkey, c0, n))

    def fm(self, name, r0, nrows, c0, n, src=None):
        h = (src or self.s)[name]
        if nrows <= 128:
            ap = h[r0:r0 + nrows, c0:c0 + n]
            keys = [r0 // 128]
        else:
            ap = h[r0:r0 + nrows, c0:c0 + n].rearrange("(c p) t -> p c t", p=128)
            keys = list(range(r0 // 128, (r0 + nrows) // 128))
        res = []
        for k in keys:
            res += self.p.dres(name, k, c0, n)
        return V(ap, res)

    def tm(self, name, c0, n, col0, ncol):
        h = self.s[name]
        if n <= 128:
            ap = h[c0:c0 + n, col0:col0 + ncol]
        else:
            ap = h[c0:c0 + n, col0:col0 + ncol].rearrange("(b p) v -> p b v", p=128)
        return V(ap, self.p.dres(name, "all", c0, n))

    def cst(self, ap):
        return V(ap, [])

    def build(self):
        nc, es = self.nc, self.es
        self.declare()
        p = self.p = Prog(nc, es)
        i = self.i
        self.Fp = [es.enter_context(nc.sbuf_tensor("poolF%d" % k, [128, 4, 512], F32)) for k in range(10)]
        self.Hp = [es.enter_context(nc.sbuf_tensor("poolH%d" % k, [128, 4, 512], BF16)) for k in range(6)]
        self.ones_bf = p.sb([128, 128], BF16, "ones_bf")
        p.memset("pool", self.ones_bf, 1.0)
        self.ones_f = p.sb([128, 512], F32, "ones_f")
        p.memset("pool", self.ones_f, 1.0)
        self.ident = p.sb([128, 128], BF16, "ident")
        p.dma("sp", self.ident, self.cst(i["ident"].ap()))
        self.masks = p.sb([128, 4, 128], F32, "masks")
        p.dma("sp", self.masks, self.cst(i["masks"].ap().rearrange("m p t -> p m t")))
        self.rmask = p.sb([128, 2, 512], F32, "rmask")
        p.dma("sp", self.rmask, self.cst(i["rmask"].ap().rearrange("m p t -> p m t")))
        self.segw = p.sb([128, 4, 4], F32, "segw")
        p.dma("sp", self.segw, self.cst(i["segw"].ap().rearrange("p (k r) -> p k r", r=4)))
        self.groups = [[0, 1, 2, 3], [4, 5, 6, 7]]
        self.Wb = p.sb([128, 8, 3968], BF16, "Wb")
        self.Wo = p.sb([128, 8, 1024], BF16, "Wo")
        self.wstg = [p.sb([128, 512], F32, "wstg%d" % k) for k in range(2)]
        self.wrr = 0
        self.banks = [p.ps([128, 512], F32, "bank%d" % b) for b in range(7)]
        self.bank_bf = p.ps([128, 1024], BF16, "bankbf")
        self.brr = 0
        self.prologue()
        p.barrier()
        stg = [self.Fv(k) for k in range(4)]
        cnt = 0
        for (c0, n, isctx) in self.tiles:
            for half in range(2):
                st = stg[cnt % 4]
                cnt += 1
                p.dma("sp", st[:, :, :n], self.cst(i["xin"][half * 512:(half + 1) * 512, c0:c0 + n].rearrange("(c p) t -> p c t", p=128)))
                p.dma("pool", self.fm("X", half * 512, 512, c0, n), st[:, :, :n])
        for l in range(self.DEPTH):
            j = l // 2
            if l % 2 == 0:
                self.load_w_in(i["e_w_in"][j], 2592)
                self.load_w_out(i["e_w_out"][j])
                self.phase1_even(l)
                self.linattn(l, "gla")
                self.lru(l)
            else:
                self.load_w_in(i["o_w_in"][j], 3968)
                self.load_w_out(i["o_w_out"][j])
                self.phase1_odd(l)
                import os
                if "ret" not in os.environ.get("KSKIP", ""):
                    self.linattn(l, "ret")
                if "swa" not in os.environ.get("KSKIP", ""):
                    self.swa(l)
            self.phase3(l)
        self.final_norm()
        p.finish()
        return nc

    def Fv(self, k, pl=None, npl=1, rows=128):
        t = self.Fp[k]
        if pl is None:
            return V(t.ap(), [Res()])
        if npl == 1:
            return V(t[0:rows, pl, :], [Res()])
        return V(t[0:rows, pl:pl + npl, :], [Res()])

    def Hv(self, k, pl=None, npl=1):
        t = self.Hp[k]
        if pl is None:
            return V(t.ap(), [Res()])
        if npl == 1:
            return V(t[:, pl, :], [Res()])
        return V(t[:, pl:pl + npl, :], [Res()])

    def bank(self):
        b = self.banks[self.brr]
        self.brr = (self.brr + 1) % len(self.banks)
        return b

    def prologue(self):
        p, i = self.p, self.i
        cv = p.sb([128, 8, 2], F32, "cvec")
        p.dma("sp", cv, self.cst(i["cvec"].ap()))
        sc = p.sb([128, 8, 2], F32, "silu_c")
        p.act(sc, cv, AF.Silu)
        self.GS, self.SH, self.GT = [], [], []
        wb = [[self.Fv(0), self.Fv(1)], [self.Fv(2), self.Fv(3)]]
        cnt = 0
        for l in range(self.DEPTH):
            ab = p.sb([128, 24], F32, "ada_b%d" % l)
            p.dma("sp", ab, self.cst(i["ada_b"][l]))
            ng = p.sb([128, 8], F32, "norm_g%d" % l)
            p.dma("sp", ng, self.cst(i["norm_g"][l]))
            mod = p.sb([128, 24, 2], F32, "mod%d" % l)
            bk = self.bank()
            for jg in range(6):
                w2 = wb[cnt % 2]
                cnt += 1
                for half in range(2):
                    p.dma("sp", w2[half], self.cst(i["ada_w"][l, half * 512:(half + 1) * 512, jg * 512:(jg + 1) * 512].rearrange("(k p) c -> p k c", p=128)))
                for j4 in range(4):
                    jj = jg * 4 + j4
                    for k in range(8):
                        p.mm(bk[:, jj * 2:jj * 2 + 2], w2[k // 4][:, k % 4, j4 * 128:(j4 + 1) * 128], sc[:, k, :],
                             start=(k == 0), stop=(k == 7))
            p.tt("dve", mod, bk[:, 0:48].re("p (j c) -> p j c", c=2), ab.re("p (j o) -> p j o", o=1).bc([128, 24, 2]), ALU.add)
            gs = p.sb([128, 8, 2], F32, "gs%d" % l)
            p.ts("dve", gs, mod[:, 8:16, :], 1.0, None, ALU.add)
            p.tt("dve", gs, gs, ng.re("p (j o) -> p j o", o=1).bc([128, 8, 2]), ALU.mult)
            self.GS.append(gs)
            self.SH.append(mod[:, 0:8, :])
            self.GT.append(mod[:, 16:24, :])
        self.fg = p.sb([128, 8], F32, "final_g")
        p.dma("sp", self.fg, self.cst(i["final_g"].ap()))

    def load_w_in(self, wap, ncol):
        p = self.p
        for k in range(8):
            for c0 in range(0, ncol, 512):
                w = min(512, ncol - c0)
                st = self.wstg[self.wrr % 2]
                self.wrr += 1
                p.dma("sp", st[:, :w], self.cst(wap[k * 128:(k + 1) * 128, c0:c0 + w]))
                p.copy("pool", self.Wb[:, k, c0:c0 + w], st[:, :w])

    def load_w_out(self, wap):
        p = self.p
        for k in range(8):
            for c0 in range(0, 1024, 512):
                st = self.wstg[self.wrr % 2]
                self.wrr += 1
                p.dma("sp", st[:, :512], self.cst(wap[k * 128:(k + 1) * 128, c0:c0 + 512]))
                p.copy("pool", self.Wo[:, k, c0:c0 + 512], st[:, :512])

    def norm_alloc(self):
        self.xt = [self.Fv(0), self.Fv(1)]
        self.sq = [self.Hv(0), self.Hv(1)]
        self.hT = [self.Hv(2), self.Hv(3)]
        self.rstd = self.Fv(2, 0)

    def norm_tile(self, l, c0, n, isctx, src="X"):
        p = self.p
        w = 1 if isctx else 0
        xt, sq, hT, rstd = self.xt, self.sq, self.hT, self.rstd
        bk = self.bank()
        for half in range(2):
            p.dma("sp", xt[half][:, :, :n], self.fm(src, half * 512, 512, c0, n))
            p.act(sq[half][:, :, :n], xt[half][:, :, :n], AF.Square)
        for k in range(8):
            p.mm(bk[:, :n], self.ones_bf, sq[k // 4][:, k % 4, :n], start=(k == 0), stop=(k == 7))
        p.ts("dve", rstd[:, :n], bk[:, :n], 1.0 / D, EPS, ALU.mult, ALU.add)
        p.act(rstd[:, :n], rstd[:, :n], AF.Sqrt)
        p.recip(rstd[:, :n], rstd[:, :n])
        for k in range(8):
            eng = "dve" if k % 2 == 0 else "pool"
            xk = xt[k // 4][:, k % 4, :n]
            p.tt(eng, xk, xk, rstd[:, :n], ALU.mult)
            if l is None:
                p.act(xk, xk, AF.Identity, scale=self.fg[:, k:k + 1])
            else:
                p.act(hT[k // 4][:, k % 4, :n], xk, AF.Identity, bias=self.SH[l][:, k, w:w + 1],
                      scale=self.GS[l][:, k, w:w + 1])
        return lambda k: hT[k // 4][:, k % 4, :]

    def proj_fm(self, hT, n, col0, m):
        p = self.p
        bk = self.bank()
        for k in range(8):
            p.mm(bk[:m, :n], self.Wb[:, k, col0:col0 + m], hT(k)[:, :n], start=(k == 0), stop=(k == 7))
        return bk

    def proj_tm(self, hT, blk, col0, m):
        p = self.p
        bk = self.bank()
        for k in range(8):
            p.mm(bk[:, :m], hT(k)[:, blk * 128:(blk + 1) * 128], self.Wb[:, k, col0:col0 + m],
                 start=(k == 0), stop=(k == 7))
        return bk

    def phase1_even(self, l):
        p = self.p
        C = EVEN_COLS
        p.barrier()
        self.norm_alloc()
        stg = [self.Fv(k) for k in (3, 4, 5, 6)]
        vstg = [self.Hv(4), self.Hv(5)]
        lrstg = [self.Fv(7, k, rows=16) for k in range(4)]
        srr = 0
        ti = 0
        for (c0, n, isctx) in self.tiles:
            hT = self.norm_tile(l, c0, n, isctx)
            for (nm, dst, scale) in (("q", "QT", 0.125), ("k", "KT", None)):
                st = stg[srr % 4]
                srr += 1
                for c in range(2):
                    bk = self.proj_fm(hT, n, C[nm][0] + c * 128, 128)
                    if scale is not None:
                        p.act(st[:, c, :n], bk[:, :n], AF.Copy, scale=scale)
                    else:
                        p.copy("dve", st[:, c, :n], bk[:, :n])
                p.dma("pool", self.fm(dst, 0, 256, c0, n), st[:, 0:2, :n])
            for (nm, dst, silu) in (("gg", "GA", True), ("xr", "XR", False), ("gl", "GB", True)):
                st = stg[srr % 4]
                srr += 1
                for c in range(4):
                    bk = self.proj_fm(hT, n, C[nm][0] + c * 128, 128)
                    if silu:
                        p.act(st[:, c, :n], bk[:, :n], AF.Silu)
                    else:
                        p.copy("dve", st[:, c, :n], bk[:, :n])
                p.dma("pool", self.fm(dst, 0, 512, c0, n), st[:, :, :n])
            for d, nm in enumerate(("lrf", "lrb")):
                bk = self.proj_fm(hT, n, C[nm][0], 16)
                ls = lrstg[(2 * ti + d) % 4]
                p.copy("dve", ls[:, :n], bk[:16, :n])
                p.dma("pool", V(self.s["LR"][d, :, c0:c0 + n], p.dres("LR", d, c0, n)), ls[:, :n])
            vs = vstg[ti % 2]
            nb = n // 128
            for b in range(nb):
                bk = self.proj_tm(hT, b, C["v"][0], 512)
                p.copy("act" if b % 2 else "dve", vs[:, b, :], bk[:, :])
            p.dma("pool", self.tm("VT", c0, n, 0, 512), vs[:, :nb, :])
            ti += 1

    def linattn(self, l, kind):
        p, i = self.p, self.i
        j = l // 2
        NSEG = self.NSEG
        p.barrier()
        if not hasattr(self, "la_S"):
            self.la_dec = [p.sb([128, 8], F32, "la_dec%d" % k) for k in range(2)]
            self.la_S = [p.sb([128, 128], F32, "la_S%d" % k) for k in range(2)]
            self.la_Sb = [[p.sb([128, 128], BF16, "la_Sb%d_%d" % (k, m)) for m in range(2)] for k in range(2)]
            self.la_att = [p.sb([128, 128], BF16, "la_att%d" % k) for k in range(4)]
            self.la_up = p.sb([16, 2, 256], F32, "la_up")
            self.la_nb = p.sb([128, 2, 2], F32, "la_nb")
            self.la_ng = p.sb([128, 1], F32, "la_ng")
            self.la_Lr = p.sb([128, 2, 2], F32, "la_Lr")
            self.la_asum = p.sb([128, 4], F32, "la_asum")
            self.la_ared = p.sb([128, 1], F32, "la_ared")
            self.la_tmpS = p.sb([128, 128], F32, "la_tmpS")
        la_q = [self.Fv(0, 0), self.Fv(0, 1)]
        la_k = [self.Fv(0, 2), self.Fv(0, 3)]
        la_L = [self.Fv(1, 0), self.Fv(1, 1)]
        la_b = [self.Fv(1, 2), self.Fv(1, 3)]
        la_e = [self.Fv(2, 0), self.Fv(2, 1)]
        la_lr = [self.Fv(2, 2, rows=16), self.Fv(2, 3, rows=16)]
        la_v = [self.Hv(0), self.Hv(1)]
        la_qi = [self.Hv(2, 0), self.Hv(2, 1)]
        la_ki = [self.Hv(2, 2), self.Hv(2, 3)]
        la_ks = [self.Hv(3, 0), self.Hv(3, 1)]
        la_kst = [self.Hv(3, 2).re("p (b c) -> p b c", c=128), self.Hv(3, 3).re("p (b c) -> p b c", c=128)]
        la_o = [self.Fv(3, 0, 2), self.Fv(3, 2, 2)]
        la_of = [self.Fv(4, 0, 2), self.Fv(4, 2, 2)]
        la_g = [self.Fv(5, 0, 2), self.Fv(5, 2, 2)]
        la_rs = self.Fv(6, 0)
        la_o2 = self.Hv(4, 0)
        la_y = [self.Hv(5, 0, 2), self.Hv(5, 2, 2)]
        la_sst = self.Fv(7, 0, 2).re("p a b -> p (a b)")
        la_G = [V(self.Fp[8 + d].ap().rearrange("p a b -> p (a b)")[:, 0:1032].rearrange("p (r q c) -> p r q c", r=4, q=2), [Res()])
                for d in range(2)]
        if kind == "gla":
            p.dma("sp", self.la_up, self.cst(i["gla_up"][j].rearrange("d r c -> r d c")))
            p.dma("sp", self.la_nb, self.cst(i["gla_b"][j].rearrange("d p c -> p d c")))
            p.ts("dve", self.la_nb, self.la_nb, -1.0, None, ALU.mult)
            p.dma("sp", self.la_ng, self.cst(i["gla_ng"][j]))
            yrow0 = 0
        else:
            p.dma("sp", self.la_ng, self.cst(i["ret_ng"][j]))
            p.dma("sp", self.la_Lr, self.cst(i["ret_dec"][j].rearrange("d p c -> p d c")))
            p.act(self.la_Lr, self.la_Lr, AF.Exp, scale=-1.0)
            p.act(self.la_Lr, self.la_Lr, AF.Ln, bias=1.0)
            p.ts("dve", self.la_Lr, self.la_Lr, 16.0, None, ALU.mult)
            yrow0 = 512
        cnt = [0]
        sbi = [0, 0]
        rotA = [self.banks[k] for k in (4, 5, 6)]
        rai = [0]

        def rbA():
            bnk = rotA[rai[0] % 3]
            rai[0] += 1
            return bnk

        def do_tile(d, c0, n, state_only):
            mask = self.masks[:, d, :]
            nblk = n // 128
            nch = n // 64
            v = la_v[cnt[0] % 2]
            p.dma("sp", v[:, :nblk, :], self.tm("VT", c0, n, 0, 512))
            for hp in range(2):
                x = hp
                qt, kt, L, bc, e = la_q[x], la_k[x], la_L[x], la_b[x], la_e[x]
                qi, ki, ks, kst, dec = la_qi[x], la_ki[x], la_ks[x], la_kst[x], self.la_dec[x]
                if not state_only:
                    p.dma("sp", qt[:, :n], self.fm("QT", hp * 128, 128, c0, n))
                p.dma("sp", kt[:, :n], self.fm("KT", hp * 128, 128, c0, n))
                if kind == "gla":
                    lr = la_lr[x]
                    p.dma("sp", lr[:, :n], V(self.s["LR"][d, :, c0:c0 + n], p.dres("LR", d, c0, n)))
                    bk = self.bank()
                    p.mm(bk[:, :n], self.la_up[:, d, hp * 128:(hp + 1) * 128], lr[:, :n])
                    p.act(L[:, :n], bk[:, :n], AF.Exp, scale=-1.0, bias=self.la_nb[:, d, hp:hp + 1])
                    p.act(L[:, :n], L[:, :n], AF.Ln, bias=1.0)
                else:
                    p.ts("dve", L[:, :n], self.ones_f[:, :n], self.la_Lr[:, d, hp:hp + 1], None, ALU.mult)
                if d == 0:
                    p.scan(bc[:, :n], self.rmask[:, 0, :n], L[:, :n], 0.0)
                else:
                    p.scan(bc[:, :n][:, ::-1], self.rmask[:, 1, :n][:, ::-1], L[:, :n][:, ::-1], 0.0)
                b3 = bc[:, :n].re("p (c i) -> p c i", i=64)
                bend = b3[:, :, 63:64] if d == 0 else b3[:, :, 0:1]
                if not state_only:
                    p.act(e[:, :n], bc[:, :n], AF.Exp, scale=-1.0 / 16)
                    p.tt("dve", qi[:, :n], qt[:, :n], e[:, :n], ALU.mult)
                    p.act(e[:, :n], bc[:, :n], AF.Exp, scale=1.0 / 16)
                    p.tt("pool", ki[:, :n], kt[:, :n], e[:, :n], ALU.mult)
                else:
                    p.reduce(self.la_ared, bend.re("p c o -> p (c o)"), ALU.add)
                    aq = self.la_asum[:, 2 * d + hp:2 * d + hp + 1]
                    p.tt("dve", aq, aq, self.la_ared, ALU.add)
                p.act(dec[:, :nch], bend.re("p c o -> p (c o)"), AF.Exp, scale=-1.0 / 16)
                p.tt("dve", e[:, :n].re("p (c i) -> p c i", i=64), bend.bc([128, nch, 64]), b3, ALU.subtract)
                p.act(e[:, :n], e[:, :n], AF.Exp, scale=-1.0 / 16)
                p.tt("pool", ks[:, :n], kt[:, :n], e[:, :n], ALU.mult)
                for b in range(nblk):
                    p.tr(self.bank_bf[:, (hp * 4 + b) * 128:(hp * 4 + b + 1) * 128], ks[:, b * 128:(b + 1) * 128], self.ident)
                p.copy("act", kst[:, :nblk, :], self.bank_bf[:, hp * 512:hp * 512 + nblk * 128].re("p (b c) -> p b c", c=128))
            blks = list(range(nblk)) if d == 0 else list(reversed(range(nblk)))
            for b in blks:
                cs = slice(b * 128, (b + 1) * 128)
                chunks = [0, 1] if d == 0 else [1, 0]
                psO = [[None, None], [None, None]]
                if not state_only:
                    for hp in range(2):
                        qi, ki = la_qi[hp], la_ki[hp]
                        psAs = []
                        for h in range(2):
                            r = slice(h * 64, (h + 1) * 64)
                            psA = rbA()
                            psAs.append(psA)
                            p.mm(psA[:, :128], ki[r, cs], qi[r, cs])
                        for h in range(2):
                            am = self.la_att[(2 * hp + h) % 4]
                            p.tt("dve", am, psAs[h][:, :128], mask, ALU.mult)
                        for h in range(2):
                            head = 2 * hp + h
                            vc = slice(head * 128, (head + 1) * 128)
                            am = self.la_att[(2 * hp + h) % 4]
                            po = self.banks[2 * hp + h]
                            psO[hp][h] = po
                            p.mm(po[:, :128], v[:, b, vc], am, start=True, stop=False)
                for ci, cc in enumerate(chunks):
                    tk = slice(b * 128 + cc * 64, b * 128 + cc * 64 + 64)
                    pk = slice(cc * 64, cc * 64 + 64)
                    for hp in range(2):
                        qi, kst, dec = la_qi[hp], la_kst[hp], self.la_dec[hp]
                        S = self.la_S[hp]
                        Sb = self.la_Sb[hp][sbi[hp] % 2]
                        psD = rbA()
                        for h in range(2):
                            r = slice(h * 64, (h + 1) * 64)
                            head = 2 * hp + h
                            vc = slice(head * 128, (head + 1) * 128)
                            if not state_only:
                                p.mm(psO[hp][h][:, cc * 64:cc * 64 + 64], Sb[r, :], qi[r, tk], start=False, stop=(ci == 1))
                            p.mm(psD[r, :128], kst[pk, b, r], v[pk, b, vc])
                        chn = b * 2 + cc
                        p.stt("dve", S, S, dec[:, chn:chn + 1], psD[:, :128], ALU.mult, ALU.add)
                        if not state_only:
                            sbi[hp] += 1
                            p.copy("act", self.la_Sb[hp][sbi[hp] % 2], S)
                if not state_only:
                    for hp in range(2):
                        for h in range(2):
                            p.copy("act" if h else "dve", la_o[hp][:, h, cs], psO[hp][h][:, :128])
            if not state_only:
                for hp in range(2):
                    x = hp
                    ot = la_o[x]
                    if d == 0:
                        p.dma("pool", self.fm("OF", hp * 256, 256, c0, n), ot[:, :, :n])
                    else:
                        oft, gt, yt = la_of[x], la_g[x], la_y[x]
                        p.dma("sp", oft[:, :, :n], self.fm("OF", hp * 256, 256, c0, n))
                        p.dma("sp", gt[:, :, :n], self.fm("GA", hp * 256, 256, c0, n))
                        for h in range(2):
                            o = ot[:, h, :n]
                            p.tt("dve", o, o, oft[:, h, :n], ALU.add)
                            p.act(la_o2[:, :n], o, AF.Square)
                            bk = rbA()
                            p.mm(bk[:, :n], self.ones_bf, la_o2[:, :n])
                            p.ts("dve", la_rs[:, :n], bk[:, :n], 1.0 / 128, EPS, ALU.mult, ALU.add)
                            p.act(la_rs[:, :n], la_rs[:, :n], AF.Sqrt)
                            p.recip(la_rs[:, :n], la_rs[:, :n])
                            p.stt("dve", o, o, self.la_ng[:, 0:1], la_rs[:, :n], ALU.mult, ALU.mult)
                            p.tt("pool", yt[:, h, :n], o, gt[:, h, :n], ALU.mult)
                        p.dma("pool", self.fm("YT", yrow0 + hp * 256, 256, c0, n), yt[:, :, :n])
            cnt[0] += 1

        def zero_state(bf=True):
            for hp in range(2):
                p.memset("dve", self.la_S[hp], 0.0)
                if bf:
                    p.memset("pool", self.la_Sb[hp][sbi[hp] % 2], 0.0)

        lat = self.tiles[1:]
        ctxt = self.tiles[0]
        if NSEG > 1:
            p.memset("dve", self.la_asum, 0.0)
            for d in range(2):
                zero_state(bf=False)
                for (c0, n, isctx) in (lat if d == 0 else list(reversed(lat))):
                    do_tile(d, c0, n, True)
                for hp in range(2):
                    q = 2 * d + hp
                    p.copy("dve", la_sst[:, q * 129:q * 129 + 128], self.la_S[hp])
                    p.act(la_sst[:, q * 129 + 128:q * 129 + 129], self.la_asum[:, q:q + 1], AF.Exp, scale=-1.0 / 16)
            cin = V(self.s["cc_s_in"].ap().rearrange("(q p) c -> p q c", p=128), p.dres("cc_s_in", 0, 0, 128))
            p.dma("pool", cin, la_sst[:, 0:516].re("p (q c) -> p q c", c=129))
            cout = V(self.s["cc_s_out"].ap(), p.dres("cc_s_out", 0, 0, 128))
            p.allgather(cout, V(self.s["cc_s_in"].ap(), cin.res), self.groups)
            for d in range(2):
                for r in range(4):
                    src = self.s["cc_s_out"][r * 512 + d * 256:r * 512 + d * 256 + 256, :].rearrange("(q p) c -> p q c", p=128)
                    p.dma("sp", la_G[d][:, r, :, :], V(src, cout.res))
        for d in range(2):
            zero_state()
            do_tile(d, ctxt[0], ctxt[1], False)
            if NSEG > 1:
                for hp in range(2):
                    S = self.la_S[hp]
                    for r in (range(4) if d == 0 else reversed(range(4))):
                        A_r = la_G[d][:, r, hp, 128:129]
                        B_r = la_G[d][:, r, hp, 0:128]
                        p.stt("dve", self.la_tmpS, S, A_r, B_r, ALU.mult, ALU.add)
                        p.tt("dve", self.la_tmpS, self.la_tmpS, S, ALU.subtract)
                        p.stt("dve", S, self.la_tmpS, self.segw[:, 2 + d, r:r + 1], S, ALU.mult, ALU.add)
                    sbi[hp] += 1
                    p.copy("act", self.la_Sb[hp][sbi[hp] % 2], S)
            for (c0, n, isctx) in (lat if d == 0 else list(reversed(lat))):
                do_tile(d, c0, n, False)


    def lru(self, l):
        p, i = self.p, self.i
        j = l // 2
        NSEG = self.NSEG
        p.barrier()
        if not hasattr(self, "lw"):
            self.lw = p.sb([128, 4, 4, 128], BF16, "lru_w")
            self.lb = p.sb([128, 4, 4], F32, "lru_b")
            self.lc8 = p.sb([128, 2, 4], F32, "lru_c8")
            self.lc16 = p.sb([128, 2, 4], F32, "lru_c16")
            self.lcw = p.sb([128, 4, 4], F32, "lru_cw")
            self.lcb = p.sb([128, 4], F32, "lru_cb")
            self.lr_carry = p.sb([128, 4], F32, "lru_carry")
            self.lr_acar = p.sb([128, 4], F32, "lru_acar")
            self.lr_hal = p.sb([128, 4, 3], F32, "lru_hal")
            self.lr_halG = p.sb([128, 4, 12], F32, "lru_halG")
            self.lr_halL = p.sb([128, 4, 2], F32, "lru_halL")
            self.lr_halR = p.sb([128, 4, 1], F32, "lru_halR")
            self.lr_sum = p.sb([128, 16], F32, "lru_sum")
            self.lr_sumG = p.sb([128, 4, 16], F32, "lru_sumG")
            self.lr_hin = p.sb([128, 2, 4], F32, "lru_hin")
            self.lr_t4 = p.sb([128, 4], F32, "lru_t4")
        lxh = [self.Fv(0, 0, 2).re("p a b -> p (a b)"), self.Fv(0, 2, 2).re("p a b -> p (a b)")]
        lxc = [self.Fv(1, 0), self.Fv(1, 1)]
        lr_r = [self.Fv(1, 2), self.Fv(1, 3)]
        lr_i = [self.Fv(2, 0), self.Fv(2, 1)]
        lr_a = [self.Fv(2, 2), self.Fv(2, 3)]
        lr_s = [self.Fv(3, 0), self.Fv(3, 1)]
        lxb = [self.Hv(0, 0), self.Hv(0, 1)]
        lr_h = [self.Fv(4), self.Fv(5)]
        lr_hf = [self.Fv(6), self.Fv(7)]
        lr_g = [self.Fv(8), self.Fv(9)]
        lr_y = [self.Hv(1), self.Hv(2)]
        for kd in range(4):
            for n_ in range(4):
                st = self.wstg[self.wrr % 2]
                self.wrr += 1
                p.dma("sp", st[:, :128], self.cst(i["lru_w"][j, kd, n_]))
                p.copy("pool", self.lw[:, kd, n_, :], st[:, :128])
        p.dma("sp", self.lb, self.cst(i["lru_b"][j].rearrange("k p c -> p k c")))
        p.dma("sp", self.lcw, self.cst(i["conv_w"][j]))
        p.dma("sp", self.lcb, self.cst(i["conv_b"][j]))
        p.dma("sp", self.lc8, self.cst(i["lru_lam"][j].rearrange("d p c -> p d c")))
        p.act(self.lc8, self.lc8, AF.Exp, scale=-1.0)
        p.act(self.lc8, self.lc8, AF.Ln, bias=1.0)
        p.ts("dve", self.lc16, self.lc8, -16.0, None, ALU.mult)
        p.ts("dve", self.lc8, self.lc8, -8.0, None, ALU.mult)
        T = self.TALL
        if NSEG > 1:
            hal = self.lr_hal
            p.dma("sp", hal[:, :, 0:1], self.fm("XR", 0, 512, CTX, 1), slow=True)
            p.dma("sp", hal[:, :, 1:3], self.fm("XR", 0, 512, T - 2, 2), slow=True)
            cin = V(self.s["cc_h_in"].ap(), p.dres("cc_h_in", 0, 0, 128))
            p.dma("pool", cin, hal.re("p c k -> p (c k)"))
            cout = V(self.s["cc_h_out"].ap(), p.dres("cc_h_out", 0, 0, 128))
            p.allgather(cout, cin, self.groups)
            p.dma("sp", self.lr_halG, V(self.s["cc_h_out"].ap().rearrange("(r p) k -> p r k", p=128), cout.res))
            G4 = self.lr_halG.re("p r (c k) -> p r c k", k=3)
            for r in range(4):
                wl = self.segw[:, 0, r:r + 1]
                wr = self.segw[:, 1, r:r + 1]
                if r == 0:
                    p.ts("dve", self.lr_halL, G4[:, r, :, 1:3], wl, None, ALU.mult)
                    p.ts("dve", self.lr_halR, G4[:, r, :, 0:1], wr, None, ALU.mult)
                else:
                    p.stt("dve", self.lr_halL, G4[:, r, :, 1:3], wl, self.lr_halL, ALU.mult, ALU.add)
                    p.stt("dve", self.lr_halR, G4[:, r, :, 0:1], wr, self.lr_halR, ALU.mult, ALU.add)
        cnt = 0
        lat = self.tiles[1:]
        for d in range(2):
            order = [self.tiles[0]] + (list(lat) if d == 0 else list(reversed(lat)))
            first = True
            for ti_, (c0, n, isctx) in enumerate(order):
                seg0, seg1 = (0, CTX) if isctx else (CTX, T)
                h = lr_h[cnt % 2]
                ac = lr_hf[cnt % 2] if (NSEG > 1 and not isctx) else None
                seg_first = (NSEG > 1 and ti_ == 1)
                for ch in range(4):
                    x = ch % 2
                    xh, xc, xb = lxh[x], lxc[x], lxb[x]
                    lo = max(seg0, c0 - 2)
                    hi = min(seg1, c0 + n + 1)
                    if lo > c0 - 2 or hi < c0 + n + 1:
                        p.memset("pool", xh[:, 0:516], 0.0)
                    p.dma("sp", xh[:, lo - (c0 - 2):hi - (c0 - 2)], self.fm("XR", ch * 128, 128, lo, hi - lo))
                    if NSEG > 1 and not isctx:
                        if lo > c0 - 2:
                            p.copy("dve", xh[:, 0:2], self.lr_halL[:, ch, :])
                        if hi < c0 + n + 1:
                            p.copy("dve", xh[:, n + 2:n + 3], self.lr_halR[:, ch, :])
                    p.ts("dve", xc[:, :n], xh[:, 0:n], self.lcw[:, ch, 0:1], self.lcb[:, ch:ch + 1], ALU.mult, ALU.add)
                    for k in range(1, 4):
                        p.stt("dve", xc[:, :n], xh[:, k:k + n], self.lcw[:, ch, k:k + 1], xc[:, :n], ALU.mult, ALU.add)
                    p.copy("act", xb[:, :n], xc[:, :n])
                    r, ig, a, s = lr_r[x], lr_i[x], lr_a[x], lr_s[x]
                    bk = self.bank()
                    p.mm(bk[:, :n], self.lw[:, 2 * d, ch, :], xb[:, :n])
                    p.act(r[:, :n], bk[:, :n], AF.Sigmoid, bias=self.lb[:, 2 * d, ch:ch + 1])
                    bk2 = self.bank()
                    p.mm(bk2[:, :n], self.lw[:, 2 * d + 1, ch, :], xb[:, :n])
                    p.act(ig[:, :n], bk2[:, :n], AF.Sigmoid, bias=self.lb[:, 2 * d + 1, ch:ch + 1])
                    p.act(a[:, :n], r[:, :n], AF.Exp, scale=self.lc8[:, d, ch:ch + 1])
                    p.act(s[:, :n], r[:, :n], AF.Exp, scale=self.lc16[:, d, ch:ch + 1])
                    p.act(s[:, :n], s[:, :n], AF.Sqrt, scale=-1.0, bias=1.0)
                    p.tt("pool", ig[:, :n], ig[:, :n], xc[:, :n], ALU.mult)
                    p.tt("dve", s[:, :n], s[:, :n], ig[:, :n], ALU.mult)
                    init = 0.0 if (first or seg_first) else self.lr_carry[:, ch:ch + 1]
                    ainit = 1.0 if seg_first else self.lr_acar[:, ch:ch + 1]
                    if NSEG > 1 and seg_first:
                        p.copy("dve", self.lr_hin[:, d, ch:ch + 1], self.lr_carry[:, ch:ch + 1])
                    if d == 0:
                        p.scan(h[:, ch, :n], a[:, :n], s[:, :n], init)
                        p.copy("dve", self.lr_carry[:, ch:ch + 1], h[:, ch, n - 1:n])
                        if ac is not None:
                            p.scan(ac[:, ch, :n], a[:, :n], self.ones_f[:, :n], ainit, ALU.mult, ALU.mult)
                            p.copy("dve", self.lr_acar[:, ch:ch + 1], ac[:, ch, n - 1:n])
                    else:
                        p.scan(h[:, ch, :n][:, ::-1], a[:, :n][:, ::-1], s[:, :n][:, ::-1], init)
                        p.copy("dve", self.lr_carry[:, ch:ch + 1], h[:, ch, 0:1])
                        if ac is not None:
                            p.scan(ac[:, ch, :n][:, ::-1], a[:, :n][:, ::-1], self.ones_f[:, :n], ainit, ALU.mult, ALU.mult)
                            p.copy("dve", self.lr_acar[:, ch:ch + 1], ac[:, ch, 0:1])
                first = False
                if NSEG > 1 and not isctx:
                    p.dma("pool", self.fm("HF" if d == 0 else "HB", 0, 512, c0, n), h[:, :, :n])
                    p.dma("pool", self.fm("AF" if d == 0 else "AB", 0, 512, c0, n), ac[:, :, :n])
                elif d == 0:
                    p.dma("pool", self.fm("HF", 0, 512, c0, n), h[:, :, :n])
                else:
                    hf, g, y = lr_hf[cnt % 2], lr_g[cnt % 2], lr_y[cnt % 2]
                    p.dma("sp", hf[:, :, :n], self.fm("HF", 0, 512, c0, n))
                    p.dma("sp", g[:, :, :n], self.fm("GB", 0, 512, c0, n))
                    p.tt("dve", h[:, :, :n], h[:, :, :n], hf[:, :, :n], ALU.add)
                    p.tt("pool", y[:, :, :n], h[:, :, :n], g[:, :, :n], ALU.mult)
                    p.dma("pool", self.fm("YT", 512, 512, c0, n), y[:, :, :n])
                cnt += 1
            if NSEG > 1:
                p.copy("dve", self.lr_sum[:, d * 8:d * 8 + 4], self.lr_acar)
                p.copy("dve", self.lr_sum[:, d * 8 + 4:d * 8 + 8], self.lr_carry)
        if NSEG > 1:
            cin = V(self.s["cc_l_in"].ap(), p.dres("cc_l_in", 0, 0, 128))
            p.dma("pool", cin, self.lr_sum)
            cout = V(self.s["cc_l_out"].ap(), p.dres("cc_l_out", 0, 0, 128))
            p.allgather(cout, cin, self.groups)
            p.dma("sp", self.lr_sumG, V(self.s["cc_l_out"].ap().rearrange("(r p) k -> p r k", p=128), cout.res))
            for d in range(2):
                hin = self.lr_hin[:, d, :]
                for r in (range(4) if d == 0 else reversed(range(4))):
                    A_r = self.lr_sumG[:, r, d * 8:d * 8 + 4]
                    B_r = self.lr_sumG[:, r, d * 8 + 4:d * 8 + 8]
                    p.tt("dve", self.lr_t4, A_r, hin, ALU.mult)
                    p.tt("dve", self.lr_t4, self.lr_t4, B_r, ALU.add)
                    p.tt("dve", self.lr_t4, self.lr_t4, hin, ALU.subtract)
                    p.stt("dve", hin, self.lr_t4, self.segw[:, 2 + d, r:r + 1], hin, ALU.mult, ALU.add)
            bufs = [[self.Fv(k) for k in (0, 1, 2, 3, 4)], [self.Fv(k) for k in (5, 6, 7, 8, 9)]]
            ys = [self.Hv(1), self.Hv(2)]
            p.barrier()
            for ti_, (c0, n, isctx) in enumerate(lat):
                hf, af, hb, ab, g = bufs[ti_ % 2]
                y = ys[ti_ % 2]
                for nm, t in (("HF", hf), ("AF", af), ("HB", hb), ("AB", ab), ("GB", g)):
                    p.dma("sp", t[:, :, :n], self.fm(nm, 0, 512, c0, n))
                for ch in range(4):
                    p.stt("dve", hf[:, ch, :n], af[:, ch, :n], self.lr_hin[:, 0, ch:ch + 1], hf[:, ch, :n], ALU.mult, ALU.add)
                    p.stt("dve", hb[:, ch, :n], ab[:, ch, :n], self.lr_hin[:, 1, ch:ch + 1], hb[:, ch, :n], ALU.mult, ALU.add)
                p.tt("pool", hf[:, :, :n], hf[:, :, :n], hb[:, :, :n], ALU.add)
                p.tt("pool", y[:, :, :n], hf[:, :, :n], g[:, :, :n], ALU.mult)
                p.dma("pool", self.fm("YT", 512, 512, c0, n), y[:, :, :n])


    def phase3(self, l):
        p = self.p
        p.barrier()
        p3y = [[self.Hv(0), self.Hv(1)], [self.Hv(2), self.Hv(3)]]
        p3x = [[self.Fv(0), self.Fv(1)], [self.Fv(2), self.Fv(3)]]
        cnt = 0
        for (c0, n, isctx) in self.tiles:
            if isctx and l == self.DEPTH - 1:
                continue
            w = 1 if isctx else 0
            y, x = p3y[cnt % 2], p3x[cnt % 2]
            for half in range(2):
                p.dma("sp", y[half][:, :, :n], self.fm("YT", half * 512, 512, c0, n))
                p.dma("sp", x[half][:, :, :n], self.fm("X", half * 512, 512, c0, n))
            for f in range(8):
                bk = self.bank()
                for k in range(8):
                    p.mm(bk[:, :n], self.Wo[:, k, f * 128:(f + 1) * 128], y[k // 4][:, k % 4, :n], start=(k == 0), stop=(k == 7))
                xf = x[f // 4][:, f % 4, :n]
                p.stt("dve", xf, bk[:, :n], self.GT[l][:, f, w:w + 1], xf, ALU.mult, ALU.add)
            for half in range(2):
                p.dma("pool", self.fm("X", half * 512, 512, c0, n), x[half][:, :, :n])
            cnt += 1

    def final_norm(self):
        p = self.p
        p.barrier()
        self.norm_alloc()
        for (c0, n, isctx) in self.tiles:
            if isctx:
                continue
            self.norm_tile(None, c0, n, False)
            o0 = c0 - CTX
            for half in range(2):
                ov = V(self.out[half * 512:(half + 1) * 512, o0:o0 + n].rearrange("(c p) t -> p c t", p=128), p.dres("out", half, o0, n))
                p.dma("pool", ov, self.xt[half][:, :, :n])

    def phase1_odd(self, l):
        p, i = self.p, self.i
        C = ODD_COLS
        p.barrier()
        self.norm_alloc()
        stg = [self.Fv(k) for k in (3, 4, 5, 6)]
        Ct = [self.Fv(7, 0), self.Fv(7, 1)]
        St = [self.Fv(7, 2), self.Fv(7, 3)]
        t1 = [self.Fv(8, 0), self.Fv(8, 1)]
        t2 = [self.Fv(8, 2), self.Fv(8, 3)]
        aq = self.Hv(4)
        ak = self.Hv(5, 0)
        k2 = self.Hv(5, 1)
        av = self.Hv(5, 2).re("p (b c) -> p b c", c=128)
        if not hasattr(self, "kmax"):
            self.kmax = p.sb([128, 2], F32, "kmax")
            self.kred = p.sb([128, 1], F32, "kred")
            self.kmx = p.sb([128, 2], F32, "kmx")
        p.memset("dve", self.kmax, 0.0)
        rr = [0]
        srr = 0
        ti = 0
        for (c0, n, isctx) in self.tiles:
            hT = self.norm_tile(l, c0, n, isctx)
            ct, st_ = Ct[ti % 2], St[ti % 2]
            if not isctx:
                p.dma("sp", ct[:, :n], self.cst(i["ropec"][:, c0 - CTX:c0 - CTX + n]))
                p.dma("sp", st_[:, :n], self.cst(i["ropes"][:, c0 - CTX:c0 - CTX + n]))

            def rope_evac(out, col, colr, scale):
                bk = self.proj_fm(hT, n, col, 128)
                if isctx:
                    p.act(out, bk[:, :n], AF.Copy, scale=scale)
                else:
                    bk2 = self.proj_fm(hT, n, colr, 128)
                    a, b = t1[rr[0] % 2], t2[rr[0] % 2]
                    rr[0] += 1
                    p.stt("dve", a[:, :n], bk[:, :n], scale, ct[:, :n], ALU.mult, ALU.mult)
                    p.stt("dve", b[:, :n], bk2[:, :n], scale, st_[:, :n], ALU.mult, ALU.mult)
                    p.tt("pool", out, a[:, :n], b[:, :n], ALU.add)

            for c in range(4):
                rope_evac(aq[:, c, :n], C["q"][0] + c * 128, C["qr"][0] + c * 128, 0.125)
            p.dma("pool", self.fm("AQ", 0, 512, c0, n), aq[:, :, :n])
            rope_evac(ak[:, :n], C["k"][0], C["kr"][0], 1.0)
            for g in range(2):
                for dup in range(2):
                    p.dma("pool", V(self.s["AK"][g, dup * 64:(dup + 1) * 64, c0:c0 + n], p.dres("AK", g, c0, n)),
                          ak[g * 64:(g + 1) * 64, :n])
            p.tt("pool", k2[:, :n], ak[:, :n], ak[:, :n], ALU.mult)
            for g in range(2):
                bk = self.bank()
                p.mm(bk[:, :n], self.ones_bf[g * 64:(g + 1) * 64, :], k2[g * 64:(g + 1) * 64, :n])
                p.reduce(self.kred, bk[:, :n], ALU.max)
                p.tt("dve", self.kmax[:, g:g + 1], self.kmax[:, g:g + 1], self.kred, ALU.max)
            nb = n // 128
            for b in range(nb):
                bk = self.proj_tm(hT, b, C["v"][0], 128)
                p.copy("act" if b % 2 else "dve", av[:, b, :], bk[:, :128])
            p.dma("pool", self.tm("AV", c0, n, 0, 128), av[:, :nb, :])
            for (nm, dst) in (("gs", "GB"), ("gr", "GA")):
                st = stg[srr % 4]
                srr += 1
                for c in range(4):
                    bk = self.proj_fm(hT, n, C[nm][0] + c * 128, 128)
                    p.act(st[:, c, :n], bk[:, :n], AF.Silu)
                p.dma("pool", self.fm(dst, 0, 512, c0, n), st[:, :, :n])
            for (nm, nmr, dst, scale) in (("rq", "rqr", "QT", 1.0), ("rk", "rkr", "KT", 0.125)):
                st = stg[srr % 4]
                srr += 1
                for c in range(2):
                    rope_evac(st[:, c, :n], C[nm][0] + c * 128, C[nmr][0] + c * 128, scale)
                p.dma("pool", self.fm(dst, 0, 256, c0, n), st[:, 0:2, :n])
            vs = self.sq[0]
            for b in range(nb):
                bk = self.proj_tm(hT, b, C["rv"][0], 512)
                p.copy("act" if b % 2 else "dve", vs[:, b, :], bk[:, :])
            p.dma("pool", self.tm("VT", c0, n, 0, 512), vs[:, :nb, :])
            ti += 1
        p.act(self.kmx, self.kmax, AF.Sqrt)
        p.ts("dve", self.kmx, self.kmx, 1.01, None, ALU.mult)

    def swa(self, l):
        p, i = self.p, self.i
        j = l // 2
        T = self.TALL
        p.barrier()
        if not hasattr(self, "sink"):
            self.sink = p.sb([128, 8], F32, "sink")
        p.dma("sp", self.sink, self.cst(i["sink"][j]))
        Kc = [self.Hv(0, 0), self.Hv(0, 1)]
        Vc = self.Hv(0, 2).re("p (b c) -> p b c", c=128)
        q2 = self.Hv(0, 3)
        for g in range(2):
            p.dma("sp", Kc[g][:, :256], V(self.s["AK"][g, :, 0:256], p.dres("AK", g, 0, 256)))
        p.dma("sp", Vc[:, :2, :], self.tm("AV", 0, 256, 0, 128))
        NSEG = self.NSEG
        if NSEG > 1:
            if not hasattr(self, "sw_vlr"):
                self.sw_vlr = p.sb([128, 2], F32, "sw_vlr")
            fb4 = self.Fp[4].bitcast(BF16)
            self.sw_KL = [V(fb4[:, 0, g * 128:(g + 1) * 128], [Res()]) for g in range(2)]
            self.sw_KR = [V(fb4[:, 0, (2 + g) * 128:(3 + g) * 128], [Res()]) for g in range(2)]
            self.sw_VL = V(fb4[:, 0, 512:640], [Res()])
            self.sw_VR = V(fb4[:, 0, 640:768], [Res()])
            self.sw_mL = V(self.Fp[3][:, 0, 0:128], [Res()])
            self.sw_mR = V(self.Fp[3][:, 1, 0:128], [Res()])
            cres = p.dres("cc_kv_in", 0, 0, 128)
            for g in range(2):
                p.dma("sp", V(self.s["cc_kv_in"][g * 128:(g + 1) * 128, 0:128], cres),
                      V(self.s["AK"][g, :, CTX:CTX + 128], p.dres("AK", g, CTX, 128)))
                p.dma("sp", V(self.s["cc_kv_in"][g * 128:(g + 1) * 128, 128:256], cres),
                      V(self.s["AK"][g, :, T - 128:T], p.dres("AK", g, T - 128, 128)))
            p.dma("sp", V(self.s["cc_kv_in"][0:128, 256:384], cres), V(self.s["AV"][CTX:CTX + 128, :], p.dres("AV", "all", CTX, 128)))
            p.dma("sp", V(self.s["cc_kv_in"][128:256, 256:384], cres), V(self.s["AV"][T - 128:T, :], p.dres("AV", "all", T - 128, 128)))
            cout = V(self.s["cc_kv_out"].ap(), p.dres("cc_kv_out", 0, 0, 128))
            p.allgather(cout, V(self.s["cc_kv_in"].ap(), cres), self.groups)
            GK = V(self.Hp[3][:, 0:4, :].rearrange("p a b -> p (a b)").rearrange("p (r g c) -> p r g c", r=4, g=2), [Res()])
            GV = V(self.Hp[2][:, 0:2, :].rearrange("p a b -> p (a b)").rearrange("p (r f c) -> p r f c", r=4, f=2), [Res()])
            for r in range(4):
                p.dma("sp", GK[:, r, :, :], V(self.s["cc_kv_out"][r * 256:(r + 1) * 256, 0:256].rearrange("(g p) c -> p g c", p=128), cout.res))
                p.dma("sp", GV[:, r, :, :], V(self.s["cc_kv_out"][r * 256:(r + 1) * 256, 256:384].rearrange("(f p) c -> p f c", p=128), cout.res))
            for r in range(4):
                wl = self.segw[:, 0, r:r + 1]
                wr = self.segw[:, 1, r:r + 1]
                sel = [(self.sw_VL, GV[:, r, 1, :], wl), (self.sw_VR, GV[:, r, 0, :], wr)]
                for g in range(2):
                    sel.append((self.sw_KL[g], GK[:, r, g, 128:256], wl))
                    sel.append((self.sw_KR[g], GK[:, r, g, 0:128], wr))
                for (dst, src, w) in sel:
                    if r == 0:
                        p.ts("dve", dst, src, w, None, ALU.mult)
                    else:
                        p.stt("dve", dst, src, w, dst, ALU.mult, ALU.add)
            p.reduce(self.sw_vlr[:, 0:1], self.segw[:, 0, :], ALU.add)
            p.reduce(self.sw_vlr[:, 1:2], self.segw[:, 1, :], ALU.add)
            p.ts("dve", self.sw_mL, self.masks[:, 2, :], self.sw_vlr[:, 0:1], None, ALU.mult)
            p.ts("dve", self.sw_mR, self.masks[:, 3, :], self.sw_vlr[:, 1:2], None, ALU.mult)
            p.barrier()
        Kw = [[V(self.Hp[1 + x][:, 0:2, :].rearrange("p a b -> p (a b)"), [Res()]),
               V(self.Hp[1 + x][:, 2:4, :].rearrange("p a b -> p (a b)"), [Res()])] for x in range(2)]
        Vw = [V(self.Hp[3][:, 2 * x:2 * x + 2, :].rearrange("p a b -> p (a b)").rearrange("p (b c) -> p b c", c=128), [Res()])
              for x in range(2)]
        Q = [self.Hv(4), self.Hv(5)]
        Pt = []
        for k in (8, 9):
            fb = self.Fp[k].bitcast(BF16)
            for pl in range(4):
                for h in range(2):
                    Pt.append(V(fb[:, pl, h * 512:(h + 1) * 512], [Res()]))
        fb7 = self.Fp[7].bitcast(BF16)
        ybf = [V(fb7[0:64, 0, 0:512], [Res()]), V(fb7[0:64, 0, 512:1024], [Res()])]
        MB = [self.Fv(0, 0), self.Fv(0, 1)]
        tmp = [self.Fv(0, 2), self.Fv(0, 3), self.Fv(1, 0)]
        esk = self.Fv(1, 1, rows=64)
        den = self.Fv(1, 2, rows=64)
        o = self.Fv(1, 3, rows=64)
        gate = [self.Fv(2, 0, rows=64), self.Fv(2, 1, rows=64)]
        ti = 0
        it = 0
        pi = 0
        import os
        STG = int(os.environ.get("KSWA_STAGE", "9"))
        rot = [self.banks[k] for k in (2, 3, 4, 5, 6)]
        rri = [0]

        def rb():
            bnk = rot[rri[0] % 5]
            rri[0] += 1
            return bnk
        psO = self.banks[0]
        psD = self.banks[1]
        for (c0, n, isctx) in self.tiles:
            x = ti % 2
            Qx = Q[x]
            p.dma("sp", Qx[:, :, :n], self.fm("AQ", 0, 512, c0, n))
            if not isctx:
                lo = max(CTX, c0 - 128)
                hi = min(T, c0 + n + 128)
                off = lo - (c0 - 128)
                for g in range(2):
                    p.dma("sp", Kw[x][g][:, off:off + hi - lo], V(self.s["AK"][g, :, lo:hi], p.dres("AK", g, lo, hi - lo)))
                p.dma("sp", Vw[x][:, off // 128:off // 128 + (hi - lo) // 128, :], self.tm("AV", lo, hi - lo, 0, 128))
                if NSEG > 1 and c0 == CTX:
                    for g in range(2):
                        p.copy("pool", Kw[x][g][:, 0:128], self.sw_KL[g])
                    p.copy("pool", Vw[x][:, 0, :], self.sw_VL)
                if NSEG > 1 and c0 + n == T:
                    for g in range(2):
                        p.copy("pool", Kw[x][g][:, n + 128:n + 256], self.sw_KR[g])
                    p.copy("pool", Vw[x][:, (n + 128) // 128, :], self.sw_VR)
            for qb in range(n // 128):
                qs = slice(qb * 128, (qb + 1) * 128)
                tpos = c0 + qb * 128
                if STG < 0:
                    continue
                for c in range(4):
                    p.tt("pool", q2[:, c * 128:(c + 1) * 128], Qx[:, c, qs], Qx[:, c, qs], ALU.mult)
                for g in range(2):
                    hl = [(par, hs, 4 * g + 2 * hs + par) for par in range(2) for hs in range(2)]
                    bkq = [rb(), rb()]
                    for (par, hs, hq) in hl:
                        r = slice(par * 64, par * 64 + 64)
                        ch = hq // 2
                        p.mm(bkq[par][:, hs * 128:(hs + 1) * 128], self.ones_bf[r, :], q2[r, ch * 128:(ch + 1) * 128])
                    mb = MB[it % 2]
                    if STG < 1:
                        continue
                    for par in range(2):
                        p.act(mb[:, par * 256:(par + 1) * 256], bkq[par][:, :256], AF.Sqrt)
                    p.ts("dve", mb, mb, self.kmx[:, g:g + 1], None, ALU.mult)
                    gt = gate[it % 2]
                    for par in range(2):
                        p.dma("sp", gt[:, par * 256:(par + 1) * 256].re("p (h t) -> p h t", h=2),
                              V(self.s["GB"][g * 256:(g + 1) * 256, tpos:tpos + 128].rearrange("(hs par d) t -> par d hs t", par=2, d=64)[par],
                                p.dres("GB", 2 * g, tpos, 128) + p.dres("GB", 2 * g + 1, tpos, 128)))
                    kb = []
                    for b in range(2):
                        kb.append((Kc[g][:, b * 128:(b + 1) * 128], Vc[:, b, g * 64:(g + 1) * 64], None))
                    if not isctx:
                        for rel in (0, -1, 1):
                            kpos = tpos + rel * 128
                            halo = kpos < CTX or kpos >= T
                            if halo and NSEG == 1:
                                continue
                            wc = kpos - (c0 - 128)
                            m = None if rel == 0 else (self.masks[:, 2, :] if rel == -1 else self.masks[:, 3, :])
                            if halo:
                                m = self.sw_mL if rel == -1 else self.sw_mR
                            kb.append((Kw[x][g][:, wc:wc + 128], Vw[x][:, wc // 128, g * 64:(g + 1) * 64], m))
                    if STG < 2:
                        continue
                    for bi, (Kb, Vb, m) in enumerate(kb):
                        psS = [rb(), rb()]
                        for (par, hs, hq) in hl:
                            r = slice(par * 64, par * 64 + 64)
                            p.mm(psS[par][:, hs * 128:(hs + 1) * 128], Kb[r, :], Qx[r, hq // 2, qs])
                        t = tmp[pi % 3]
                        P = Pt[pi % 16]
                        pi += 1
                        for par in range(2):
                            cs_ = slice(par * 256, (par + 1) * 256)
                            p.tt("dve", t[:, cs_], psS[par][:, :256], mb[:, cs_], ALU.subtract)
                        p.act(P, t, AF.Exp)
                        if m is not None:
                            P3 = P.re("p (h t) -> p h t", h=4)
                            p.tt("dve", P3, P3, V(m.ap.unsqueeze(1).to_broadcast([128, 4, 128]), m.res), ALU.mult)
                        last = bi == len(kb) - 1
                        if STG < 3:
                            continue
                        p.mm(psO[0:64, :], Vb, P, start=(bi == 0), stop=last)
                        p.mm(psD[0:64, :], self.ones_bf[:, 0:64], P, start=(bi == 0), stop=last)
                    if STG < 4:
                        continue
                    for cb, (par, hs, hq) in enumerate(hl):
                        p.act(esk[:, cb * 128:(cb + 1) * 128], mb[0:64, cb * 128:(cb + 1) * 128], AF.Exp, scale=-1.0,
                              bias=self.sink[0:64, hq:hq + 1])
                    p.tt("dve", den, psD[0:64, :], esk, ALU.add)
                    p.recip(den, den)
                    p.tt("dve", o, psO[0:64, :], den, ALU.mult)
                    if STG < 5:
                        continue
                    yb = ybf[it % 2]
                    p.tt("pool", yb, o, gt, ALU.mult)
                    for par in range(2):
                        p.dma("pool", V(self.s["YT"][g * 256:(g + 1) * 256, tpos:tpos + 128].rearrange("(hs par d) t -> par d hs t", par=2, d=64)[par],
                                        p.dres("YT", 2 * g, tpos, 128) + p.dres("YT", 2 * g + 1, tpos, 128)),
                              yb[:, par * 256:(par + 1) * 256].re("p (h t) -> p h t", h=2))
                    it += 1
            ti += 1


def _pc(v, c):
    return np.ascontiguousarray(np.asarray(v, np.float32).reshape(c, 128).T)


def make_consts(SEQ):
    s = np.arange(128)[:, None]
    t = np.arange(128)[None, :]
    same = (s // 64) == (t // 64)
    masks = np.stack([(s <= t) & same, (s >= t) & same, s >= t, s <= t]).astype(np.float32)
    col = np.arange(512)
    rmask = np.stack([np.broadcast_to((col % 64 != 0), (128, 512)),
                      np.broadcast_to((col % 64 != 63), (128, 512))]).astype(np.float32)
    tok = np.arange(SEQ)
    row, cl = tok // 64, tok % 64
    inv = (10000.0 ** (-np.arange(16, dtype=np.float32) / 16)).astype(np.float32)
    ang = np.concatenate([row[:, None].astype(np.float32) * inv[None], cl[:, None].astype(np.float32) * inv[None]], axis=-1)
    cos, sin = np.cos(ang).astype(np.float32), np.sin(ang).astype(np.float32)
    C64 = np.concatenate([cos, cos], axis=1).T
    S64 = np.concatenate([-sin, sin], axis=1).T
    ropec = np.ascontiguousarray(np.concatenate([C64, C64], axis=0))
    ropes = np.ascontiguousarray(np.concatenate([S64, S64], axis=0))
    ident = np.eye(128, dtype=np.float32).astype(ml_dtypes.bfloat16)
    return dict(masks=masks, rmask=rmask, ropec=ropec, ropes=ropes, ident=ident)


def rot_cols(w, nheads, dh=64):
    K = w.shape[0]
    w4 = w.reshape(K, nheads, 2, dh // 2)
    return w4[:, :, ::-1, :].reshape(K, nheads * dh)


def prep_inputs(inp, b, SEQ):
    f = lambda a: np.ascontiguousarray(np.asarray(a, np.float32))
    m = {}
    m["xin"] = np.ascontiguousarray(np.concatenate([f(inp["ctx"][b]).T, f(inp["x"][b])[:SEQ].T], axis=1))
    m["cvec"] = np.ascontiguousarray(np.stack([_pc(inp["c"][b], 8), _pc(inp["c_ctx"], 8)], axis=-1))
    m["ada_w"] = f(inp["ada_w"])
    m["ada_b"] = np.stack([_pc(inp["ada_b"][l], 24) for l in range(4)])
    m["norm_g"] = np.stack([_pc(inp["norm_g"][l], 8) for l in range(4)])
    m["final_g"] = _pc(inp["final_g"], 8)
    m["e_w_in"] = f(inp["e_w_in"])
    m["e_w_out"] = f(inp["e_w_out"])
    ow = f(inp["o_w_in"])
    C = ODD_COLS
    rot = [np.concatenate([rot_cols(ow[j][:, C["q"][0]:C["q"][1]], 8), rot_cols(ow[j][:, C["k"][0]:C["k"][1]], 2),
                           rot_cols(ow[j][:, C["rq"][0]:C["rq"][1]], 4), rot_cols(ow[j][:, C["rk"][0]:C["rk"][1]], 4)], axis=1)
           for j in range(2)]
    m["o_w_in"] = np.ascontiguousarray(np.concatenate([ow, np.stack(rot)], axis=2))
    m["o_w_out"] = f(inp["o_w_out"])
    m["gla_up"] = np.ascontiguousarray(np.stack([f(inp["gla_up_fw"]), f(inp["gla_up_bw"])], axis=1))
    m["gla_b"] = np.stack([np.stack([_pc(inp["gla_b_fw"][j], 2), _pc(inp["gla_b_bw"][j], 2)]) for j in range(2)])
    m["gla_ng"] = np.stack([_pc(inp["gla_norm_g"][j], 1) for j in range(2)])
    m["ret_ng"] = np.stack([_pc(inp["ret_norm_g"][j], 1) for j in range(2)])
    cw = f(inp["lru_conv_w"])
    m["conv_w"] = np.ascontiguousarray(cw.reshape(2, 4, 4, 128).transpose(0, 3, 2, 1))
    m["conv_b"] = np.stack([_pc(inp["lru_conv_b"][j], 4) for j in range(2)])
    m["lru_w"] = np.ascontiguousarray(np.stack([f(inp["lru_wa_fw"]), f(inp["lru_wx_fw"]), f(inp["lru_wa_bw"]), f(inp["lru_wx_bw"])], axis=1))
    m["lru_b"] = np.stack([np.stack([_pc(inp[k][j], 4) for k in ("lru_ba_fw", "lru_bx_fw", "lru_ba_bw", "lru_bx_bw")]) for j in range(2)])
    m["lru_lam"] = np.stack([np.stack([_pc(inp["lru_lam_fw"][j], 4), _pc(inp["lru_lam_bw"][j], 4)]) for j in range(2)])
    m["sink"] = np.ascontiguousarray(np.broadcast_to(f(inp["swa_sink"])[:, None, :], (2, 128, 8)))
    rd = np.stack([f(inp["ret_dec_fw"]), f(inp["ret_dec_bw"])], axis=1)
    m["ret_dec"] = np.ascontiguousarray(np.repeat(rd, 64, axis=2).reshape(2, 2, 2, 128).transpose(0, 1, 3, 2))
    m.update(make_consts(SEQ))
    return m


_CACHE = {}


def seg_weights(s, nseg):
    w = np.zeros((4, 4), np.float32)
    if nseg > 1:
        if s > 0:
            w[0, s - 1] = 1.0
        if s < nseg - 1:
            w[1, s + 1] = 1.0
        w[2, :s] = 1.0
        w[3, s + 1:] = 1.0
    return np.ascontiguousarray(np.broadcast_to(w.reshape(1, 16), (128, 16)))


def make_in_maps(inp, SEQ_TOTAL, NSEG):
    maps = []
    L = SEQ_TOTAL // NSEG
    base = [prep_inputs(inp, b, SEQ_TOTAL) for b in range(2)]
    for b in range(2):
        for s in range(NSEG):
            m = dict(base[b])
            if NSEG > 1:
                xin = base[b]["xin"]
                m["xin"] = np.ascontiguousarray(np.concatenate([xin[:, :CTX], xin[:, CTX + s * L:CTX + (s + 1) * L]], axis=1))
                m["ropec"] = np.ascontiguousarray(base[b]["ropec"][:, s * L:(s + 1) * L])
                m["ropes"] = np.ascontiguousarray(base[b]["ropes"][:, s * L:(s + 1) * L])
            m["segw"] = seg_weights(s, NSEG)
            maps.append(m)
    return maps


def run(inp, SEQ_TOTAL, DEPTH, dbg=(), NSEG=4):
    key = (SEQ_TOTAL, DEPTH, tuple(dbg), NSEG)
    if key not in _CACHE:
        _CACHE[key] = Builder(SEQ_TOTAL // NSEG, DEPTH, dbg, NSEG).build()
    nc = _CACHE[key]
    in_maps = make_in_maps(inp, SEQ_TOTAL, NSEG)
    res = run_bass_kernel_spmd(nc, in_maps, core_ids=list(range(2 * NSEG)), trace=(_os.environ.get("KTRACE", "0") == "1"))
    return res


def gather_out(res, NSEG):
    outs = []
    for b in range(2):
        outs.append(np.concatenate([res.results[b * NSEG + s]["out"].T for s in range(NSEG)], axis=0))
    return np.stack(outs).astype(np.float32)


def kernel(**inputs):
    SEQ = inputs["x"].shape[1]
    res = run(inputs, SEQ, 4, NSEG=4)
    return gather_out(res, 4)
```

```python
import numpy as np
import ml_dtypes
from contextlib import ExitStack
import concourse.bass as bass
import concourse.mybir as mybir
from concourse.bass_utils import run_bass_kernel_spmd

F32 = mybir.dt.float32
BF16 = mybir.dt.bfloat16
AF = mybir.ActivationFunctionType
ALU = mybir.AluOpType
AX = mybir.AxisListType

D = 1024
CTX = 256
EPS = 1e-6
import os as _os
SAME_ENG_SYNC = _os.environ.get("KSYNC", "1") == "1"


class Res:
    __slots__ = ("w", "r")

    def __init__(self):
        self.w = None
        self.r = {}


class V:
    __slots__ = ("ap", "res")

    def __init__(self, ap, res):
        self.ap = ap
        self.res = res

    def __getitem__(self, key):
        return V(self.ap[key], self.res)

    def re(self, s, **kw):
        return V(self.ap.rearrange(s, **kw), self.res)

    def bc(self, shape):
        return V(self.ap.to_broadcast(list(shape)), self.res)


class Prog:
    def __init__(self, nc, es):
        self.nc = nc
        self.es = es
        self.ops = {e: [] for e in ("pe", "act", "dve", "pool", "sp")}
        self.sem = {}
        self.cnt = {}
        for e in ("pe", "act", "dve", "pool"):
            self.sem[e] = es.enter_context(nc.semaphore("s_" + e))
            self.cnt[e] = 0
        self.dsem = {}
        self.dcnt = {}
        self.drr = {}
        for q, n in (("sp", 16), ("pool", 8)):
            self.dsem[q] = [es.enter_context(nc.semaphore("d_%s%d" % (q, i))) for i in range(n)]
            self.dcnt[q] = [0] * n
            self.drr[q] = 0
        self.waited = {e: {} for e in self.ops}
        self.dram_res = {}
        self.n_sb = 0

    def sb(self, shape, dt=F32, name=None):
        self.n_sb += 1
        t = self.es.enter_context(self.nc.sbuf_tensor("sb_" + (name or ("t%d" % self.n_sb)), list(shape), dt))
        return V(t[:] if len(shape) == 2 else t.ap(), [Res()])

    def ps(self, shape, dt=F32, name=None):
        self.n_sb += 1
        t = self.es.enter_context(self.nc.psum_tensor("ps_" + (name or ("t%d" % self.n_sb)), list(shape), dt))
        return V(t.ap(), [Res()])

    def dres(self, name, rowkey, c0, n):
        out = []
        for b in range(c0 // 128, (c0 + n - 1) // 128 + 1):
            k = (name, rowkey, b)
            if k not in self.dram_res:
                self.dram_res[k] = Res()
            out.append(self.dram_res[k])
        return out

    def emit(self, eng, fn, reads, writes, dma=False):
        toks = []
        for v in reads:
            for r in v.res:
                if r.w is not None:
                    toks.append(r.w)
        for v in writes:
            for r in v.res:
                if r.w is not None:
                    toks.append(r.w)
                toks.extend(r.r.values())
        waits = []
        wd = self.waited[eng]
        for (s, val, te) in toks:
            if te == eng and not dma and (eng == "pe" or not SAME_ENG_SYNC):
                continue
            if wd.get(id(s), 0) >= val:
                continue
            wd[id(s)] = val
            waits.append((s, val))
        if dma:
            pool = self.dsem[eng]
            j = self.drr[eng]
            self.drr[eng] = (j + 1) % len(pool)
            s = pool[j]
            k = self.dcnt[eng][j]
            if k > 0 and wd.get(id(s), 0) < 16 * k:
                wd[id(s)] = 16 * k
                waits.append((s, 16 * k))
            self.dcnt[eng][j] = k + 1
            tok = (s, 16 * (k + 1), eng + "q")
            inc = 16
        else:
            self.cnt[eng] += 1
            tok = (self.sem[eng], self.cnt[eng], eng)
            inc = 1
        self.ops[eng].append((waits, fn, tok[0], inc))
        for v in reads:
            for r in v.res:
                r.r[id(tok[0])] = tok
        for v in writes:
            for r in v.res:
                r.w = tok
                r.r = {}
        return tok

    @staticmethod
    def _a(x):
        return x.ap if isinstance(x, V) else x

    @staticmethod
    def _vs(*xs):
        return [x for x in xs if isinstance(x, V)]

    def mm(self, out, lhsT, rhs, start=True, stop=True):
        self.emit("pe", lambda e: e.matmul(out.ap, lhsT.ap, rhs.ap, start=start, stop=stop),
                  [lhsT, rhs], [out])

    def tr(self, out, in_, ident):
        self.emit("pe", lambda e: e.transpose(out.ap, in_.ap, ident.ap), [in_, ident], [out])

    def act(self, out, in_, func, bias=None, scale=None, accum=None, eng="act"):
        kw = {}
        if bias is not None:
            kw["bias"] = self._a(bias)
        if scale is not None:
            kw["scale"] = self._a(scale)
        if accum is not None:
            kw["accum_out"] = accum.ap
        self.emit("act", lambda e: e.activation(out.ap, in_.ap, func, **kw),
                  self._vs(in_, bias, scale), self._vs(out, accum))

    def ts(self, eng, out, in0, s1, s2, op0, op1=None):
        a1, a2 = self._a(s1), self._a(s2)
        if op1 is None:
            f = lambda e: e.tensor_scalar(out.ap, in0.ap, a1, None, op0)
        else:
            f = lambda e: e.tensor_scalar(out.ap, in0.ap, a1, a2, op0, op1)
        self.emit(eng, f, self._vs(in0, s1, s2), [out])

    def tt(self, eng, out, in0, in1, op):
        self.emit(eng, lambda e: e.tensor_tensor(out.ap, in0.ap, in1.ap, op), [in0, in1], [out])

    def stt(self, eng, out, in0, scalar, in1, op0, op1):
        sc = self._a(scalar)
        self.emit(eng, lambda e: e.scalar_tensor_tensor(out.ap, in0.ap, sc, in1.ap, op0, op1),
                  self._vs(in0, scalar, in1), [out])

    def copy(self, eng, out, in_):
        if eng == "act":
            self.emit("act", lambda e: e.copy(out.ap, in_.ap), [in_], [out])
        else:
            self.emit(eng, lambda e: e.tensor_copy(out.ap, in_.ap), [in_], [out])

    def scan(self, out, d0, d1, init, op0=ALU.mult, op1=ALU.add):
        ia = self._a(init)
        self.emit("dve", lambda e: e.tensor_tensor_scan(out.ap, d0.ap, d1.ap, ia, op0, op1),
                  self._vs(d0, d1, init), [out])

    def memset(self, eng, out, val):
        self.emit(eng, lambda e: e.memset(out.ap, val), [], [out])

    def recip(self, out, in_):
        self.emit("dve", lambda e: e.reciprocal(out.ap, in_.ap), [in_], [out])

    def reduce(self, out, in_, op, axis=AX.X):
        self.emit("dve", lambda e: e.tensor_reduce(out.ap, in_.ap, axis, op), [in_], [out])

    def dma(self, q, out, in_, slow=False):
        if slow:
            self.emit(q, lambda e: e.dma_start(out=out.ap, in_=in_.ap, allow_slow_non_contiguous=True), [in_], [out], dma=True)
        else:
            self.emit(q, lambda e: e.dma_start(out=out.ap, in_=in_.ap), [in_], [out], dma=True)

    def allgather(self, out, in_, groups):
        self.n_cc = getattr(self, "n_cc", 0) + 1
        sem = self.es.enter_context(self.nc.semaphore("cc%d" % self.n_cc))
        eng = "pool"
        toks = []
        for r in in_.res:
            if r.w is not None:
                toks.append(r.w)
        for r in out.res:
            if r.w is not None:
                toks.append(r.w)
            toks.extend(r.r.values())
        waits = []
        wd = self.waited[eng]
        for (sm, val, te) in toks:
            if wd.get(id(sm), 0) >= val:
                continue
            wd[id(sm)] = val
            waits.append((sm, val))
        oa, ia = out.ap, in_.ap
        fn = lambda e: e.collective_compute("AllGather", ALU.bypass, replica_groups=groups, ins=[ia], outs=[oa])
        tok = (sem, 1, "ccq")
        self.ops[eng].append((waits, fn, sem, None))
        self.cc_sems = getattr(self, "cc_sems", []) + [sem]
        for r in in_.res:
            r.r[id(sem)] = tok
        for r in out.res:
            r.w = tok
            r.r = {}

    def barrier(self):
        allw = []
        for q in self.dsem:
            for sm, k in zip(self.dsem[q], self.dcnt[q]):
                if k > 0:
                    allw.append((sm, 16 * k))
        for e in ("pe", "act", "dve", "pool"):
            if self.cnt[e] > 0:
                allw.append((self.sem[e], self.cnt[e]))
        for sm in getattr(self, "cc_sems", []):
            allw.append((sm, 1))
        for e in self.ops:
            wd = self.waited[e]
            waits = []
            for (sm, val) in allw:
                if sm is self.sem.get(e):
                    continue
                if wd.get(id(sm), 0) >= val:
                    continue
                wd[id(sm)] = val
                waits.append((sm, val))
            if waits:
                self.ops[e].append((waits, None, None, 0))

    def finish(self):
        nc = self.nc
        fin = []
        for q in self.dsem:
            for s, k in zip(self.dsem[q], self.dcnt[q]):
                if k > 0:
                    fin.append((s, 16 * k))
        for e in ("pe", "act", "dve", "pool"):
            if self.cnt[e] > 0:
                fin.append((self.sem[e], self.cnt[e]))
        ops = self.ops

        def run(e, lst, extra=()):
            for waits, fn, s, inc in lst:
                for (ws, wv) in waits:
                    e.wait_ge(ws, wv)
                if fn is not None:
                    if inc is None:
                        fn(e).then_inc(s)
                    else:
                        fn(e).then_inc(s, inc)
            for (ws, wv) in extra:
                e.wait_ge(ws, wv)

        with nc.Block() as block:
            @block.sync
            def _(e):
                run(e, ops["sp"], fin)

            @block.tensor
            def _(e):
                run(e, ops["pe"])

            @block.scalar
            def _(e):
                run(e, ops["act"])

            @block.vector
            def _(e):
                run(e, ops["dve"])

            @block.gpsimd
            def _(e):
                run(e, ops["pool"])


EVEN_COLS = dict(q=(0, 256), k=(256, 512), v=(512, 1024), gg=(1024, 1536), lrf=(1536, 1552),
                 lrb=(1552, 1568), xr=(1568, 2080), gl=(2080, 2592))
ODD_COLS = dict(q=(0, 512), k=(512, 640), v=(640, 768), gs=(768, 1280), rq=(1280, 1536),
                rk=(1536, 1792), rv=(1792, 2304), gr=(2304, 2816),
                qr=(2816, 3328), kr=(3328, 3456), rqr=(3456, 3712), rkr=(3712, 3968))


class Builder:
    def __init__(self, SEQ, DEPTH, dbg=(), NSEG=1):
        self.NSEG = NSEG
        self.SEQ = SEQ
        self.DEPTH = DEPTH
        self.TALL = CTX + SEQ
        self.dbg = dbg
        self.nc = bass.Bass("TRN2", target_bir_lowering=False)
        self.es = ExitStack()
        self.tiles = [(0, CTX, True)] + [(CTX + i * 512, 512, False) for i in range(SEQ // 512)]

    def din(self, name, shape, dt=F32):
        return self.nc.dram_tensor(name, list(shape), dt, kind="ExternalInput")

    def dscr(self, name, shape, dt=F32):
        kind = "ExternalOutput" if name in self.dbg else "Internal"
        return self.nc.dram_tensor(name, list(shape), dt, kind=kind)

    def declare(self):
        T = self.TALL
        S = self.SEQ
        i = {}
        i["xin"] = self.din("xin", [D, T])
        i["cvec"] = self.din("cvec", [128, 8, 2])
        i["ada_w"] = self.din("ada_w", [4, D, 3 * D])
        i["ada_b"] = self.din("ada_b", [4, 128, 24])
        i["norm_g"] = self.din("norm_g", [4, 128, 8])
        i["final_g"] = self.din("final_g", [128, 8])
        i["e_w_in"] = self.din("e_w_in", [2, D, 2592])
        i["e_w_out"] = self.din("e_w_out", [2, D, D])
        i["o_w_in"] = self.din("o_w_in", [2, D, 3968])
        i["o_w_out"] = self.din("o_w_out", [2, D, D])
        i["gla_up"] = self.din("gla_up", [2, 2, 16, 256])
        i["gla_b"] = self.din("gla_b", [2, 2, 128, 2])
        i["gla_ng"] = self.din("gla_ng", [2, 128, 1])
        i["ret_ng"] = self.din("ret_ng", [2, 128, 1])
        i["conv_w"] = self.din("conv_w", [2, 128, 4, 4])
        i["conv_b"] = self.din("conv_b", [2, 128, 4])
        i["lru_w"] = self.din("lru_w", [2, 4, 4, 128, 128])
        i["lru_b"] = self.din("lru_b", [2, 4, 128, 4])
        i["lru_lam"] = self.din("lru_lam", [2, 2, 128, 4])
        i["sink"] = self.din("sink", [2, 128, 8])
        i["ret_dec"] = self.din("ret_dec", [2, 2, 128, 2])
        i["ident"] = self.din("ident", [128, 128], BF16)
        i["masks"] = self.din("masks", [4, 128, 128])
        i["rmask"] = self.din("rmask", [2, 128, 512])
        i["ropec"] = self.din("ropec", [128, S])
        i["ropes"] = self.din("ropes", [128, S])
        i["segw"] = self.din("segw", [128, 16])
        self.i = i
        s = {}
        s["cc_h_in"] = self.dscr("cc_h_in", [128, 12])
        s["cc_h_out"] = self.dscr("cc_h_out", [512, 12])
        s["cc_kv_in"] = self.dscr("cc_kv_in", [256, 384], BF16)
        s["cc_kv_out"] = self.dscr("cc_kv_out", [1024, 384], BF16)
        s["cc_s_in"] = self.dscr("cc_s_in", [512, 129])
        s["cc_s_out"] = self.dscr("cc_s_out", [2048, 129])
        s["cc_l_in"] = self.dscr("cc_l_in", [128, 16])
        s["cc_l_out"] = self.dscr("cc_l_out", [512, 16])
        s["AF"] = self.dscr("AF", [512, T])
        s["HB"] = self.dscr("HB", [512, T])
        s["AB"] = self.dscr("AB", [512, T])
        s["X"] = self.dscr("X", [D, T])
        s["YT"] = self.dscr("YT", [D, T], BF16)
        s["QT"] = self.dscr("QT", [256, T])
        s["KT"] = self.dscr("KT", [256, T])
        s["VT"] = self.dscr("VT", [T, 512], BF16)
        s["GA"] = self.dscr("GA", [512, T])
        s["LR"] = self.dscr("LR", [2, 16, T])
        s["XR"] = self.dscr("XR", [512, T])
        s["GB"] = self.dscr("GB", [512, T])
        s["OF"] = self.dscr("OF", [512, T])
        s["HF"] = self.dscr("HF", [512, T])
        s["AQ"] = self.dscr("AQ", [512, T], BF16)
        s["AK"] = self.dscr("AK", [2, 128, T], BF16)
        s["AV"] = self.dscr("AV", [T, 128], BF16)
        self.s = s
        self.out = self.nc.dram_tensor("out", [D, S], F32, kind="ExternalOutput")

    def dv(self, handle, name, rowkey, ap, c0, n):
        return V(ap, self.p.dres(name, rowkey, c0, n))

    def fm(self, name, r0, nrows, c0, n, src=None):
        h = (src or self.s)[name]
        if nrows <= 128:
            ap = h[r0:r0 + nrows, c0:c0 + n]
            keys = [r0 // 128]
        else:
            ap = h[r0:r0 + nrows, c0:c0 + n].rearrange("(c p) t -> p c t", p=128)
            keys = list(range(r0 // 128, (r0 + nrows) // 128))
        res = []
        for k in keys:
            res += self.p.dres(name, k, c0, n)
        return V(ap, res)

    def tm(self, name, c0, n, col0, ncol):
        h = self.s[name]
        if n <= 128:
            ap = h[c0:c0 + n, col0:col0 + ncol]
        else:
            ap = h[c0:c0 + n, col0:col0 + ncol].rearrange("(b p) v -> p b v", p=128)
        return V(ap, self.p.dres(name, "all", c0, n))

    def cst(self, ap):
        return V(ap, [])

    def build(self):
        nc, es = self.nc, self.es
        self.declare()
        p = self.p = Prog(nc, es)
        i = self.i
        self.Fp = [es.enter_context(nc.sbuf_tensor("poolF%d" % k, [128, 4, 512], F32)) for k in range(10)]
        self.Hp = [es.enter_context(nc.sbuf_tensor("poolH%d" % k, [128, 4, 512], BF16)) for k in range(6)]
        self.ones_bf = p.sb([128, 128], BF16, "ones_bf")
        p.memset("pool", self.ones_bf, 1.0)
        self.ones_f = p.sb([128, 512], F32, "ones_f")
        p.memset("pool", self.ones_f, 1.0)
        self.ident = p.sb([128, 128], BF16, "ident")
        p.dma("sp", self.ident, self.cst(i["ident"].ap()))
        self.masks = p.sb([128, 4, 128], F32, "masks")
        p.dma("sp", self.masks, self.cst(i["masks"].ap().rearrange("m p t -> p m t")))
        self.rmask = p.sb([128, 2, 512], F32, "rmask")
        p.dma("sp", self.rmask, self.cst(i["rmask"].ap().rearrange("m p t -> p m t")))
        self.segw = p.sb([128, 4, 4], F32, "segw")
        p.dma("sp", self.segw, self.cst(i["segw"].ap().rearrange("p (k r) -> p k r", r=4)))
        self.groups = [[0, 1, 2, 3], [4, 5, 6, 7]]
        self.Wb = p.sb([128, 8, 3968], BF16, "Wb")
        self.Wo = p.sb([128, 8, 1024], BF16, "Wo")
        self.wstg = [p.sb([128, 512], F32, "wstg%d" % k) for k in range(2)]
        self.wrr = 0
        self.banks = [p.ps([128, 512], F32, "bank%d" % b) for b in range(7)]
        self.bank_bf = p.ps([128, 1024], BF16, "bankbf")
        self.brr = 0
        self.prologue()
        p.barrier()
        stg = [self.Fv(k) for k in range(4)]
        cnt = 0
        for (c0, n, isctx) in self.tiles:
            for half in range(2):
                st = stg[cnt % 4]
                cnt += 1
                p.dma("sp", st[:, :, :n], self.cst(i["xin"][half * 512:(half + 1) * 512, c0:c0 + n].rearrange("(c p) t -> p c t", p=128)))
                p.dma("pool", self.fm("X", half * 512, 512, c0, n), st[:, :, :n])
        self.wq = []
        for l in range(self.DEPTH):
            j = l // 2
            if l == 0:
                self.load_w_in(i["e_w_in"][j], 2592)
            else:
                self.pump(10 ** 6)
            if l % 2 == 0:
                self.load_w_out(i["e_w_out"][j])
                self.phase1_even(l)
                if l + 1 < self.DEPTH:
                    self.queue_w_in(i["o_w_in"][j], 3968)
                self.linattn(l, "gla")
                self.lru(l)
            else:
                self.load_w_out(i["o_w_out"][j])
                self.phase1_odd(l)
                if l + 1 < self.DEPTH:
                    self.queue_w_in(i["e_w_in"][j + 1], 2592)
                import os
                if "ret" not in os.environ.get("KSKIP", ""):
                    self.linattn(l, "ret")
                if "swa" not in os.environ.get("KSKIP", ""):
                    self.swa(l)
            self.phase3(l)
        self.final_norm()
        p.finish()
        return nc

    def Fv(self, k, pl=None, npl=1, rows=128):
        t = self.Fp[k]
        if pl is None:
            return V(t.ap(), [Res()])
        if npl == 1:
            return V(t[0:rows, pl, :], [Res()])
        return V(t[0:rows, pl:pl + npl, :], [Res()])

    def Hv(self, k, pl=None, npl=1):
        t = self.Hp[k]
        if pl is None:
            return V(t.ap(), [Res()])
        if npl == 1:
            return V(t[:, pl, :], [Res()])
        return V(t[:, pl:pl + npl, :], [Res()])

    def bank(self):
        b = self.banks[self.brr]
        self.brr = (self.brr + 1) % len(self.banks)
        return b

    def prologue(self):
        p, i = self.p, self.i
        cv = p.sb([128, 8, 2], F32, "cvec")
        p.dma("sp", cv, self.cst(i["cvec"].ap()))
        sc = p.sb([128, 8, 2], F32, "silu_c")
        p.act(sc, cv, AF.Silu)
        self.GS, self.SH, self.GT = [], [], []
        wb = [[self.Fv(0), self.Fv(1)], [self.Fv(2), self.Fv(3)]]
        cnt = 0
        for l in range(self.DEPTH):
            ab = p.sb([128, 24], F32, "ada_b%d" % l)
            p.dma("sp", ab, self.cst(i["ada_b"][l]))
            ng = p.sb([128, 8], F32, "norm_g%d" % l)
            p.dma("sp", ng, self.cst(i["norm_g"][l]))
            mod = p.sb([128, 24, 2], F32, "mod%d" % l)
            bk = self.bank()
            for jg in range(6):
                w2 = wb[cnt % 2]
                cnt += 1
                for half in range(2):
                    p.dma("sp", w2[half], self.cst(i["ada_w"][l, half * 512:(half + 1) * 512, jg * 512:(jg + 1) * 512].rearrange("(k p) c -> p k c", p=128)))
                for j4 in range(4):
                    jj = jg * 4 + j4
                    for k in range(8):
                        p.mm(bk[:, jj * 2:jj * 2 + 2], w2[k // 4][:, k % 4, j4 * 128:(j4 + 1) * 128], sc[:, k, :],
                             start=(k == 0), stop=(k == 7))
            p.tt("dve", mod, bk[:, 0:48].re("p (j c) -> p j c", c=2), ab.re("p (j o) -> p j o", o=1).bc([128, 24, 2]), ALU.add)
            gs = p.sb([128, 8, 2], F32, "gs%d" % l)
            p.ts("dve", gs, mod[:, 8:16, :], 1.0, None, ALU.add)
            p.tt("dve", gs, gs, ng.re("p (j o) -> p j o", o=1).bc([128, 8, 2]), ALU.mult)
            self.GS.append(gs)
            self.SH.append(mod[:, 0:8, :])
            self.GT.append(mod[:, 16:24, :])
        self.fg = p.sb([128, 8], F32, "final_g")
        p.dma("sp", self.fg, self.cst(i["final_g"].ap()))

    def load_w_in(self, wap, ncol):
        p = self.p
        for k in range(8):
            for c0 in range(0, ncol, 512):
                w = min(512, ncol - c0)
                st = self.wstg[self.wrr % 2]
                self.wrr += 1
                p.dma("sp", st[:, :w], self.cst(wap[k * 128:(k + 1) * 128, c0:c0 + w]))
                p.copy("pool", self.Wb[:, k, c0:c0 + w], st[:, :w])

    def queue_w_in(self, wap, ncol):
        for k in range(8):
            for c0 in range(0, ncol, 512):
                w = min(512, ncol - c0)
                self.wq.append((k, c0, w, wap))

    def pump(self, n):
        p = self.p
        while n > 0 and self.wq:
            k, c0, w, wap = self.wq.pop(0)
            st = self.wstg[self.wrr % 2]
            self.wrr += 1
            p.dma("sp", st[:, :w], self.cst(wap[k * 128:(k + 1) * 128, c0:c0 + w]))
            p.copy("pool", self.Wb[:, k, c0:c0 + w], st[:, :w])
            n -= 1

    def load_w_out(self, wap):
        p = self.p
        for k in range(8):
            for c0 in range(0, 1024, 512):
                st = self.wstg[self.wrr % 2]
                self.wrr += 1
                p.dma("sp", st[:, :512], self.cst(wap[k * 128:(k + 1) * 128, c0:c0 + 512]))
                p.copy("pool", self.Wo[:, k, c0:c0 + 512], st[:, :512])

    def norm_alloc(self):
        self.xt = [self.Fv(0), self.Fv(1)]
        self.sq = [self.Hv(0), self.Hv(1)]
        self.hT = [self.Hv(2), self.Hv(3)]
        self.rstd = self.Fv(2, 0)

    def norm_tile(self, l, c0, n, isctx, src="X"):
        p = self.p
        w = 1 if isctx else 0
        xt, sq, hT, rstd = self.xt, self.sq, self.hT, self.rstd
        bk = self.bank()
        for half in range(2):
            p.dma("sp", xt[half][:, :, :n], self.fm(src, half * 512, 512, c0, n))
            p.act(sq[half][:, :, :n], xt[half][:, :, :n], AF.Square)
        for k in range(8):
            p.mm(bk[:, :n], self.ones_bf, sq[k // 4][:, k % 4, :n], start=(k == 0), stop=(k == 7))
        p.ts("dve", rstd[:, :n], bk[:, :n], 1.0 / D, EPS, ALU.mult, ALU.add)
        p.act(rstd[:, :n], rstd[:, :n], AF.Ln)
        p.act(rstd[:, :n], rstd[:, :n], AF.Exp, scale=-0.5)
        for k in range(8):
            eng = "dve" if k % 2 == 0 else "pool"
            xk = xt[k // 4][:, k % 4, :n]
            p.tt(eng, xk, xk, rstd[:, :n], ALU.mult)
            if l is None:
                p.act(xk, xk, AF.Identity, scale=self.fg[:, k:k + 1])
            else:
                p.act(hT[k // 4][:, k % 4, :n], xk, AF.Identity, bias=self.SH[l][:, k, w:w + 1],
                      scale=self.GS[l][:, k, w:w + 1])
        return lambda k: hT[k // 4][:, k % 4, :]

    def proj_fm(self, hT, n, col0, m):
        p = self.p
        bk = self.bank()
        for k in range(8):
            p.mm(bk[:m, :n], self.Wb[:, k, col0:col0 + m], hT(k)[:, :n], start=(k == 0), stop=(k == 7))
        return bk

    def proj_tm(self, hT, blk, col0, m):
        p = self.p
        bk = self.bank()
        for k in range(8):
            p.mm(bk[:, :m], hT(k)[:, blk * 128:(blk + 1) * 128], self.Wb[:, k, col0:col0 + m],
                 start=(k == 0), stop=(k == 7))
        return bk

    def phase1_even(self, l):
        p = self.p
        C = EVEN_COLS
        p.barrier()
        self.norm_alloc()
        stg = [self.Fv(k) for k in (3, 4, 5, 6)]
        vstg = [self.Hv(4), self.Hv(5)]
        lrstg = [self.Fv(7, k, rows=16) for k in range(4)]
        srr = 0
        ti = 0
        for (c0, n, isctx) in self.tiles:
            hT = self.norm_tile(l, c0, n, isctx)
            for (nm, dst, scale) in (("q", "QT", 0.125), ("k", "KT", None)):
                st = stg[srr % 4]
                srr += 1
                for c in range(2):
                    bk = self.proj_fm(hT, n, C[nm][0] + c * 128, 128)
                    if scale is not None:
                        p.act(st[:, c, :n], bk[:, :n], AF.Copy, scale=scale)
                    else:
                        p.copy("dve", st[:, c, :n], bk[:, :n])
                p.dma("pool", self.fm(dst, 0, 256, c0, n), st[:, 0:2, :n])
            for (nm, dst, silu) in (("gg", "GA", True), ("xr", "XR", False), ("gl", "GB", True)):
                st = stg[srr % 4]
                srr += 1
                for c in range(4):
                    bk = self.proj_fm(hT, n, C[nm][0] + c * 128, 128)
                    if silu:
                        p.act(st[:, c, :n], bk[:, :n], AF.Silu)
                    else:
                        p.copy("dve", st[:, c, :n], bk[:, :n])
                p.dma("pool", self.fm(dst, 0, 512, c0, n), st[:, :, :n])
            for d, nm in enumerate(("lrf", "lrb")):
                bk = self.proj_fm(hT, n, C[nm][0], 16)
                ls = lrstg[(2 * ti + d) % 4]
                p.copy("dve", ls[:, :n], bk[:16, :n])
                p.dma("pool", V(self.s["LR"][d, :, c0:c0 + n], p.dres("LR", d, c0, n)), ls[:, :n])
            vs = vstg[ti % 2]
            nb = n // 128
            for b in range(nb):
                bk = self.proj_tm(hT, b, C["v"][0], 512)
                p.copy("act" if b % 2 else "dve", vs[:, b, :], bk[:, :])
            p.dma("pool", self.tm("VT", c0, n, 0, 512), vs[:, :nb, :])
            ti += 1

    def linattn(self, l, kind):
        p, i = self.p, self.i
        j = l // 2
        NSEG = self.NSEG
        p.barrier()
        if not hasattr(self, "la_S"):
            self.la_dec = [p.sb([128, 8], F32, "la_dec%d" % k) for k in range(2)]
            self.la_S = [p.sb([128, 128], F32, "la_S%d" % k) for k in range(2)]
            self.la_Sb = [[p.sb([128, 128], BF16, "la_Sb%d_%d" % (k, m)) for m in range(2)] for k in range(2)]
            self.la_att = [p.sb([128, 128], BF16, "la_att%d" % k) for k in range(4)]
            self.la_up = p.sb([16, 2, 256], F32, "la_up")
            self.la_nb = p.sb([128, 2, 2], F32, "la_nb")
            self.la_ng = p.sb([128, 1], F32, "la_ng")
            self.la_Lr = p.sb([128, 2, 2], F32, "la_Lr")
            self.la_asum = p.sb([128, 4], F32, "la_asum")
            self.la_ared = p.sb([128, 1], F32, "la_ared")
            self.la_tmpS = p.sb([128, 128], F32, "la_tmpS")
        la_q = [self.Fv(0, 0), self.Fv(0, 1)]
        la_k = [self.Fv(0, 2), self.Fv(0, 3)]
        la_L = [self.Fv(1, 0), self.Fv(1, 1)]
        la_b = [self.Fv(1, 2), self.Fv(1, 3)]
        la_e = [self.Fv(2, 0), self.Fv(2, 1)]
        la_lr = [self.Fv(2, 2, rows=16), self.Fv(2, 3, rows=16)]
        la_v = [self.Hv(0), self.Hv(1)]
        la_qi = [self.Hv(2, 0), self.Hv(2, 1)]
        la_ki = [self.Hv(2, 2), self.Hv(2, 3)]
        la_ks = [self.Hv(3, 0), self.Hv(3, 1)]
        la_kst = [self.Hv(3, 2).re("p (b c) -> p b c", c=128), self.Hv(3, 3).re("p (b c) -> p b c", c=128)]
        la_o = [self.Fv(3, 0, 2), self.Fv(3, 2, 2)]
        la_of = [self.Fv(4, 0, 2), self.Fv(4, 2, 2)]
        la_g = [self.Fv(5, 0, 2), self.Fv(5, 2, 2)]
        la_rs = self.Fv(6, 0)
        la_o2 = self.Hv(4, 0)
        la_y = [self.Hv(5, 0, 2), self.Hv(5, 2, 2)]
        la_sst = self.Fv(7, 0, 2).re("p a b -> p (a b)")
        la_G = [V(self.Fp[8 + d].ap().rearrange("p a b -> p (a b)")[:, 0:1032].rearrange("p (r q c) -> p r q c", r=4, q=2), [Res()])
                for d in range(2)]
        if kind == "gla":
            p.dma("sp", self.la_up, self.cst(i["gla_up"][j].rearrange("d r c -> r d c")))
            p.dma("sp", self.la_nb, self.cst(i["gla_b"][j].rearrange("d p c -> p d c")))
            p.ts("dve", self.la_nb, self.la_nb, -1.0, None, ALU.mult)
            p.dma("sp", self.la_ng, self.cst(i["gla_ng"][j]))
            yrow0 = 0
        else:
            p.dma("sp", self.la_ng, self.cst(i["ret_ng"][j]))
            p.dma("sp", self.la_Lr, self.cst(i["ret_dec"][j].rearrange("d p c -> p d c")))
            p.act(self.la_Lr, self.la_Lr, AF.Exp, scale=-1.0)
            p.act(self.la_Lr, self.la_Lr, AF.Ln, bias=1.0)
            p.ts("dve", self.la_Lr, self.la_Lr, 16.0, None, ALU.mult)
            yrow0 = 512
        cnt = [0]
        sbi = [0, 0]
        rotA = [self.banks[k] for k in (4, 5, 6)]
        rai = [0]

        def rbA():
            bnk = rotA[rai[0] % 3]
            rai[0] += 1
            return bnk

        def do_tile(d, c0, n, state_only):
            mask = self.masks[:, d, :]
            nblk = n // 128
            nch = n // 64
            v = la_v[cnt[0] % 2]
            p.dma("sp", v[:, :nblk, :], self.tm("VT", c0, n, 0, 512))
            self.pump(3)
            for hp in range(2):
                x = hp
                qt, kt, L, bc, e = la_q[x], la_k[x], la_L[x], la_b[x], la_e[x]
                qi, ki, ks, kst, dec = la_qi[x], la_ki[x], la_ks[x], la_kst[x], self.la_dec[x]
                if not state_only:
                    p.dma("sp", qt[:, :n], self.fm("QT", hp * 128, 128, c0, n))
                p.dma("sp", kt[:, :n], self.fm("KT", hp * 128, 128, c0, n))
                if kind == "gla":
                    lr = la_lr[x]
                    p.dma("sp", lr[:, :n], V(self.s["LR"][d, :, c0:c0 + n], p.dres("LR", d, c0, n)))
                    bk = rbA()
                    p.mm(bk[:, :n], self.la_up[:, d, hp * 128:(hp + 1) * 128], lr[:, :n])
                    p.act(L[:, :n], bk[:, :n], AF.Exp, scale=-1.0, bias=self.la_nb[:, d, hp:hp + 1])
                    p.act(L[:, :n], L[:, :n], AF.Ln, bias=1.0)
                else:
                    p.ts("dve", L[:, :n], self.ones_f[:, :n], self.la_Lr[:, d, hp:hp + 1], None, ALU.mult)
                if d == 0:
                    p.scan(bc[:, :n], self.rmask[:, 0, :n], L[:, :n], 0.0)
                else:
                    p.scan(bc[:, :n][:, ::-1], self.rmask[:, 1, :n][:, ::-1], L[:, :n][:, ::-1], 0.0)
                b3 = bc[:, :n].re("p (c i) -> p c i", i=64)
                bend = b3[:, :, 63:64] if d == 0 else b3[:, :, 0:1]
                if not state_only:
                    p.act(e[:, :n], bc[:, :n], AF.Exp, scale=-1.0 / 16)
                    p.tt("dve", qi[:, :n], qt[:, :n], e[:, :n], ALU.mult)
                    p.act(e[:, :n], bc[:, :n], AF.Exp, scale=1.0 / 16)
                    p.tt("pool", ki[:, :n], kt[:, :n], e[:, :n], ALU.mult)
                else:
                    p.reduce(self.la_ared, bend.re("p c o -> p (c o)"), ALU.add)
                    aq = self.la_asum[:, 2 * d + hp:2 * d + hp + 1]
                    p.tt("dve", aq, aq, self.la_ared, ALU.add)
                p.act(dec[:, :nch], bend.re("p c o -> p (c o)"), AF.Exp, scale=-1.0 / 16)
                p.tt("dve", e[:, :n].re("p (c i) -> p c i", i=64), bend.bc([128, nch, 64]), b3, ALU.subtract)
                p.act(e[:, :n], e[:, :n], AF.Exp, scale=-1.0 / 16)
                p.tt("pool", ks[:, :n], kt[:, :n], e[:, :n], ALU.mult)
                for b in range(nblk):
                    p.tr(self.bank_bf[:, (hp * 4 + b) * 128:(hp * 4 + b + 1) * 128], ks[:, b * 128:(b + 1) * 128], self.ident)
                p.copy("act", kst[:, :nblk, :], self.bank_bf[:, hp * 512:hp * 512 + nblk * 128].re("p (b c) -> p b c", c=128))
            blks = list(range(nblk)) if d == 0 else list(reversed(range(nblk)))
            for b in blks:
                cs = slice(b * 128, (b + 1) * 128)
                chunks = [0, 1] if d == 0 else [1, 0]
                psO = [[None, None], [None, None]]
                if not state_only:
                    for hp in range(2):
                        qi, ki = la_qi[hp], la_ki[hp]
                        psAs = []
                        for h in range(2):
                            r = slice(h * 64, (h + 1) * 64)
                            psA = rbA()
                            psAs.append(psA)
                            p.mm(psA[:, :128], ki[r, cs], qi[r, cs])
                        for h in range(2):
                            am = self.la_att[(2 * hp + h) % 4]
                            p.tt("dve", am, psAs[h][:, :128], mask, ALU.mult)
                        for h in range(2):
                            head = 2 * hp + h
                            vc = slice(head * 128, (head + 1) * 128)
                            am = self.la_att[(2 * hp + h) % 4]
                            po = self.banks[2 * hp + h]
                            psO[hp][h] = po
                            p.mm(po[:, :128], v[:, b, vc], am, start=True, stop=False)
                for ci, cc in enumerate(chunks):
                    tk = slice(b * 128 + cc * 64, b * 128 + cc * 64 + 64)
                    pk = slice(cc * 64, cc * 64 + 64)
                    for hp in range(2):
                        qi, kst, dec = la_qi[hp], la_kst[hp], self.la_dec[hp]
                        S = self.la_S[hp]
                        Sb = self.la_Sb[hp][sbi[hp] % 2]
                        psD = rbA()
                        for h in range(2):
                            r = slice(h * 64, (h + 1) * 64)
                            head = 2 * hp + h
                            vc = slice(head * 128, (head + 1) * 128)
                            if not state_only:
                                p.mm(psO[hp][h][:, cc * 64:cc * 64 + 64], Sb[r, :], qi[r, tk], start=False, stop=(ci == 1))
                            p.mm(psD[r, :128], kst[pk, b, r], v[pk, b, vc])
                        chn = b * 2 + cc
                        p.stt("dve", S, S, dec[:, chn:chn + 1], psD[:, :128], ALU.mult, ALU.add)
                        if not state_only:
                            sbi[hp] += 1
                            p.copy("act", self.la_Sb[hp][sbi[hp] % 2], S)
                if not state_only:
                    for hp in range(2):
                        for h in range(2):
                            p.copy("act" if h else "dve", la_o[hp][:, h, cs], psO[hp][h][:, :128])
            if not state_only:
                for hp in range(2):
                    x = hp
                    ot = la_o[x]
                    if d == 0:
                        p.dma("pool", self.fm("OF", hp * 256, 256, c0, n), ot[:, :, :n])
                    else:
                        oft, gt, yt = la_of[x], la_g[x], la_y[x]
                        p.dma("sp", oft[:, :, :n], self.fm("OF", hp * 256, 256, c0, n))
                        p.dma("sp", gt[:, :, :n], self.fm("GA", hp * 256, 256, c0, n))
                        for h in range(2):
                            o = ot[:, h, :n]
                            p.tt("dve", o, o, oft[:, h, :n], ALU.add)
                            p.act(la_o2[:, :n], o, AF.Square)
                            bk = rbA()
                            p.mm(bk[:, :n], self.ones_bf, la_o2[:, :n])
                            p.ts("dve", la_rs[:, :n], bk[:, :n], 1.0 / 128, EPS, ALU.mult, ALU.add)
                            p.act(la_rs[:, :n], la_rs[:, :n], AF.Ln)
                            p.act(la_rs[:, :n], la_rs[:, :n], AF.Exp, scale=-0.5)
                            p.stt("dve", o, o, self.la_ng[:, 0:1], la_rs[:, :n], ALU.mult, ALU.mult)
                            p.tt("pool", yt[:, h, :n], o, gt[:, h, :n], ALU.mult)
                        p.dma("pool", self.fm("YT", yrow0 + hp * 256, 256, c0, n), yt[:, :, :n])
            cnt[0] += 1

        def zero_state(bf=True):
            for hp in range(2):
                p.memset("dve", self.la_S[hp], 0.0)
                if bf:
                    p.memset("pool", self.la_Sb[hp][sbi[hp] % 2], 0.0)

        lat = self.tiles[1:]
        ctxt = self.tiles[0]
        if NSEG > 1:
            p.memset("dve", self.la_asum, 0.0)
            for d in range(2):
                zero_state(bf=False)
                for (c0, n, isctx) in (lat if d == 0 else list(reversed(lat))):
                    do_tile(d, c0, n, True)
                for hp in range(2):
                    q = 2 * d + hp
                    p.copy("dve", la_sst[:, q * 129:q * 129 + 128], self.la_S[hp])
                    p.act(la_sst[:, q * 129 + 128:q * 129 + 129], self.la_asum[:, q:q + 1], AF.Exp, scale=-1.0 / 16)
            cin = V(self.s["cc_s_in"].ap().rearrange("(q p) c -> p q c", p=128), p.dres("cc_s_in", 0, 0, 128))
            p.dma("pool", cin, la_sst[:, 0:516].re("p (q c) -> p q c", c=129))
            cout = V(self.s["cc_s_out"].ap(), p.dres("cc_s_out", 0, 0, 128))
            p.allgather(cout, V(self.s["cc_s_in"].ap(), cin.res), self.groups)
            for d in range(2):
                for r in range(4):
                    src = self.s["cc_s_out"][r * 512 + d * 256:r * 512 + d * 256 + 256, :].rearrange("(q p) c -> p q c", p=128)
                    p.dma("sp", la_G[d][:, r, :, :], V(src, cout.res))
        for d in range(2):
            zero_state()
            do_tile(d, ctxt[0], ctxt[1], False)
            if NSEG > 1:
                for hp in range(2):
                    S = self.la_S[hp]
                    for r in (range(4) if d == 0 else reversed(range(4))):
                        A_r = la_G[d][:, r, hp, 128:129]
                        B_r = la_G[d][:, r, hp, 0:128]
                        p.stt("dve", self.la_tmpS, S, A_r, B_r, ALU.mult, ALU.add)
                        p.tt("dve", self.la_tmpS, self.la_tmpS, S, ALU.subtract)
                        p.stt("dve", S, self.la_tmpS, self.segw[:, 2 + d, r:r + 1], S, ALU.mult, ALU.add)
                    sbi[hp] += 1
                    p.copy("act", self.la_Sb[hp][sbi[hp] % 2], S)
            for (c0, n, isctx) in (lat if d == 0 else list(reversed(lat))):
                do_tile(d, c0, n, False)


    def lru(self, l):
        p, i = self.p, self.i
        j = l // 2
        NSEG = self.NSEG
        p.barrier()
        if not hasattr(self, "lw"):
            self.lw = p.sb([128, 4, 4, 128], BF16, "lru_w")
            self.lb = p.sb([128, 4, 4], F32, "lru_b")
            self.lc8 = p.sb([128, 2, 4], F32, "lru_c8")
            self.lc16 = p.sb([128, 2, 4], F32, "lru_c16")
            self.lcw = p.sb([128, 4, 4], F32, "lru_cw")
            self.lcb = p.sb([128, 4], F32, "lru_cb")
            self.lr_carry = p.sb([128, 4], F32, "lru_carry")
            self.lr_acar = p.sb([128, 4], F32, "lru_acar")
            self.lr_hal = p.sb([128, 4, 3], F32, "lru_hal")
            self.lr_halG = p.sb([128, 4, 12], F32, "lru_halG")
            self.lr_halL = p.sb([128, 4, 2], F32, "lru_halL")
            self.lr_halR = p.sb([128, 4, 1], F32, "lru_halR")
            self.lr_sum = p.sb([128, 16], F32, "lru_sum")
            self.lr_sumG = p.sb([128, 4, 16], F32, "lru_sumG")
            self.lr_hin = p.sb([128, 2, 4], F32, "lru_hin")
            self.lr_t4 = p.sb([128, 4], F32, "lru_t4")
        lxh = [self.Fv(0, 0, 2).re("p a b -> p (a b)"), self.Fv(0, 2, 2).re("p a b -> p (a b)")]
        lxc = [self.Fv(1, 0), self.Fv(1, 1)]
        lr_r = [self.Fv(1, 2), self.Fv(1, 3)]
        lr_i = [self.Fv(2, 0), self.Fv(2, 1)]
        lr_a = [self.Fv(2, 2), self.Fv(2, 3)]
        lr_s = [self.Fv(3, 0), self.Fv(3, 1)]
        lxb = [self.Hv(0, 0), self.Hv(0, 1)]
        lr_h = [self.Fv(4), self.Fv(5)]
        lr_hf = [self.Fv(6), self.Fv(7)]
        lr_g = [self.Fv(8), self.Fv(9)]
        lr_y = [self.Hv(1), self.Hv(2)]
        for kd in range(4):
            for n_ in range(4):
                st = self.wstg[self.wrr % 2]
                self.wrr += 1
                p.dma("sp", st[:, :128], self.cst(i["lru_w"][j, kd, n_]))
                p.copy("pool", self.lw[:, kd, n_, :], st[:, :128])
        p.dma("sp", self.lb, self.cst(i["lru_b"][j].rearrange("k p c -> p k c")))
        p.dma("sp", self.lcw, self.cst(i["conv_w"][j]))
        p.dma("sp", self.lcb, self.cst(i["conv_b"][j]))
        p.dma("sp", self.lc8, self.cst(i["lru_lam"][j].rearrange("d p c -> p d c")))
        p.act(self.lc8, self.lc8, AF.Exp, scale=-1.0)
        p.act(self.lc8, self.lc8, AF.Ln, bias=1.0)
        p.ts("dve", self.lc16, self.lc8, -16.0, None, ALU.mult)
        p.ts("dve", self.lc8, self.lc8, -8.0, None, ALU.mult)
        T = self.TALL
        if NSEG > 1:
            hal = self.lr_hal
            p.dma("sp", hal[:, :, 0:1], self.fm("XR", 0, 512, CTX, 1), slow=True)
            p.dma("sp", hal[:, :, 1:3], self.fm("XR", 0, 512, T - 2, 2), slow=True)
            cin = V(self.s["cc_h_in"].ap(), p.dres("cc_h_in", 0, 0, 128))
            p.dma("pool", cin, hal.re("p c k -> p (c k)"))
            cout = V(self.s["cc_h_out"].ap(), p.dres("cc_h_out", 0, 0, 128))
            p.allgather(cout, cin, self.groups)
            p.dma("sp", self.lr_halG, V(self.s["cc_h_out"].ap().rearrange("(r p) k -> p r k", p=128), cout.res))
            G4 = self.lr_halG.re("p r (c k) -> p r c k", k=3)
            for r in range(4):
                wl = self.segw[:, 0, r:r + 1]
                wr = self.segw[:, 1, r:r + 1]
                if r == 0:
                    p.ts("dve", self.lr_halL, G4[:, r, :, 1:3], wl, None, ALU.mult)
                    p.ts("dve", self.lr_halR, G4[:, r, :, 0:1], wr, None, ALU.mult)
                else:
                    p.stt("dve", self.lr_halL, G4[:, r, :, 1:3], wl, self.lr_halL, ALU.mult, ALU.add)
                    p.stt("dve", self.lr_halR, G4[:, r, :, 0:1], wr, self.lr_halR, ALU.mult, ALU.add)
        cnt = 0
        lat = self.tiles[1:]
        for d in range(2):
            order = [self.tiles[0]] + (list(lat) if d == 0 else list(reversed(lat)))
            first = True
            for ti_, (c0, n, isctx) in enumerate(order):
                seg0, seg1 = (0, CTX) if isctx else (CTX, T)
                h = lr_h[cnt % 2]
                ac = lr_hf[cnt % 2] if (NSEG > 1 and not isctx) else None
                seg_first = (NSEG > 1 and ti_ == 1)
                for ch in range(4):
                    x = ch % 2
                    xh, xc, xb = lxh[x], lxc[x], lxb[x]
                    lo = max(seg0, c0 - 2)
                    hi = min(seg1, c0 + n + 1)
                    if lo > c0 - 2 or hi < c0 + n + 1:
                        p.memset("pool", xh[:, 0:516], 0.0)
                    p.dma("sp", xh[:, lo - (c0 - 2):hi - (c0 - 2)], self.fm("XR", ch * 128, 128, lo, hi - lo))
                    if NSEG > 1 and not isctx:
                        if lo > c0 - 2:
                            p.copy("dve", xh[:, 0:2], self.lr_halL[:, ch, :])
                        if hi < c0 + n + 1:
                            p.copy("dve", xh[:, n + 2:n + 3], self.lr_halR[:, ch, :])
                    p.ts("dve", xc[:, :n], xh[:, 0:n], self.lcw[:, ch, 0:1], self.lcb[:, ch:ch + 1], ALU.mult, ALU.add)
                    for k in range(1, 4):
                        p.stt("dve", xc[:, :n], xh[:, k:k + n], self.lcw[:, ch, k:k + 1], xc[:, :n], ALU.mult, ALU.add)
                    p.copy("act", xb[:, :n], xc[:, :n])
                    r, ig, a, s = lr_r[x], lr_i[x], lr_a[x], lr_s[x]
                    bk = self.bank()
                    p.mm(bk[:, :n], self.lw[:, 2 * d, ch, :], xb[:, :n])
                    p.act(r[:, :n], bk[:, :n], AF.Sigmoid, bias=self.lb[:, 2 * d, ch:ch + 1])
                    bk2 = self.bank()
                    p.mm(bk2[:, :n], self.lw[:, 2 * d + 1, ch, :], xb[:, :n])
                    p.act(ig[:, :n], bk2[:, :n], AF.Sigmoid, bias=self.lb[:, 2 * d + 1, ch:ch + 1])
                    p.act(a[:, :n], r[:, :n], AF.Exp, scale=self.lc8[:, d, ch:ch + 1])
                    p.act(s[:, :n], r[:, :n], AF.Exp, scale=self.lc16[:, d, ch:ch + 1])
                    p.act(s[:, :n], s[:, :n], AF.Sqrt, scale=-1.0, bias=1.0)
                    p.tt("pool", ig[:, :n], ig[:, :n], xc[:, :n], ALU.mult)
                    p.tt("dve", s[:, :n], s[:, :n], ig[:, :n], ALU.mult)
                    init = 0.0 if (first or seg_first) else self.lr_carry[:, ch:ch + 1]
                    ainit = 1.0 if seg_first else self.lr_acar[:, ch:ch + 1]
                    if NSEG > 1 and seg_first:
                        p.copy("dve", self.lr_hin[:, d, ch:ch + 1], self.lr_carry[:, ch:ch + 1])
                    if d == 0:
                        p.scan(h[:, ch, :n], a[:, :n], s[:, :n], init)
                        p.copy("dve", self.lr_carry[:, ch:ch + 1], h[:, ch, n - 1:n])
                        if ac is not None:
                            p.scan(ac[:, ch, :n], a[:, :n], self.ones_f[:, :n], ainit, ALU.mult, ALU.mult)
                            p.copy("dve", self.lr_acar[:, ch:ch + 1], ac[:, ch, n - 1:n])
                    else:
                        p.scan(h[:, ch, :n][:, ::-1], a[:, :n][:, ::-1], s[:, :n][:, ::-1], init)
                        p.copy("dve", self.lr_carry[:, ch:ch + 1], h[:, ch, 0:1])
                        if ac is not None:
                            p.scan(ac[:, ch, :n][:, ::-1], a[:, :n][:, ::-1], self.ones_f[:, :n], ainit, ALU.mult, ALU.mult)
                            p.copy("dve", self.lr_acar[:, ch:ch + 1], ac[:, ch, 0:1])
                first = False
                if NSEG > 1 and not isctx:
                    p.dma("pool", self.fm("HF" if d == 0 else "HB", 0, 512, c0, n), h[:, :, :n])
                    p.dma("pool", self.fm("AF" if d == 0 else "AB", 0, 512, c0, n), ac[:, :, :n])
                elif d == 0:
                    p.dma("pool", self.fm("HF", 0, 512, c0, n), h[:, :, :n])
                else:
                    hf, g, y = lr_hf[cnt % 2], lr_g[cnt % 2], lr_y[cnt % 2]
                    p.dma("sp", hf[:, :, :n], self.fm("HF", 0, 512, c0, n))
                    p.dma("sp", g[:, :, :n], self.fm("GB", 0, 512, c0, n))
                    p.tt("dve", h[:, :, :n], h[:, :, :n], hf[:, :, :n], ALU.add)
                    p.tt("pool", y[:, :, :n], h[:, :, :n], g[:, :, :n], ALU.mult)
                    p.dma("pool", self.fm("YT", 512, 512, c0, n), y[:, :, :n])
                cnt += 1
            if NSEG > 1:
                p.copy("dve", self.lr_sum[:, d * 8:d * 8 + 4], self.lr_acar)
                p.copy("dve", self.lr_sum[:, d * 8 + 4:d * 8 + 8], self.lr_carry)
        if NSEG > 1:
            cin = V(self.s["cc_l_in"].ap(), p.dres("cc_l_in", 0, 0, 128))
            p.dma("pool", cin, self.lr_sum)
            cout = V(self.s["cc_l_out"].ap(), p.dres("cc_l_out", 0, 0, 128))
            p.allgather(cout, cin, self.groups)
            p.dma("sp", self.lr_sumG, V(self.s["cc_l_out"].ap().rearrange("(r p) k -> p r k", p=128), cout.res))
            for d in range(2):
                hin = self.lr_hin[:, d, :]
                for r in (range(4) if d == 0 else reversed(range(4))):
                    A_r = self.lr_sumG[:, r, d * 8:d * 8 + 4]
                    B_r = self.lr_sumG[:, r, d * 8 + 4:d * 8 + 8]
                    p.tt("dve", self.lr_t4, A_r, hin, ALU.mult)
                    p.tt("dve", self.lr_t4, self.lr_t4, B_r, ALU.add)
                    p.tt("dve", self.lr_t4, self.lr_t4, hin, ALU.subtract)
                    p.stt("dve", hin, self.lr_t4, self.segw[:, 2 + d, r:r + 1], hin, ALU.mult, ALU.add)
            bufs = [[self.Fv(k) for k in (0, 1, 2, 3, 4)], [self.Fv(k) for k in (5, 6, 7, 8, 9)]]
            ys = [self.Hv(1), self.Hv(2)]
            p.barrier()
            for ti_, (c0, n, isctx) in enumerate(lat):
                hf, af, hb, ab, g = bufs[ti_ % 2]
                y = ys[ti_ % 2]
                for nm, t in (("HF", hf), ("AF", af), ("HB", hb), ("AB", ab), ("GB", g)):
                    p.dma("sp", t[:, :, :n], self.fm(nm, 0, 512, c0, n))
                for ch in range(4):
                    p.stt("dve", hf[:, ch, :n], af[:, ch, :n], self.lr_hin[:, 0, ch:ch + 1], hf[:, ch, :n], ALU.mult, ALU.add)
                    p.stt("dve", hb[:, ch, :n], ab[:, ch, :n], self.lr_hin[:, 1, ch:ch + 1], hb[:, ch, :n], ALU.mult, ALU.add)
                p.tt("pool", hf[:, :, :n], hf[:, :, :n], hb[:, :, :n], ALU.add)
                p.tt("pool", y[:, :, :n], hf[:, :, :n], g[:, :, :n], ALU.mult)
                p.dma("pool", self.fm("YT", 512, 512, c0, n), y[:, :, :n])


    def phase3(self, l):
        p = self.p
        p.barrier()
        p3y = [[self.Hv(0), self.Hv(1)], [self.Hv(2), self.Hv(3)]]
        p3x = [[self.Fv(0), self.Fv(1)], [self.Fv(2), self.Fv(3)]]
        cnt = 0
        for (c0, n, isctx) in self.tiles:
            if isctx and l == self.DEPTH - 1:
                continue
            w = 1 if isctx else 0
            y, x = p3y[cnt % 2], p3x[cnt % 2]
            for half in range(2):
                p.dma("sp", y[half][:, :, :n], self.fm("YT", half * 512, 512, c0, n))
                p.dma("sp", x[half][:, :, :n], self.fm("X", half * 512, 512, c0, n))
            for f in range(8):
                bk = self.bank()
                for k in range(8):
                    p.mm(bk[:, :n], self.Wo[:, k, f * 128:(f + 1) * 128], y[k // 4][:, k % 4, :n], start=(k == 0), stop=(k == 7))
                xf = x[f // 4][:, f % 4, :n]
                p.stt("dve", xf, bk[:, :n], self.GT[l][:, f, w:w + 1], xf, ALU.mult, ALU.add)
            for half in range(2):
                p.dma("pool", self.fm("X", half * 512, 512, c0, n), x[half][:, :, :n])
            cnt += 1

    def final_norm(self):
        p = self.p
        p.barrier()
        self.norm_alloc()
        for (c0, n, isctx) in self.tiles:
            if isctx:
                continue
            self.norm_tile(None, c0, n, False)
            o0 = c0 - CTX
            for half in range(2):
                ov = V(self.out[half * 512:(half + 1) * 512, o0:o0 + n].rearrange("(c p) t -> p c t", p=128), p.dres("out", half, o0, n))
                p.dma("pool", ov, self.xt[half][:, :, :n])

    def phase1_odd(self, l):
        p, i = self.p, self.i
        C = ODD_COLS
        p.barrier()
        self.norm_alloc()
        stg = [self.Fv(k) for k in (3, 4, 5, 6)]
        Ct = [self.Fv(7, 0), self.Fv(7, 1)]
        St = [self.Fv(7, 2), self.Fv(7, 3)]
        t1 = [self.Fv(8, 0), self.Fv(8, 1)]
        t2 = [self.Fv(8, 2), self.Fv(8, 3)]
        aq = self.Hv(4)
        ak = self.Hv(5, 0)
        k2 = self.Hv(5, 1)
        av = self.Hv(5, 2).re("p (b c) -> p b c", c=128)
        if not hasattr(self, "kmax"):
            self.kmax = p.sb([128, 2], F32, "kmax")
            self.kred = p.sb([128, 1], F32, "kred")
            self.kmx = p.sb([128, 2], F32, "kmx")
        p.memset("dve", self.kmax, 0.0)
        rr = [0]
        srr = 0
        ti = 0
        for (c0, n, isctx) in self.tiles:
            hT = self.norm_tile(l, c0, n, isctx)
            ct, st_ = Ct[ti % 2], St[ti % 2]
            if not isctx:
                p.dma("sp", ct[:, :n], self.cst(i["ropec"][:, c0 - CTX:c0 - CTX + n]))
                p.dma("sp", st_[:, :n], self.cst(i["ropes"][:, c0 - CTX:c0 - CTX + n]))

            def rope_evac(out, col, colr, scale):
                bk = self.proj_fm(hT, n, col, 128)
                if isctx:
                    p.act(out, bk[:, :n], AF.Copy, scale=scale)
                else:
                    bk2 = self.proj_fm(hT, n, colr, 128)
                    a, b = t1[rr[0] % 2], t2[rr[0] % 2]
                    rr[0] += 1
                    p.stt("dve", a[:, :n], bk[:, :n], scale, ct[:, :n], ALU.mult, ALU.mult)
                    p.stt("dve", b[:, :n], bk2[:, :n], scale, st_[:, :n], ALU.mult, ALU.mult)
                    p.tt("pool", out, a[:, :n], b[:, :n], ALU.add)

            for c in range(4):
                rope_evac(aq[:, c, :n], C["q"][0] + c * 128, C["qr"][0] + c * 128, 0.125)
            p.dma("pool", self.fm("AQ", 0, 512, c0, n), aq[:, :, :n])
            rope_evac(ak[:, :n], C["k"][0], C["kr"][0], 1.0)
            for g in range(2):
                for dup in range(2):
                    p.dma("pool", V(self.s["AK"][g, dup * 64:(dup + 1) * 64, c0:c0 + n], p.dres("AK", g, c0, n)),
                          ak[g * 64:(g + 1) * 64, :n])
            p.tt("pool", k2[:, :n], ak[:, :n], ak[:, :n], ALU.mult)
            for g in range(2):
                bk = self.bank()
                p.mm(bk[:, :n], self.ones_bf[g * 64:(g + 1) * 64, :], k2[g * 64:(g + 1) * 64, :n])
                p.reduce(self.kred, bk[:, :n], ALU.max)
                p.tt("dve", self.kmax[:, g:g + 1], self.kmax[:, g:g + 1], self.kred, ALU.max)
            nb = n // 128
            for b in range(nb):
                bk = self.proj_tm(hT, b, C["v"][0], 128)
                p.copy("act" if b % 2 else "dve", av[:, b, :], bk[:, :128])
            p.dma("pool", self.tm("AV", c0, n, 0, 128), av[:, :nb, :])
            for (nm, dst) in (("gs", "GB"), ("gr", "GA")):
                st = stg[srr % 4]
                srr += 1
                for c in range(4):
                    bk = self.proj_fm(hT, n, C[nm][0] + c * 128, 128)
                    p.act(st[:, c, :n], bk[:, :n], AF.Silu)
                p.dma("pool", self.fm(dst, 0, 512, c0, n), st[:, :, :n])
            for (nm, nmr, dst, scale) in (("rq", "rqr", "QT", 1.0), ("rk", "rkr", "KT", 0.125)):
                st = stg[srr % 4]
                srr += 1
                for c in range(2):
                    rope_evac(st[:, c, :n], C[nm][0] + c * 128, C[nmr][0] + c * 128, scale)
                p.dma("pool", self.fm(dst, 0, 256, c0, n), st[:, 0:2, :n])
            vs = self.sq[0]
            for b in range(nb):
                bk = self.proj_tm(hT, b, C["rv"][0], 512)
                p.copy("act" if b % 2 else "dve", vs[:, b, :], bk[:, :])
            p.dma("pool", self.tm("VT", c0, n, 0, 512), vs[:, :nb, :])
            ti += 1
        p.act(self.kmx, self.kmax, AF.Sqrt)
        p.ts("dve", self.kmx, self.kmx, 1.01, None, ALU.mult)

    def swa(self, l):
        p, i = self.p, self.i
        j = l // 2
        T = self.TALL
        p.barrier()
        if not hasattr(self, "sink"):
            self.sink = p.sb([128, 8], F32, "sink")
        p.dma("sp", self.sink, self.cst(i["sink"][j]))
        Kc = [self.Hv(0, 0), self.Hv(0, 1)]
        Vc = self.Hv(0, 2).re("p (b c) -> p b c", c=128)
        q2 = self.Hv(0, 3)
        for g in range(2):
            p.dma("sp", Kc[g][:, :256], V(self.s["AK"][g, :, 0:256], p.dres("AK", g, 0, 256)))
        p.dma("sp", Vc[:, :2, :], self.tm("AV", 0, 256, 0, 128))
        NSEG = self.NSEG
        if NSEG > 1:
            if not hasattr(self, "sw_vlr"):
                self.sw_vlr = p.sb([128, 2], F32, "sw_vlr")
            fb4 = self.Fp[4].bitcast(BF16)
            self.sw_KL = [V(fb4[:, 0, g * 128:(g + 1) * 128], [Res()]) for g in range(2)]
            self.sw_KR = [V(fb4[:, 0, (2 + g) * 128:(3 + g) * 128], [Res()]) for g in range(2)]
            self.sw_VL = V(fb4[:, 0, 512:640], [Res()])
            self.sw_VR = V(fb4[:, 0, 640:768], [Res()])
            self.sw_mL = V(self.Fp[3][:, 0, 0:128], [Res()])
            self.sw_mR = V(self.Fp[3][:, 1, 0:128], [Res()])
            cres = p.dres("cc_kv_in", 0, 0, 128)
            for g in range(2):
                p.dma("sp", V(self.s["cc_kv_in"][g * 128:(g + 1) * 128, 0:128], cres),
                      V(self.s["AK"][g, :, CTX:CTX + 128], p.dres("AK", g, CTX, 128)))
                p.dma("sp", V(self.s["cc_kv_in"][g * 128:(g + 1) * 128, 128:256], cres),
                      V(self.s["AK"][g, :, T - 128:T], p.dres("AK", g, T - 128, 128)))
            p.dma("sp", V(self.s["cc_kv_in"][0:128, 256:384], cres), V(self.s["AV"][CTX:CTX + 128, :], p.dres("AV", "all", CTX, 128)))
            p.dma("sp", V(self.s["cc_kv_in"][128:256, 256:384], cres), V(self.s["AV"][T - 128:T, :], p.dres("AV", "all", T - 128, 128)))
            cout = V(self.s["cc_kv_out"].ap(), p.dres("cc_kv_out", 0, 0, 128))
            p.allgather(cout, V(self.s["cc_kv_in"].ap(), cres), self.groups)
            GK = V(self.Hp[3][:, 0:4, :].rearrange("p a b -> p (a b)").rearrange("p (r g c) -> p r g c", r=4, g=2), [Res()])
            GV = V(self.Hp[2][:, 0:2, :].rearrange("p a b -> p (a b)").rearrange("p (r f c) -> p r f c", r=4, f=2), [Res()])
            for r in range(4):
                p.dma("sp", GK[:, r, :, :], V(self.s["cc_kv_out"][r * 256:(r + 1) * 256, 0:256].rearrange("(g p) c -> p g c", p=128), cout.res))
                p.dma("sp", GV[:, r, :, :], V(self.s["cc_kv_out"][r * 256:(r + 1) * 256, 256:384].rearrange("(f p) c -> p f c", p=128), cout.res))
            for r in range(4):
                wl = self.segw[:, 0, r:r + 1]
                wr = self.segw[:, 1, r:r + 1]
                sel = [(self.sw_VL, GV[:, r, 1, :], wl), (self.sw_VR, GV[:, r, 0, :], wr)]
                for g in range(2):
                    sel.append((self.sw_KL[g], GK[:, r, g, 128:256], wl))
                    sel.append((self.sw_KR[g], GK[:, r, g, 0:128], wr))
                for (dst, src, w) in sel:
                    if r == 0:
                        p.ts("dve", dst, src, w, None, ALU.mult)
                    else:
                        p.stt("dve", dst, src, w, dst, ALU.mult, ALU.add)
            p.reduce(self.sw_vlr[:, 0:1], self.segw[:, 0, :], ALU.add)
            p.reduce(self.sw_vlr[:, 1:2], self.segw[:, 1, :], ALU.add)
            p.ts("dve", self.sw_mL, self.masks[:, 2, :], self.sw_vlr[:, 0:1], None, ALU.mult)
            p.ts("dve", self.sw_mR, self.masks[:, 3, :], self.sw_vlr[:, 1:2], None, ALU.mult)
            p.barrier()
        Kw = [[V(self.Hp[1 + x][:, 0:2, :].rearrange("p a b -> p (a b)"), [Res()]),
               V(self.Hp[1 + x][:, 2:4, :].rearrange("p a b -> p (a b)"), [Res()])] for x in range(2)]
        Vw = [V(self.Hp[3][:, 2 * x:2 * x + 2, :].rearrange("p a b -> p (a b)").rearrange("p (b c) -> p b c", c=128), [Res()])
              for x in range(2)]
        Q = [self.Hv(4), self.Hv(5)]
        Pt = []
        for k in (8, 9):
            fb = self.Fp[k].bitcast(BF16)
            for pl in range(4):
                for h in range(2):
                    Pt.append(V(fb[:, pl, h * 512:(h + 1) * 512], [Res()]))
        fb7 = self.Fp[7].bitcast(BF16)
        ybf = [V(fb7[0:64, 0, 0:512], [Res()]), V(fb7[0:64, 0, 512:1024], [Res()])]
        MB = [self.Fv(0, 0), self.Fv(0, 1)]
        tmp = [self.Fv(0, 2), self.Fv(0, 3), self.Fv(1, 0)]
        esk = self.Fv(1, 1, rows=64)
        den = self.Fv(1, 2, rows=64)
        o = self.Fv(1, 3, rows=64)
        gate = [self.Fv(2, 0, rows=64), self.Fv(2, 1, rows=64)]
        ti = 0
        it = 0
        pi = 0
        import os
        STG = int(os.environ.get("KSWA_STAGE", "9"))
        rot = [self.banks[k] for k in (2, 3, 4, 5, 6)]
        rri = [0]

        def rb():
            bnk = rot[rri[0] % 5]
            rri[0] += 1
            return bnk
        psO = self.banks[0]
        psD = self.banks[1]
        for (c0, n, isctx) in self.tiles:
            x = ti % 2
            Qx = Q[x]
            p.dma("sp", Qx[:, :, :n], self.fm("AQ", 0, 512, c0, n))
            if not isctx:
                lo = max(CTX, c0 - 128)
                hi = min(T, c0 + n + 128)
                off = lo - (c0 - 128)
                for g in range(2):
                    p.dma("sp", Kw[x][g][:, off:off + hi - lo], V(self.s["AK"][g, :, lo:hi], p.dres("AK", g, lo, hi - lo)))
                p.dma("sp", Vw[x][:, off // 128:off // 128 + (hi - lo) // 128, :], self.tm("AV", lo, hi - lo, 0, 128))
                if NSEG > 1 and c0 == CTX:
                    for g in range(2):
                        p.copy("pool", Kw[x][g][:, 0:128], self.sw_KL[g])
                    p.copy("pool", Vw[x][:, 0, :], self.sw_VL)
                if NSEG > 1 and c0 + n == T:
                    for g in range(2):
                        p.copy("pool", Kw[x][g][:, n + 128:n + 256], self.sw_KR[g])
                    p.copy("pool", Vw[x][:, (n + 128) // 128, :], self.sw_VR)
            for qb in range(n // 128):
                qs = slice(qb * 128, (qb + 1) * 128)
                tpos = c0 + qb * 128
                if STG < 0:
                    continue
                for c in range(4):
                    p.tt("pool", q2[:, c * 128:(c + 1) * 128], Qx[:, c, qs], Qx[:, c, qs], ALU.mult)
                for g in range(2):
                    hl = [(par, hs, 4 * g + 2 * hs + par) for par in range(2) for hs in range(2)]
                    bkq = [rb(), rb()]
                    for (par, hs, hq) in hl:
                        r = slice(par * 64, par * 64 + 64)
                        ch = hq // 2
                        p.mm(bkq[par][:, hs * 128:(hs + 1) * 128], self.ones_bf[r, :], q2[r, ch * 128:(ch + 1) * 128])
                    mb = MB[it % 2]
                    if STG < 1:
                        continue
                    for par in range(2):
                        p.act(mb[:, par * 256:(par + 1) * 256], bkq[par][:, :256], AF.Ln)
                    p.act(mb, mb, AF.Exp, scale=0.5)
                    p.ts("dve", mb, mb, self.kmx[:, g:g + 1], None, ALU.mult)
                    gt = gate[it % 2]
                    for par in range(2):
                        p.dma("sp", gt[:, par * 256:(par + 1) * 256].re("p (h t) -> p h t", h=2),
                              V(self.s["GB"][g * 256:(g + 1) * 256, tpos:tpos + 128].rearrange("(hs par d) t -> par d hs t", par=2, d=64)[par],
                                p.dres("GB", 2 * g, tpos, 128) + p.dres("GB", 2 * g + 1, tpos, 128)))
                    kb = []
                    for b in range(2):
                        kb.append((Kc[g][:, b * 128:(b + 1) * 128], Vc[:, b, g * 64:(g + 1) * 64], None))
                    if not isctx:
                        for rel in (0, -1, 1):
                            kpos = tpos + rel * 128
                            halo = kpos < CTX or kpos >= T
                            if halo and NSEG == 1:
                                continue
                            wc = kpos - (c0 - 128)
                            m = None if rel == 0 else (self.masks[:, 2, :] if rel == -1 else self.masks[:, 3, :])
                            if halo:
                                m = self.sw_mL if rel == -1 else self.sw_mR
                            kb.append((Kw[x][g][:, wc:wc + 128], Vw[x][:, wc // 128, g * 64:(g + 1) * 64], m))
                    if STG < 2:
                        continue
                    for bi, (Kb, Vb, m) in enumerate(kb):
                        psS = [rb(), rb()]
                        for (par, hs, hq) in hl:
                            r = slice(par * 64, par * 64 + 64)
                            p.mm(psS[par][:, hs * 128:(hs + 1) * 128], Kb[r, :], Qx[r, hq // 2, qs])
                        t = tmp[pi % 3]
                        P = Pt[pi % 16]
                        pi += 1
                        for par in range(2):
                            cs_ = slice(par * 256, (par + 1) * 256)
                            p.tt("dve", t[:, cs_], psS[par][:, :256], mb[:, cs_], ALU.subtract)
                        p.act(P, t, AF.Exp)
                        if m is not None:
                            P3 = P.re("p (h t) -> p h t", h=4)
                            p.tt("dve", P3, P3, V(m.ap.unsqueeze(1).to_broadcast([128, 4, 128]), m.res), ALU.mult)
                        last = bi == len(kb) - 1
                        if STG < 3:
                            continue
                        p.mm(psO[0:64, :], Vb, P, start=(bi == 0), stop=last)
                        p.mm(psD[0:64, :], self.ones_bf[:, 0:64], P, start=(bi == 0), stop=last)
                    if STG < 4:
                        continue
                    for cb, (par, hs, hq) in enumerate(hl):
                        p.act(esk[:, cb * 128:(cb + 1) * 128], mb[0:64, cb * 128:(cb + 1) * 128], AF.Exp, scale=-1.0,
                              bias=self.sink[0:64, hq:hq + 1])
                    p.tt("dve", den, psD[0:64, :], esk, ALU.add)
                    p.recip(den, den)
                    p.tt("dve", o, psO[0:64, :], den, ALU.mult)
                    if STG < 5:
                        continue
                    yb = ybf[it % 2]
                    p.tt("pool", yb, o, gt, ALU.mult)
                    for par in range(2):
                        p.dma("pool", V(self.s["YT"][g * 256:(g + 1) * 256, tpos:tpos + 128].rearrange("(hs par d) t -> par d hs t", par=2, d=64)[par],
                                        p.dres("YT", 2 * g, tpos, 128) + p.dres("YT", 2 * g + 1, tpos, 128)),
                              yb[:, par * 256:(par + 1) * 256].re("p (h t) -> p h t", h=2))
                    it += 1
            ti += 1


def _pc(v, c):
    return np.ascontiguousarray(np.asarray(v, np.float32).reshape(c, 128).T)


def make_consts(SEQ):
    s = np.arange(128)[:, None]
    t = np.arange(128)[None, :]
    same = (s // 64) == (t // 64)
    masks = np.stack([(s <= t) & same, (s >= t) & same, s >= t, s <= t]).astype(np.float32)
    col = np.arange(512)
    rmask = np.stack([np.broadcast_to((col % 64 != 0), (128, 512)),
                      np.broadcast_to((col % 64 != 63), (128, 512))]).astype(np.float32)
    tok = np.arange(SEQ)
    row, cl = tok // 64, tok % 64
    inv = (10000.0 ** (-np.arange(16, dtype=np.float32) / 16)).astype(np.float32)
    ang = np.concatenate([row[:, None].astype(np.float32) * inv[None], cl[:, None].astype(np.float32) * inv[None]], axis=-1)
    cos, sin = np.cos(ang).astype(np.float32), np.sin(ang).astype(np.float32)
    C64 = np.concatenate([cos, cos], axis=1).T
    S64 = np.concatenate([-sin, sin], axis=1).T
    ropec = np.ascontiguousarray(np.concatenate([C64, C64], axis=0))
    ropes = np.ascontiguousarray(np.concatenate([S64, S64], axis=0))
    ident = np.eye(128, dtype=np.float32).astype(ml_dtypes.bfloat16)
    return dict(masks=masks, rmask=rmask, ropec=ropec, ropes=ropes, ident=ident)


def rot_cols(w, nheads, dh=64):
    K = w.shape[0]
    w4 = w.reshape(K, nheads, 2, dh // 2)
    return w4[:, :, ::-1, :].reshape(K, nheads * dh)


def prep_inputs(inp, b, SEQ):
    f = lambda a: np.ascontiguousarray(np.asarray(a, np.float32))
    m = {}
    m["xin"] = np.ascontiguousarray(np.concatenate([f(inp["ctx"][b]).T, f(inp["x"][b])[:SEQ].T], axis=1))
    m["cvec"] = np.ascontiguousarray(np.stack([_pc(inp["c"][b], 8), _pc(inp["c_ctx"], 8)], axis=-1))
    m["ada_w"] = f(inp["ada_w"])
    m["ada_b"] = np.stack([_pc(inp["ada_b"][l], 24) for l in range(4)])
    m["norm_g"] = np.stack([_pc(inp["norm_g"][l], 8) for l in range(4)])
    m["final_g"] = _pc(inp["final_g"], 8)
    m["e_w_in"] = f(inp["e_w_in"])
    m["e_w_out"] = f(inp["e_w_out"])
    ow = f(inp["o_w_in"])
    C = ODD_COLS
    rot = [np.concatenate([rot_cols(ow[j][:, C["q"][0]:C["q"][1]], 8), rot_cols(ow[j][:, C["k"][0]:C["k"][1]], 2),
                           rot_cols(ow[j][:, C["rq"][0]:C["rq"][1]], 4), rot_cols(ow[j][:, C["rk"][0]:C["rk"][1]], 4)], axis=1)
           for j in range(2)]
    m["o_w_in"] = np.ascontiguousarray(np.concatenate([ow, np.stack(rot)], axis=2))
    m["o_w_out"] = f(inp["o_w_out"])
    m["gla_up"] = np.ascontiguousarray(np.stack([f(inp["gla_up_fw"]), f(inp["gla_up_bw"])], axis=1))
    m["gla_b"] = np.stack([np.stack([_pc(inp["gla_b_fw"][j], 2), _pc(inp["gla_b_bw"][j], 2)]) for j in range(2)])
    m["gla_ng"] = np.stack([_pc(inp["gla_norm_g"][j], 1) for j in range(2)])
    m["ret_ng"] = np.stack([_pc(inp["ret_norm_g"][j], 1) for j in range(2)])
    cw = f(inp["lru_conv_w"])
    m["conv_w"] = np.ascontiguousarray(cw.reshape(2, 4, 4, 128).transpose(0, 3, 2, 1))
    m["conv_b"] = np.stack([_pc(inp["lru_conv_b"][j], 4) for j in range(2)])
    m["lru_w"] = np.ascontiguousarray(np.stack([f(inp["lru_wa_fw"]), f(inp["lru_wx_fw"]), f(inp["lru_wa_bw"]), f(inp["lru_wx_bw"])], axis=1))
    m["lru_b"] = np.stack([np.stack([_pc(inp[k][j], 4) for k in ("lru_ba_fw", "lru_bx_fw", "lru_ba_bw", "lru_bx_bw")]) for j in range(2)])
    m["lru_lam"] = np.stack([np.stack([_pc(inp["lru_lam_fw"][j], 4), _pc(inp["lru_lam_bw"][j], 4)]) for j in range(2)])
    m["sink"] = np.ascontiguousarray(np.broadcast_to(f(inp["swa_sink"])[:, None, :], (2, 128, 8)))
    rd = np.stack([f(inp["ret_dec_fw"]), f(inp["ret_dec_bw"])], axis=1)
    m["ret_dec"] = np.ascontiguousarray(np.repeat(rd, 64, axis=2).reshape(2, 2, 2, 128).transpose(0, 1, 3, 2))
    m.update(make_consts(SEQ))
    return m


_CACHE = {}


def seg_weights(s, nseg):
    w = np.zeros((4, 4), np.float32)
    if nseg > 1:
        if s > 0:
            w[0, s - 1] = 1.0
        if s < nseg - 1:
            w[1, s + 1] = 1.0
        w[2, :s] = 1.0
        w[3, s + 1:] = 1.0
    return np.ascontiguousarray(np.broadcast_to(w.reshape(1, 16), (128, 16)))


def make_in_maps(inp, SEQ_TOTAL, NSEG):
    maps = []
    L = SEQ_TOTAL // NSEG
    base = [prep_inputs(inp, b, SEQ_TOTAL) for b in range(2)]
    for b in range(2):
        for s in range(NSEG):
            m = dict(base[b])
            if NSEG > 1:
                xin = base[b]["xin"]
                m["xin"] = np.ascontiguousarray(np.concatenate([xin[:, :CTX], xin[:, CTX + s * L:CTX + (s + 1) * L]], axis=1))
                m["ropec"] = np.ascontiguousarray(base[b]["ropec"][:, s * L:(s + 1) * L])
                m["ropes"] = np.ascontiguousarray(base[b]["ropes"][:, s * L:(s + 1) * L])
            m["segw"] = seg_weights(s, NSEG)
            maps.append(m)
    return maps


def run(inp, SEQ_TOTAL, DEPTH, dbg=(), NSEG=4):
    key = (SEQ_TOTAL, DEPTH, tuple(dbg), NSEG)
    if key not in _CACHE:
        _CACHE[key] = Builder(SEQ_TOTAL // NSEG, DEPTH, dbg, NSEG).build()
    nc = _CACHE[key]
    in_maps = make_in_maps(inp, SEQ_TOTAL, NSEG)
    res = run_bass_kernel_spmd(nc, in_maps, core_ids=list(range(2 * NSEG)), trace=(_os.environ.get("KTRACE", "0") == "1"))
    return res


def gather_out(res, NSEG):
    outs = []
    for b in range(2):
        outs.append(np.concatenate([res.results[b * NSEG + s]["out"].T for s in range(NSEG)], axis=0))
    return np.stack(outs).astype(np.float32)


def kernel(**inputs):
    SEQ = inputs["x"].shape[1]
    res = run(inputs, SEQ, 4, NSEG=4)
    return gather_out(res, 4)
```

```python
import numpy as np
import ml_dtypes
from contextlib import ExitStack
import concourse.bass as bass
import concourse.mybir as mybir
from concourse.bass_utils import run_bass_kernel_spmd

F32 = mybir.dt.float32
BF16 = mybir.dt.bfloat16
AF = mybir.ActivationFunctionType
ALU = mybir.AluOpType
AX = mybir.AxisListType

D = 1024
CTX = 256
EPS = 1e-6
import os as _os
SAME_ENG_SYNC = _os.environ.get("KSYNC", "1") == "1"


class Res:
    __slots__ = ("w", "r")

    def __init__(self):
        self.w = None
        self.r = {}


class V:
    __slots__ = ("ap", "res")

    def __init__(self, ap, res):
        self.ap = ap
        self.res = res

    def __getitem__(self, key):
        return V(self.ap[key], self.res)

    def re(self, s, **kw):
        return V(self.ap.rearrange(s, **kw), self.res)

    def bc(self, shape):
        return V(self.ap.to_broadcast(list(shape)), self.res)


class Prog:
    def __init__(self, nc, es):
        self.nc = nc
        self.es = es
        self.ops = {e: [] for e in ("pe", "act", "dve", "pool", "sp")}
        self.sem = {}
        self.cnt = {}
        for e in ("pe", "act", "dve", "pool"):
            self.sem[e] = es.enter_context(nc.semaphore("s_" + e))
            self.cnt[e] = 0
        self.dsem = {}
        self.dcnt = {}
        self.drr = {}
        for q, n in (("sp", 16), ("pool", 8)):
            self.dsem[q] = [es.enter_context(nc.semaphore("d_%s%d" % (q, i))) for i in range(n)]
            self.dcnt[q] = [0] * n
            self.drr[q] = 0
        self.waited = {e: {} for e in self.ops}
        self.dram_res = {}
        self.n_sb = 0

    def sb(self, shape, dt=F32, name=None):
        self.n_sb += 1
        t = self.es.enter_context(self.nc.sbuf_tensor("sb_" + (name or ("t%d" % self.n_sb)), list(shape), dt))
        return V(t[:] if len(shape) == 2 else t.ap(), [Res()])

    def ps(self, shape, dt=F32, name=None):
        self.n_sb += 1
        t = self.es.enter_context(self.nc.psum_tensor("ps_" + (name or ("t%d" % self.n_sb)), list(shape), dt))
        return V(t.ap(), [Res()])

    def dres(self, name, rowkey, c0, n):
        out = []
        for b in range(c0 // 128, (c0 + n - 1) // 128 + 1):
            k = (name, rowkey, b)
            if k not in self.dram_res:
                self.dram_res[k] = Res()
            out.append(self.dram_res[k])
        return out

    def emit(self, eng, fn, reads, writes, dma=False):
        toks = []
        for v in reads:
            for r in v.res:
                if r.w is not None:
                    toks.append(r.w)
        for v in writes:
            for r in v.res:
                if r.w is not None:
                    toks.append(r.w)
                toks.extend(r.r.values())
        waits = []
        wd = self.waited[eng]
        for (s, val, te) in toks:
            if te == eng and not dma and (eng == "pe" or not SAME_ENG_SYNC):
                continue
            if wd.get(id(s), 0) >= val:
                continue
            wd[id(s)] = val
            waits.append((s, val))
        if dma:
            pool = self.dsem[eng]
            j = self.drr[eng]
            self.drr[eng] = (j + 1) % len(pool)
            s = pool[j]
            k = self.dcnt[eng][j]
            if k > 0 and wd.get(id(s), 0) < 16 * k:
                wd[id(s)] = 16 * k
                waits.append((s, 16 * k))
            self.dcnt[eng][j] = k + 1
            tok = (s, 16 * (k + 1), eng + "q")
            inc = 16
        else:
            self.cnt[eng] += 1
            tok = (self.sem[eng], self.cnt[eng], eng)
            inc = 1
        self.ops[eng].append((waits, fn, tok[0], inc))
        for v in reads:
            for r in v.res:
                r.r[id(tok[0])] = tok
        for v in writes:
            for r in v.res:
                r.w = tok
                r.r = {}
        return tok

    @staticmethod
    def _a(x):
        return x.ap if isinstance(x, V) else x

    @staticmethod
    def _vs(*xs):
        return [x for x in xs if isinstance(x, V)]

    def mm(self, out, lhsT, rhs, start=True, stop=True):
        self.emit("pe", lambda e: e.matmul(out.ap, lhsT.ap, rhs.ap, start=start, stop=stop),
                  [lhsT, rhs], [out])

    def tr(self, out, in_, ident):
        self.emit("pe", lambda e: e.transpose(out.ap, in_.ap, ident.ap), [in_, ident], [out])

    def act(self, out, in_, func, bias=None, scale=None, accum=None, eng="act"):
        kw = {}
        if bias is not None:
            kw["bias"] = self._a(bias)
        if scale is not None:
            kw["scale"] = self._a(scale)
        if accum is not None:
            kw["accum_out"] = accum.ap
        self.emit("act", lambda e: e.activation(out.ap, in_.ap, func, **kw),
                  self._vs(in_, bias, scale), self._vs(out, accum))

    def ts(self, eng, out, in0, s1, s2, op0, op1=None):
        a1, a2 = self._a(s1), self._a(s2)
        if op1 is None:
            f = lambda e: e.tensor_scalar(out.ap, in0.ap, a1, None, op0)
        else:
            f = lambda e: e.tensor_scalar(out.ap, in0.ap, a1, a2, op0, op1)
        self.emit(eng, f, self._vs(in0, s1, s2), [out])

    def tt(self, eng, out, in0, in1, op):
        self.emit(eng, lambda e: e.tensor_tensor(out.ap, in0.ap, in1.ap, op), [in0, in1], [out])

    def stt(self, eng, out, in0, scalar, in1, op0, op1):
        sc = self._a(scalar)
        self.emit(eng, lambda e: e.scalar_tensor_tensor(out.ap, in0.ap, sc, in1.ap, op0, op1),
                  self._vs(in0, scalar, in1), [out])

    def copy(self, eng, out, in_):
        if eng == "act":
            self.emit("act", lambda e: e.copy(out.ap, in_.ap), [in_], [out])
        else:
            self.emit(eng, lambda e: e.tensor_copy(out.ap, in_.ap), [in_], [out])

    def scan(self, out, d0, d1, init, op0=ALU.mult, op1=ALU.add):
        ia = self._a(init)
        self.emit("dve", lambda e: e.tensor_tensor_scan(out.ap, d0.ap, d1.ap, ia, op0, op1),
                  self._vs(d0, d1, init), [out])

    def memset(self, eng, out, val):
        self.emit(eng, lambda e: e.memset(out.ap, val), [], [out])

    def recip(self, out, in_):
        self.emit("dve", lambda e: e.reciprocal(out.ap, in_.ap), [in_], [out])

    def reduce(self, out, in_, op, axis=AX.X):
        self.emit("dve", lambda e: e.tensor_reduce(out.ap, in_.ap, axis, op), [in_], [out])

    def dma(self, q, out, in_, slow=False):
        if slow:
            self.emit(q, lambda e: e.dma_start(out=out.ap, in_=in_.ap, allow_slow_non_contiguous=True), [in_], [out], dma=True)
        else:
            self.emit(q, lambda e: e.dma_start(out=out.ap, in_=in_.ap), [in_], [out], dma=True)

    def allgather(self, out, in_, groups):
        self.n_cc = getattr(self, "n_cc", 0) + 1
        sem = self.es.enter_context(self.nc.semaphore("cc%d" % self.n_cc))
        eng = "pool"
        toks = []
        for r in in_.res:
            if r.w is not None:
                toks.append(r.w)
        for r in out.res:
            if r.w is not None:
                toks.append(r.w)
            toks.extend(r.r.values())
        waits = []
        wd = self.waited[eng]
        for (sm, val, te) in toks:
            if wd.get(id(sm), 0) >= val:
                continue
            wd[id(sm)] = val
            waits.append((sm, val))
        oa, ia = out.ap, in_.ap
        fn = lambda e: e.collective_compute("AllGather", ALU.bypass, replica_groups=groups, ins=[ia], outs=[oa])
        tok = (sem, 1, "ccq")
        self.ops[eng].append((waits, fn, sem, None))
        self.cc_sems = getattr(self, "cc_sems", []) + [sem]
        for r in in_.res:
            r.r[id(sem)] = tok
        for r in out.res:
            r.w = tok
            r.r = {}

    def barrier(self):
        allw = []
        for q in self.dsem:
            for sm, k in zip(self.dsem[q], self.dcnt[q]):
                if k > 0:
                    allw.append((sm, 16 * k))
        for e in ("pe", "act", "dve", "pool"):
            if self.cnt[e] > 0:
                allw.append((self.sem[e], self.cnt[e]))
        for sm in getattr(self, "cc_sems", []):
            allw.append((sm, 1))
        for e in self.ops:
            wd = self.waited[e]
            waits = []
            for (sm, val) in allw:
                if sm is self.sem.get(e):
                    continue
                if wd.get(id(sm), 0) >= val:
                    continue
                wd[id(sm)] = val
                waits.append((sm, val))
            if waits:
                self.ops[e].append((waits, None, None, 0))

    def finish(self):
        nc = self.nc
        fin = []
        for q in self.dsem:
            for s, k in zip(self.dsem[q], self.dcnt[q]):
                if k > 0:
                    fin.append((s, 16 * k))
        for e in ("pe", "act", "dve", "pool"):
            if self.cnt[e] > 0:
                fin.append((self.sem[e], self.cnt[e]))
        ops = self.ops

        def run(e, lst, extra=()):
            for waits, fn, s, inc in lst:
                for (ws, wv) in waits:
                    e.wait_ge(ws, wv)
                if fn is not None:
                    if inc is None:
                        fn(e).then_inc(s)
                    else:
                        fn(e).then_inc(s, inc)
            for (ws, wv) in extra:
                e.wait_ge(ws, wv)

        with nc.Block() as block:
            @block.sync
            def _(e):
                run(e, ops["sp"], fin)

            @block.tensor
            def _(e):
                run(e, ops["pe"])

            @block.scalar
            def _(e):
                run(e, ops["act"])

            @block.vector
            def _(e):
                run(e, ops["dve"])

            @block.gpsimd
            def _(e):
                run(e, ops["pool"])


EVEN_COLS = dict(q=(0, 256), k=(256, 512), v=(512, 1024), gg=(1024, 1536), lrf=(1536, 1552),
                 lrb=(1552, 1568), xr=(1568, 2080), gl=(2080, 2592))
ODD_COLS = dict(q=(0, 512), k=(512, 640), v=(640, 768), gs=(768, 1280), rq=(1280, 1536),
                rk=(1536, 1792), rv=(1792, 2304), gr=(2304, 2816),
                qr=(2816, 3328), kr=(3328, 3456), rqr=(3456, 3712), rkr=(3712, 3968))


class Builder:
    def __init__(self, SEQ, DEPTH, dbg=(), NSEG=1):
        self.NSEG = NSEG
        self.SEQ = SEQ
        self.DEPTH = DEPTH
        self.TALL = CTX + SEQ
        self.dbg = dbg
        self.nc = bass.Bass("TRN2", target_bir_lowering=False)
        self.es = ExitStack()
        self.tiles = [(0, CTX, True)] + [(CTX + i * 512, 512, False) for i in range(SEQ // 512)]

    def din(self, name, shape, dt=F32):
        return self.nc.dram_tensor(name, list(shape), dt, kind="ExternalInput")

    def dscr(self, name, shape, dt=F32):
        kind = "ExternalOutput" if name in self.dbg else "Internal"
        return self.nc.dram_tensor(name, list(shape), dt, kind=kind)

    def declare(self):
        T = self.TALL
        S = self.SEQ
        i = {}
        i["xin"] = self.din("xin", [D, T])
        i["cvec"] = self.din("cvec", [128, 8, 2])
        i["ada_w"] = self.din("ada_w", [4, D, 3 * D])
        i["ada_b"] = self.din("ada_b", [4, 128, 24])
        i["norm_g"] = self.din("norm_g", [4, 128, 8])
        i["final_g"] = self.din("final_g", [128, 8])
        i["e_w_in"] = self.din("e_w_in", [2, D, 2592])
        i["e_w_out"] = self.din("e_w_out", [2, D, D])
        i["o_w_in"] = self.din("o_w_in", [2, D, 3968])
        i["o_w_out"] = self.din("o_w_out", [2, D, D])
        i["gla_up"] = self.din("gla_up", [2, 2, 16, 256])
        i["gla_b"] = self.din("gla_b", [2, 2, 128, 2])
        i["gla_ng"] = self.din("gla_ng", [2, 128, 1])
        i["ret_ng"] = self.din("ret_ng", [2, 128, 1])
        i["conv_w"] = self.din("conv_w", [2, 128, 4, 4])
        i["conv_b"] = self.din("conv_b", [2, 128, 4])
        i["lru_w"] = self.din("lru_w", [2, 4, 4, 128, 128])
        i["lru_b"] = self.din("lru_b", [2, 4, 128, 4])
        i["lru_lam"] = self.din("lru_lam", [2, 2, 128, 4])
        i["sink"] = self.din("sink", [2, 128, 8])
        i["ret_dec"] = self.din("ret_dec", [2, 2, 128, 2])
        i["ident"] = self.din("ident", [128, 128], BF16)
        i["masks"] = self.din("masks", [4, 128, 128])
        i["rmask"] = self.din("rmask", [2, 128, 512])
        i["ropec"] = self.din("ropec", [128, S])
        i["ropes"] = self.din("ropes", [128, S])
        i["segw"] = self.din("segw", [128, 16])
        self.i = i
        s = {}
        s["cc_h_in"] = self.dscr("cc_h_in", [128, 12])
        s["cc_h_out"] = self.dscr("cc_h_out", [512, 12])
        s["cc_kv_in"] = self.dscr("cc_kv_in", [256, 384], BF16)
        s["cc_kv_out"] = self.dscr("cc_kv_out", [1024, 384], BF16)
        s["cc_s_in"] = self.dscr("cc_s_in", [512, 129])
        s["cc_s_out"] = self.dscr("cc_s_out", [2048, 129])
        s["cc_l_in"] = self.dscr("cc_l_in", [128, 16])
        s["cc_l_out"] = self.dscr("cc_l_out", [512, 16])
        s["AF"] = self.dscr("AF", [512, T])
        s["HB"] = self.dscr("HB", [512, T])
        s["AB"] = self.dscr("AB", [512, T])
        s["X"] = self.dscr("X", [D, T])
        s["YT"] = self.dscr("YT", [D, T], BF16)
        s["QT"] = self.dscr("QT", [256, T])
        s["KT"] = self.dscr("KT", [256, T])
        s["VT"] = self.dscr("VT", [T, 512], BF16)
        s["GA"] = self.dscr("GA", [512, T])
        s["LR"] = self.dscr("LR", [2, 16, T])
        s["XR"] = self.dscr("XR", [512, T])
        s["GB"] = self.dscr("GB", [512, T])
        s["OF"] = self.dscr("OF", [512, T])
        s["HF"] = self.dscr("HF", [512, T])
        s["AQ"] = self.dscr("AQ", [512, T], BF16)
        s["AK"] = self.dscr("AK", [2, 128, T], BF16)
        s["AV"] = self.dscr("AV", [T, 128], BF16)
        self.s = s
        self.out = self.nc.dram_tensor("out", [D, S], F32, kind="ExternalOutput")

    def dv(self, handle, name, rowkey, ap, c0, n):
        return V(ap, self.p.dres(name, rowkey, c0, n))

    def fm(self, name, r0, nrows, c0, n, src=None):
        h = (src or self.s)[name]
        if nrows <= 128:
            ap = h[r0:r0 + nrows, c0:c0 + n]
            keys = [r0 // 128]
        else:
            ap = h[r0:r0 + nrows, c0:c0 + n].rearrange("(c p) t -> p c t", p=128)
            keys = list(range(r0 // 128, (r0 + nrows) // 128))
        res = []
        for k in keys:
            res += self.p.dres(name, k, c0, n)
        return V(ap, res)

    def tm(self, name, c0, n, col0, ncol):
        h = self.s[name]
        if n <= 128:
            ap = h[c0:c0 + n, col0:col0 + ncol]
        else:
            ap = h[c0:c0 + n, col0:col0 + ncol].rearrange("(b p) v -> p b v", p=128)
        return V(ap, self.p.dres(name, "all", c0, n))

    def cst(self, ap):
        return V(ap, [])

    def build(self):
        nc, es = self.nc, self.es
        self.declare()
        p = self.p = Prog(nc, es)
        i = self.i
        self.Fp = [es.enter_context(nc.sbuf_tensor("poolF%d" % k, [128, 4, 512], F32)) for k in range(10)]
        self.Hp = [es.enter_context(nc.sbuf_tensor("poolH%d" % k, [128, 4, 512], BF16)) for k in range(6)]
        self.ones_bf = p.sb([128, 128], BF16, "ones_bf")
        p.memset("pool", self.ones_bf, 1.0)
        self.ones_f = p.sb([128, 512], F32, "ones_f")
        p.memset("pool", self.ones_f, 1.0)
        self.ident = p.sb([128, 128], BF16, "ident")
        p.dma("sp", self.ident, self.cst(i["ident"].ap()))
        self.masks = p.sb([128, 4, 128], F32, "masks")
        p.dma("sp", self.masks, self.cst(i["masks"].ap().rearrange("m p t -> p m t")))
        self.rmask = p.sb([128, 2, 512], F32, "rmask")
        p.dma("sp", self.rmask, self.cst(i["rmask"].ap().rearrange("m p t -> p m t")))
        self.segw = p.sb([128, 4, 4], F32, "segw")
        p.dma("sp", self.segw, self.cst(i["segw"].ap().rearrange("p (k r) -> p k r", r=4)))
        self.groups = [[0, 1, 2, 3], [4, 5, 6, 7]]
        self.Wb = p.sb([128, 8, 3968], BF16, "Wb")
        self.Wo = p.sb([128, 8, 1024], BF16, "Wo")
        self.wstg = [p.sb([128, 512], F32, "wstg%d" % k) for k in range(2)]
        self.wrr = 0
        self.banks = [p.ps([128, 512], F32, "bank%d" % b) for b in range(7)]
        self.bank_bf = p.ps([128, 1024], BF16, "bankbf")
        self.brr = 0
        self.prologue()
        p.barrier()
        stg = [self.Fv(k) for k in range(4)]
        cnt = 0
        for (c0, n, isctx) in self.tiles:
            for half in range(2):
                st = stg[cnt % 4]
                cnt += 1
                p.dma("sp", st[:, :, :n], self.cst(i["xin"][half * 512:(half + 1) * 512, c0:c0 + n].rearrange("(c p) t -> p c t", p=128)))
                p.dma("pool", self.fm("X", half * 512, 512, c0, n), st[:, :, :n])
        self.wq = []
        for l in range(self.DEPTH):
            j = l // 2
            if l == 0:
                self.load_w_in(i["e_w_in"][j], 2592)
            else:
                self.pump(10 ** 6)
            if l % 2 == 0:
                self.queue_w_out(i["e_w_out"][j])
                self.phase1_even(l)
                if l + 1 < self.DEPTH:
                    self.queue_w_in(i["o_w_in"][j], 3968)
                self.linattn(l, "gla")
                self.lru(l)
            else:
                self.queue_w_out(i["o_w_out"][j])
                self.phase1_odd(l)
                if l + 1 < self.DEPTH:
                    self.queue_w_in(i["e_w_in"][j + 1], 2592)
                import os
                if "ret" not in os.environ.get("KSKIP", ""):
                    self.linattn(l, "ret")
                if "swa" not in os.environ.get("KSKIP", ""):
                    self.swa(l)
            self.phase3(l)
        self.final_norm()
        p.finish()
        return nc

    def Fv(self, k, pl=None, npl=1, rows=128):
        t = self.Fp[k]
        if pl is None:
            return V(t.ap(), [Res()])
        if npl == 1:
            return V(t[0:rows, pl, :], [Res()])
        return V(t[0:rows, pl:pl + npl, :], [Res()])

    def Hv(self, k, pl=None, npl=1):
        t = self.Hp[k]
        if pl is None:
            return V(t.ap(), [Res()])
        if npl == 1:
            return V(t[:, pl, :], [Res()])
        return V(t[:, pl:pl + npl, :], [Res()])

    def bank(self):
        b = self.banks[self.brr]
        self.brr = (self.brr + 1) % len(self.banks)
        return b

    def prologue(self):
        p, i = self.p, self.i
        cv = p.sb([128, 8, 2], F32, "cvec")
        p.dma("sp", cv, self.cst(i["cvec"].ap()))
        sc = p.sb([128, 8, 2], F32, "silu_c")
        p.act(sc, cv, AF.Silu)
        self.GS, self.SH, self.GT = [], [], []
        wb = [[self.Fv(0), self.Fv(1)], [self.Fv(2), self.Fv(3)]]
        cnt = 0
        for l in range(self.DEPTH):
            ab = p.sb([128, 24], F32, "ada_b%d" % l)
            p.dma("sp", ab, self.cst(i["ada_b"][l]))
            ng = p.sb([128, 8], F32, "norm_g%d" % l)
            p.dma("sp", ng, self.cst(i["norm_g"][l]))
            mod = p.sb([128, 24, 2], F32, "mod%d" % l)
            bk = self.bank()
            for jg in range(6):
                w2 = wb[cnt % 2]
                cnt += 1
                for half in range(2):
                    p.dma("sp", w2[half], self.cst(i["ada_w"][l, half * 512:(half + 1) * 512, jg * 512:(jg + 1) * 512].rearrange("(k p) c -> p k c", p=128)))
                for j4 in range(4):
                    jj = jg * 4 + j4
                    for k in range(8):
                        p.mm(bk[:, jj * 2:jj * 2 + 2], w2[k // 4][:, k % 4, j4 * 128:(j4 + 1) * 128], sc[:, k, :],
                             start=(k == 0), stop=(k == 7))
            p.tt("dve", mod, bk[:, 0:48].re("p (j c) -> p j c", c=2), ab.re("p (j o) -> p j o", o=1).bc([128, 24, 2]), ALU.add)
            gs = p.sb([128, 8, 2], F32, "gs%d" % l)
            p.ts("dve", gs, mod[:, 8:16, :], 1.0, None, ALU.add)
            p.tt("dve", gs, gs, ng.re("p (j o) -> p j o", o=1).bc([128, 8, 2]), ALU.mult)
            self.GS.append(gs)
            self.SH.append(mod[:, 0:8, :])
            self.GT.append(mod[:, 16:24, :])
        self.fg = p.sb([128, 8], F32, "final_g")
        p.dma("sp", self.fg, self.cst(i["final_g"].ap()))

    def load_w_in(self, wap, ncol):
        p = self.p
        for k in range(8):
            for c0 in range(0, ncol, 512):
                w = min(512, ncol - c0)
                st = self.wstg[self.wrr % 2]
                self.wrr += 1
                p.dma("sp", st[:, :w], self.cst(wap[k * 128:(k + 1) * 128, c0:c0 + w]))
                p.copy("pool", self.Wb[:, k, c0:c0 + w], st[:, :w])

    def queue_w_in(self, wap, ncol):
        for k in range(8):
            for c0 in range(0, ncol, 512):
                w = min(512, ncol - c0)
                self.wq.append((k, c0, w, wap))

    def queue_w_out(self, wap):
        for k in range(8):
            for c0 in range(0, 1024, 512):
                self.wq.append((k, c0, 512, wap, "o"))

    def pump(self, n):
        p = self.p
        while n > 0 and self.wq:
            item = self.wq.pop(0)
            k, c0, w, wap = item[:4]
            dst = self.Wo if len(item) > 4 else self.Wb
            st = self.wstg[self.wrr % 2]
            self.wrr += 1
            p.dma("sp", st[:, :w], self.cst(wap[k * 128:(k + 1) * 128, c0:c0 + w]))
            p.copy("pool", dst[:, k, c0:c0 + w], st[:, :w])
            n -= 1

    def load_w_out(self, wap):
        p = self.p
        for k in range(8):
            for c0 in range(0, 1024, 512):
                st = self.wstg[self.wrr % 2]
                self.wrr += 1
                p.dma("sp", st[:, :512], self.cst(wap[k * 128:(k + 1) * 128, c0:c0 + 512]))
                p.copy("pool", self.Wo[:, k, c0:c0 + 512], st[:, :512])

    def norm_alloc(self):
        self.xt = [self.Fv(0), self.Fv(1)]
        self.sq = [self.Hv(0), self.Hv(1)]
        self.hT = [self.Hv(2), self.Hv(3)]
        self.rstd = self.Fv(2, 0)

    def norm_tile(self, l, c0, n, isctx, src="X"):
        p = self.p
        w = 1 if isctx else 0
        xt, sq, hT, rstd = self.xt, self.sq, self.hT, self.rstd
        bk = self.bank()
        for half in range(2):
            p.dma("sp", xt[half][:, :, :n], self.fm(src, half * 512, 512, c0, n))
            p.act(sq[half][:, :, :n], xt[half][:, :, :n], AF.Square)
        for k in range(8):
            p.mm(bk[:, :n], self.ones_bf, sq[k // 4][:, k % 4, :n], start=(k == 0), stop=(k == 7))
        p.ts("dve", rstd[:, :n], bk[:, :n], 1.0 / D, EPS, ALU.mult, ALU.add)
        p.act(rstd[:, :n], rstd[:, :n], AF.Ln)
        p.act(rstd[:, :n], rstd[:, :n], AF.Exp, scale=-0.5)
        for k in range(8):
            eng = "dve" if k % 2 == 0 else "pool"
            xk = xt[k // 4][:, k % 4, :n]
            p.tt(eng, xk, xk, rstd[:, :n], ALU.mult)
            if l is None:
                p.act(xk, xk, AF.Identity, scale=self.fg[:, k:k + 1])
            else:
                p.act(hT[k // 4][:, k % 4, :n], xk, AF.Identity, bias=self.SH[l][:, k, w:w + 1],
                      scale=self.GS[l][:, k, w:w + 1])
        return lambda k: hT[k // 4][:, k % 4, :]

    def proj_fm(self, hT, n, col0, m):
        p = self.p
        bk = self.bank()
        for k in range(8):
            p.mm(bk[:m, :n], self.Wb[:, k, col0:col0 + m], hT(k)[:, :n], start=(k == 0), stop=(k == 7))
        return bk

    def proj_tm(self, hT, blk, col0, m):
        p = self.p
        bk = self.bank()
        for k in range(8):
            p.mm(bk[:, :m], hT(k)[:, blk * 128:(blk + 1) * 128], self.Wb[:, k, col0:col0 + m],
                 start=(k == 0), stop=(k == 7))
        return bk

    def phase1_even(self, l):
        p = self.p
        C = EVEN_COLS
        p.barrier()
        self.norm_alloc()
        stg = [self.Fv(k) for k in (3, 4, 5, 6)]
        vstg = [self.Hv(4), self.Hv(5)]
        lrstg = [self.Fv(7, k, rows=16) for k in range(4)]
        srr = 0
        ti = 0
        for (c0, n, isctx) in self.tiles:
            hT = self.norm_tile(l, c0, n, isctx)
            self.pump(2)
            for (nm, dst, scale) in (("q", "QT", 0.125), ("k", "KT", None)):
                st = stg[srr % 4]
                srr += 1
                for c in range(2):
                    bk = self.proj_fm(hT, n, C[nm][0] + c * 128, 128)
                    if scale is not None:
                        p.act(st[:, c, :n], bk[:, :n], AF.Copy, scale=scale)
                    else:
                        p.copy("dve", st[:, c, :n], bk[:, :n])
                p.dma("pool", self.fm(dst, 0, 256, c0, n), st[:, 0:2, :n])
            for (nm, dst, silu) in (("gg", "GA", True), ("xr", "XR", False), ("gl", "GB", True)):
                st = stg[srr % 4]
                srr += 1
                for c in range(4):
                    bk = self.proj_fm(hT, n, C[nm][0] + c * 128, 128)
                    if silu:
                        p.act(st[:, c, :n], bk[:, :n], AF.Silu)
                    else:
                        p.copy("dve", st[:, c, :n], bk[:, :n])
                p.dma("pool", self.fm(dst, 0, 512, c0, n), st[:, :, :n])
            for d, nm in enumerate(("lrf", "lrb")):
                bk = self.proj_fm(hT, n, C[nm][0], 16)
                ls = lrstg[(2 * ti + d) % 4]
                p.copy("dve", ls[:, :n], bk[:16, :n])
                p.dma("pool", V(self.s["LR"][d, :, c0:c0 + n], p.dres("LR", d, c0, n)), ls[:, :n])
            vs = vstg[ti % 2]
            nb = n // 128
            for b in range(nb):
                bk = self.proj_tm(hT, b, C["v"][0], 512)
                p.copy("act" if b % 2 else "dve", vs[:, b, :], bk[:, :])
            p.dma("pool", self.tm("VT", c0, n, 0, 512), vs[:, :nb, :])
            ti += 1

    def linattn(self, l, kind):
        p, i = self.p, self.i
        j = l // 2
        NSEG = self.NSEG
        p.barrier()
        if not hasattr(self, "la_S"):
            self.la_dec = [p.sb([128, 8], F32, "la_dec%d" % k) for k in range(2)]
            self.la_S = [p.sb([128, 128], F32, "la_S%d" % k) for k in range(2)]
            self.la_Sb = [[p.sb([128, 128], BF16, "la_Sb%d_%d" % (k, m)) for m in range(2)] for k in range(2)]
            self.la_att = [p.sb([128, 128], BF16, "la_att%d" % k) for k in range(4)]
            self.la_up = p.sb([16, 2, 256], F32, "la_up")
            self.la_nb = p.sb([128, 2, 2], F32, "la_nb")
            self.la_ng = p.sb([128, 1], F32, "la_ng")
            self.la_Lr = p.sb([128, 2, 2], F32, "la_Lr")
            self.la_asum = p.sb([128, 4], F32, "la_asum")
            self.la_ared = p.sb([128, 1], F32, "la_ared")
            self.la_tmpS = p.sb([128, 128], F32, "la_tmpS")
        la_q = [self.Fv(0, 0), self.Fv(0, 1)]
        la_k = [self.Fv(0, 2), self.Fv(0, 3)]
        la_L = [self.Fv(1, 0), self.Fv(1, 1)]
        la_b = [self.Fv(1, 2), self.Fv(1, 3)]
        la_e = [self.Fv(2, 0), self.Fv(2, 1)]
        la_lr = [self.Fv(2, 2, rows=16), self.Fv(2, 3, rows=16)]
        la_v = [self.Hv(0), self.Hv(1)]
        la_qi = [self.Hv(2, 0), self.Hv(2, 1)]
        la_ki = [self.Hv(2, 2), self.Hv(2, 3)]
        la_ks = [self.Hv(3, 0), self.Hv(3, 1)]
        la_kst = [self.Hv(3, 2).re("p (b c) -> p b c", c=128), self.Hv(3, 3).re("p (b c) -> p b c", c=128)]
        la_o = [self.Fv(3, 0, 2), self.Fv(3, 2, 2)]
        la_of = [self.Fv(4, 0, 2), self.Fv(4, 2, 2)]
        la_g = [self.Fv(5, 0, 2), self.Fv(5, 2, 2)]
        la_rs = self.Fv(6, 0)
        la_o2 = self.Hv(4, 0)
        la_y = [self.Hv(5, 0, 2), self.Hv(5, 2, 2)]
        la_sst = self.Fv(7, 0, 2).re("p a b -> p (a b)")
        la_G = [V(self.Fp[8 + d].ap().rearrange("p a b -> p (a b)")[:, 0:1032].rearrange("p (r q c) -> p r q c", r=4, q=2), [Res()])
                for d in range(2)]
        if kind == "gla":
            p.dma("sp", self.la_up, self.cst(i["gla_up"][j].rearrange("d r c -> r d c")))
            p.dma("sp", self.la_nb, self.cst(i["gla_b"][j].rearrange("d p c -> p d c")))
            p.ts("dve", self.la_nb, self.la_nb, -1.0, None, ALU.mult)
            p.dma("sp", self.la_ng, self.cst(i["gla_ng"][j]))
            yrow0 = 0
        else:
            p.dma("sp", self.la_ng, self.cst(i["ret_ng"][j]))
            p.dma("sp", self.la_Lr, self.cst(i["ret_dec"][j].rearrange("d p c -> p d c")))
            p.act(self.la_Lr, self.la_Lr, AF.Exp, scale=-1.0)
            p.act(self.la_Lr, self.la_Lr, AF.Ln, bias=1.0)
            p.ts("dve", self.la_Lr, self.la_Lr, 16.0, None, ALU.mult)
            yrow0 = 512
        cnt = [0]
        sbi = [0, 0]
        rotA = [self.banks[k] for k in (4, 5, 6)]
        rai = [0]

        def rbA():
            bnk = rotA[rai[0] % 3]
            rai[0] += 1
            return bnk

        def do_tile(d, c0, n, state_only):
            mask = self.masks[:, d, :]
            nblk = n // 128
            nch = n // 64
            v = la_v[cnt[0] % 2]
            p.dma("sp", v[:, :nblk, :], self.tm("VT", c0, n, 0, 512))
            self.pump(3)
            for hp in range(2):
                x = hp
                qt, kt, L, bc, e = la_q[x], la_k[x], la_L[x], la_b[x], la_e[x]
                qi, ki, ks, kst, dec = la_qi[x], la_ki[x], la_ks[x], la_kst[x], self.la_dec[x]
                if not state_only:
                    p.dma("sp", qt[:, :n], self.fm("QT", hp * 128, 128, c0, n))
                p.dma("sp", kt[:, :n], self.fm("KT", hp * 128, 128, c0, n))
                if kind == "gla":
                    lr = la_lr[x]
                    p.dma("sp", lr[:, :n], V(self.s["LR"][d, :, c0:c0 + n], p.dres("LR", d, c0, n)))
                    bk = rbA()
                    p.mm(bk[:, :n], self.la_up[:, d, hp * 128:(hp + 1) * 128], lr[:, :n])
                    p.act(L[:, :n], bk[:, :n], AF.Exp, scale=-1.0, bias=self.la_nb[:, d, hp:hp + 1])
                    p.act(L[:, :n], L[:, :n], AF.Ln, bias=1.0)
                else:
                    p.ts("dve", L[:, :n], self.ones_f[:, :n], self.la_Lr[:, d, hp:hp + 1], None, ALU.mult)
                if d == 0:
                    p.scan(bc[:, :n], self.rmask[:, 0, :n], L[:, :n], 0.0)
                else:
                    p.scan(bc[:, :n][:, ::-1], self.rmask[:, 1, :n][:, ::-1], L[:, :n][:, ::-1], 0.0)
                b3 = bc[:, :n].re("p (c i) -> p c i", i=64)
                bend = b3[:, :, 63:64] if d == 0 else b3[:, :, 0:1]
                if not state_only:
                    p.act(e[:, :n], bc[:, :n], AF.Exp, scale=-1.0 / 16)
                    p.tt("dve", qi[:, :n], qt[:, :n], e[:, :n], ALU.mult)
                    p.act(e[:, :n], bc[:, :n], AF.Exp, scale=1.0 / 16)
                    p.tt("pool", ki[:, :n], kt[:, :n], e[:, :n], ALU.mult)
                else:
                    p.reduce(self.la_ared, bend.re("p c o -> p (c o)"), ALU.add)
                    aq = self.la_asum[:, 2 * d + hp:2 * d + hp + 1]
                    p.tt("dve", aq, aq, self.la_ared, ALU.add)
                p.act(dec[:, :nch], bend.re("p c o -> p (c o)"), AF.Exp, scale=-1.0 / 16)
                p.tt("dve", e[:, :n].re("p (c i) -> p c i", i=64), bend.bc([128, nch, 64]), b3, ALU.subtract)
                p.act(e[:, :n], e[:, :n], AF.Exp, scale=-1.0 / 16)
                p.tt("pool", ks[:, :n], kt[:, :n], e[:, :n], ALU.mult)
                for b in range(nblk):
                    p.tr(self.bank_bf[:, (hp * 4 + b) * 128:(hp * 4 + b + 1) * 128], ks[:, b * 128:(b + 1) * 128], self.ident)
                p.copy("act", kst[:, :nblk, :], self.bank_bf[:, hp * 512:hp * 512 + nblk * 128].re("p (b c) -> p b c", c=128))
            blks = list(range(nblk)) if d == 0 else list(reversed(range(nblk)))
            for b in blks:
                cs = slice(b * 128, (b + 1) * 128)
                chunks = [0, 1] if d == 0 else [1, 0]
                psO = [[None, None], [None, None]]
                if not state_only:
                    for hp in range(2):
                        qi, ki = la_qi[hp], la_ki[hp]
                        psAs = []
                        for h in range(2):
                            r = slice(h * 64, (h + 1) * 64)
                            psA = rbA()
                            psAs.append(psA)
                            p.mm(psA[:, :128], ki[r, cs], qi[r, cs])
                        for h in range(2):
                            am = self.la_att[(2 * hp + h) % 4]
                            p.tt("dve", am, psAs[h][:, :128], mask, ALU.mult)
                        for h in range(2):
                            head = 2 * hp + h
                            vc = slice(head * 128, (head + 1) * 128)
                            am = self.la_att[(2 * hp + h) % 4]
                            po = self.banks[2 * hp + h]
                            psO[hp][h] = po
                            p.mm(po[:, :128], v[:, b, vc], am, start=True, stop=False)
                for ci, cc in enumerate(chunks):
                    tk = slice(b * 128 + cc * 64, b * 128 + cc * 64 + 64)
                    pk = slice(cc * 64, cc * 64 + 64)
                    for hp in range(2):
                        qi, kst, dec = la_qi[hp], la_kst[hp], self.la_dec[hp]
                        S = self.la_S[hp]
                        Sb = self.la_Sb[hp][sbi[hp] % 2]
                        psD = rbA()
                        for h in range(2):
                            r = slice(h * 64, (h + 1) * 64)
                            head = 2 * hp + h
                            vc = slice(head * 128, (head + 1) * 128)
                            if not state_only:
                                p.mm(psO[hp][h][:, cc * 64:cc * 64 + 64], Sb[r, :], qi[r, tk], start=False, stop=(ci == 1))
                            p.mm(psD[r, :128], kst[pk, b, r], v[pk, b, vc])
                        chn = b * 2 + cc
                        p.stt("dve", S, S, dec[:, chn:chn + 1], psD[:, :128], ALU.mult, ALU.add)
                        if not state_only:
                            sbi[hp] += 1
                            p.copy("act", self.la_Sb[hp][sbi[hp] % 2], S)
                if not state_only:
                    for hp in range(2):
                        for h in range(2):
                            p.copy("act" if h else "dve", la_o[hp][:, h, cs], psO[hp][h][:, :128])
            if not state_only:
                for hp in range(2):
                    x = hp
                    ot = la_o[x]
                    if d == 0:
                        p.dma("pool", self.fm("OF", hp * 256, 256, c0, n), ot[:, :, :n])
                    else:
                        oft, gt, yt = la_of[x], la_g[x], la_y[x]
                        p.dma("sp", oft[:, :, :n], self.fm("OF", hp * 256, 256, c0, n))
                        p.dma("sp", gt[:, :, :n], self.fm("GA", hp * 256, 256, c0, n))
                        for h in range(2):
                            o = ot[:, h, :n]
                            p.tt("dve", o, o, oft[:, h, :n], ALU.add)
                            p.act(la_o2[:, :n], o, AF.Square)
                            bk = rbA()
                            p.mm(bk[:, :n], self.ones_bf, la_o2[:, :n])
                            p.ts("dve", la_rs[:, :n], bk[:, :n], 1.0 / 128, EPS, ALU.mult, ALU.add)
                            p.act(la_rs[:, :n], la_rs[:, :n], AF.Ln)
                            p.act(la_rs[:, :n], la_rs[:, :n], AF.Exp, scale=-0.5)
                            p.stt("dve", o, o, self.la_ng[:, 0:1], la_rs[:, :n], ALU.mult, ALU.mult)
                            p.tt("pool", yt[:, h, :n], o, gt[:, h, :n], ALU.mult)
                        p.dma("pool", self.fm("YT", yrow0 + hp * 256, 256, c0, n), yt[:, :, :n])
            cnt[0] += 1

        def zero_state(bf=True):
            for hp in range(2):
                p.memset("dve", self.la_S[hp], 0.0)
                if bf:
                    p.memset("pool", self.la_Sb[hp][sbi[hp] % 2], 0.0)

        lat = self.tiles[1:]
        ctxt = self.tiles[0]
        if NSEG > 1:
            p.memset("dve", self.la_asum, 0.0)
            for d in range(2):
                zero_state(bf=False)
                for (c0, n, isctx) in (lat if d == 0 else list(reversed(lat))):
                    do_tile(d, c0, n, True)
                for hp in range(2):
                    q = 2 * d + hp
                    p.copy("dve", la_sst[:, q * 129:q * 129 + 128], self.la_S[hp])
                    p.act(la_sst[:, q * 129 + 128:q * 129 + 129], self.la_asum[:, q:q + 1], AF.Exp, scale=-1.0 / 16)
            cin = V(self.s["cc_s_in"].ap().rearrange("(q p) c -> p q c", p=128), p.dres("cc_s_in", 0, 0, 128))
            p.dma("pool", cin, la_sst[:, 0:516].re("p (q c) -> p q c", c=129))
            cout = V(self.s["cc_s_out"].ap(), p.dres("cc_s_out", 0, 0, 128))
            p.allgather(cout, V(self.s["cc_s_in"].ap(), cin.res), self.groups)
            for d in range(2):
                for r in range(4):
                    src = self.s["cc_s_out"][r * 512 + d * 256:r * 512 + d * 256 + 256, :].rearrange("(q p) c -> p q c", p=128)
                    p.dma("sp", la_G[d][:, r, :, :], V(src, cout.res))
        for d in range(2):
            zero_state()
            do_tile(d, ctxt[0], ctxt[1], False)
            if NSEG > 1:
                for hp in range(2):
                    S = self.la_S[hp]
                    for r in (range(4) if d == 0 else reversed(range(4))):
                        A_r = la_G[d][:, r, hp, 128:129]
                        B_r = la_G[d][:, r, hp, 0:128]
                        p.stt("dve", self.la_tmpS, S, A_r, B_r, ALU.mult, ALU.add)
                        p.tt("dve", self.la_tmpS, self.la_tmpS, S, ALU.subtract)
                        p.stt("dve", S, self.la_tmpS, self.segw[:, 2 + d, r:r + 1], S, ALU.mult, ALU.add)
                    sbi[hp] += 1
                    p.copy("act", self.la_Sb[hp][sbi[hp] % 2], S)
            for (c0, n, isctx) in (lat if d == 0 else list(reversed(lat))):
                do_tile(d, c0, n, False)


    def lru(self, l):
        p, i = self.p, self.i
        j = l // 2
        NSEG = self.NSEG
        p.barrier()
        if not hasattr(self, "lw"):
            self.lw = p.sb([128, 4, 4, 128], BF16, "lru_w")
            self.lb = p.sb([128, 4, 4], F32, "lru_b")
            self.lc8 = p.sb([128, 2, 4], F32, "lru_c8")
            self.lc16 = p.sb([128, 2, 4], F32, "lru_c16")
            self.lcw = p.sb([128, 4, 4], F32, "lru_cw")
            self.lcb = p.sb([128, 4], F32, "lru_cb")
            self.lr_carry = p.sb([128, 4], F32, "lru_carry")
            self.lr_acar = p.sb([128, 4], F32, "lru_acar")
            self.lr_hal = p.sb([128, 4, 3], F32, "lru_hal")
            self.lr_halG = p.sb([128, 4, 12], F32, "lru_halG")
            self.lr_halL = p.sb([128, 4, 2], F32, "lru_halL")
            self.lr_halR = p.sb([128, 4, 1], F32, "lru_halR")
            self.lr_sum = p.sb([128, 16], F32, "lru_sum")
            self.lr_sumG = p.sb([128, 4, 16], F32, "lru_sumG")
            self.lr_hin = p.sb([128, 2, 4], F32, "lru_hin")
            self.lr_t4 = p.sb([128, 4], F32, "lru_t4")
        lxh = [self.Fv(0, 0, 2).re("p a b -> p (a b)"), self.Fv(0, 2, 2).re("p a b -> p (a b)")]
        lxc = [self.Fv(1, 0), self.Fv(1, 1)]
        lr_r = [self.Fv(1, 2), self.Fv(1, 3)]
        lr_i = [self.Fv(2, 0), self.Fv(2, 1)]
        lr_a = [self.Fv(2, 2), self.Fv(2, 3)]
        lr_s = [self.Fv(3, 0), self.Fv(3, 1)]
        lxb = [self.Hv(0, 0), self.Hv(0, 1)]
        lr_h = [self.Fv(4), self.Fv(5)]
        lr_hf = [self.Fv(6), self.Fv(7)]
        lr_g = [self.Fv(8), self.Fv(9)]
        lr_y = [self.Hv(1), self.Hv(2)]
        for kd in range(4):
            for n_ in range(4):
                st = self.wstg[self.wrr % 2]
                self.wrr += 1
                p.dma("sp", st[:, :128], self.cst(i["lru_w"][j, kd, n_]))
                p.copy("pool", self.lw[:, kd, n_, :], st[:, :128])
        p.dma("sp", self.lb, self.cst(i["lru_b"][j].rearrange("k p c -> p k c")))
        p.dma("sp", self.lcw, self.cst(i["conv_w"][j]))
        p.dma("sp", self.lcb, self.cst(i["conv_b"][j]))
        p.dma("sp", self.lc8, self.cst(i["lru_lam"][j].rearrange("d p c -> p d c")))
        p.act(self.lc8, self.lc8, AF.Exp, scale=-1.0)
        p.act(self.lc8, self.lc8, AF.Ln, bias=1.0)
        p.ts("dve", self.lc16, self.lc8, -16.0, None, ALU.mult)
        p.ts("dve", self.lc8, self.lc8, -8.0, None, ALU.mult)
        T = self.TALL
        if NSEG > 1:
            hal = self.lr_hal
            p.dma("sp", hal[:, :, 0:1], self.fm("XR", 0, 512, CTX, 1), slow=True)
            p.dma("sp", hal[:, :, 1:3], self.fm("XR", 0, 512, T - 2, 2), slow=True)
            cin = V(self.s["cc_h_in"].ap(), p.dres("cc_h_in", 0, 0, 128))
            p.dma("pool", cin, hal.re("p c k -> p (c k)"))
            cout = V(self.s["cc_h_out"].ap(), p.dres("cc_h_out", 0, 0, 128))
            p.allgather(cout, cin, self.groups)
            p.dma("sp", self.lr_halG, V(self.s["cc_h_out"].ap().rearrange("(r p) k -> p r k", p=128), cout.res))
            G4 = self.lr_halG.re("p r (c k) -> p r c k", k=3)
            for r in range(4):
                wl = self.segw[:, 0, r:r + 1]
                wr = self.segw[:, 1, r:r + 1]
                if r == 0:
                    p.ts("dve", self.lr_halL, G4[:, r, :, 1:3], wl, None, ALU.mult)
                    p.ts("dve", self.lr_halR, G4[:, r, :, 0:1], wr, None, ALU.mult)
                else:
                    p.stt("dve", self.lr_halL, G4[:, r, :, 1:3], wl, self.lr_halL, ALU.mult, ALU.add)
                    p.stt("dve", self.lr_halR, G4[:, r, :, 0:1], wr, self.lr_halR, ALU.mult, ALU.add)
        cnt = 0
        lat = self.tiles[1:]
        for d in range(2):
            order = [self.tiles[0]] + (list(lat) if d == 0 else list(reversed(lat)))
            first = True
            for ti_, (c0, n, isctx) in enumerate(order):
                seg0, seg1 = (0, CTX) if isctx else (CTX, T)
                h = lr_h[cnt % 2]
                ac = lr_hf[cnt % 2] if (NSEG > 1 and not isctx) else None
                seg_first = (NSEG > 1 and ti_ == 1)
                for ch in range(4):
                    x = ch % 2
                    xh, xc, xb = lxh[x], lxc[x], lxb[x]
                    lo = max(seg0, c0 - 2)
                    hi = min(seg1, c0 + n + 1)
                    if lo > c0 - 2 or hi < c0 + n + 1:
                        p.memset("pool", xh[:, 0:516], 0.0)
                    p.dma("sp", xh[:, lo - (c0 - 2):hi - (c0 - 2)], self.fm("XR", ch * 128, 128, lo, hi - lo))
                    if NSEG > 1 and not isctx:
                        if lo > c0 - 2:
                            p.copy("dve", xh[:, 0:2], self.lr_halL[:, ch, :])
                        if hi < c0 + n + 1:
                            p.copy("dve", xh[:, n + 2:n + 3], self.lr_halR[:, ch, :])
                    p.ts("dve", xc[:, :n], xh[:, 0:n], self.lcw[:, ch, 0:1], self.lcb[:, ch:ch + 1], ALU.mult, ALU.add)
                    for k in range(1, 4):
                        p.stt("dve", xc[:, :n], xh[:, k:k + n], self.lcw[:, ch, k:k + 1], xc[:, :n], ALU.mult, ALU.add)
                    p.copy("act", xb[:, :n], xc[:, :n])
                    r, ig, a, s = lr_r[x], lr_i[x], lr_a[x], lr_s[x]
                    bk = self.bank()
                    p.mm(bk[:, :n], self.lw[:, 2 * d, ch, :], xb[:, :n])
                    p.act(r[:, :n], bk[:, :n], AF.Sigmoid, bias=self.lb[:, 2 * d, ch:ch + 1])
                    bk2 = self.bank()
                    p.mm(bk2[:, :n], self.lw[:, 2 * d + 1, ch, :], xb[:, :n])
                    p.act(ig[:, :n], bk2[:, :n], AF.Sigmoid, bias=self.lb[:, 2 * d + 1, ch:ch + 1])
                    p.act(a[:, :n], r[:, :n], AF.Exp, scale=self.lc8[:, d, ch:ch + 1])
                    p.act(s[:, :n], r[:, :n], AF.Exp, scale=self.lc16[:, d, ch:ch + 1])
                    p.act(s[:, :n], s[:, :n], AF.Sqrt, scale=-1.0, bias=1.0)
                    p.tt("pool", ig[:, :n], ig[:, :n], xc[:, :n], ALU.mult)
                    p.tt("dve", s[:, :n], s[:, :n], ig[:, :n], ALU.mult)
                    init = 0.0 if (first or seg_first) else self.lr_carry[:, ch:ch + 1]
                    ainit = 1.0 if seg_first else self.lr_acar[:, ch:ch + 1]
                    if NSEG > 1 and seg_first:
                        p.copy("dve", self.lr_hin[:, d, ch:ch + 1], self.lr_carry[:, ch:ch + 1])
                    if d == 0:
                        p.scan(h[:, ch, :n], a[:, :n], s[:, :n], init)
                        p.copy("dve", self.lr_carry[:, ch:ch + 1], h[:, ch, n - 1:n])
                        if ac is not None:
                            p.scan(ac[:, ch, :n], a[:, :n], self.ones_f[:, :n], ainit, ALU.mult, ALU.mult)
                            p.copy("dve", self.lr_acar[:, ch:ch + 1], ac[:, ch, n - 1:n])
                    else:
                        p.scan(h[:, ch, :n][:, ::-1], a[:, :n][:, ::-1], s[:, :n][:, ::-1], init)
                        p.copy("dve", self.lr_carry[:, ch:ch + 1], h[:, ch, 0:1])
                        if ac is not None:
                            p.scan(ac[:, ch, :n][:, ::-1], a[:, :n][:, ::-1], self.ones_f[:, :n], ainit, ALU.mult, ALU.mult)
                            p.copy("dve", self.lr_acar[:, ch:ch + 1], ac[:, ch, 0:1])
                first = False
                if NSEG > 1 and not isctx:
                    p.dma("pool", self.fm("HF" if d == 0 else "HB", 0, 512, c0, n), h[:, :, :n])
                    p.dma("pool", self.fm("AF" if d == 0 else "AB", 0, 512, c0, n), ac[:, :, :n])
                elif d == 0:
                    p.dma("pool", self.fm("HF", 0, 512, c0, n), h[:, :, :n])
                else:
                    hf, g, y = lr_hf[cnt % 2], lr_g[cnt % 2], lr_y[cnt % 2]
                    p.dma("sp", hf[:, :, :n], self.fm("HF", 0, 512, c0, n))
                    p.dma("sp", g[:, :, :n], self.fm("GB", 0, 512, c0, n))
                    p.tt("dve", h[:, :, :n], h[:, :, :n], hf[:, :, :n], ALU.add)
                    p.tt("pool", y[:, :, :n], h[:, :, :n], g[:, :, :n], ALU.mult)
                    p.dma("pool", self.fm("YT", 512, 512, c0, n), y[:, :, :n])
                cnt += 1
            if NSEG > 1:
                p.copy("dve", self.lr_sum[:, d * 8:d * 8 + 4], self.lr_acar)
                p.copy("dve", self.lr_sum[:, d * 8 + 4:d * 8 + 8], self.lr_carry)
        if NSEG > 1:
            cin = V(self.s["cc_l_in"].ap(), p.dres("cc_l_in", 0, 0, 128))
            p.dma("pool", cin, self.lr_sum)
            cout = V(self.s["cc_l_out"].ap(), p.dres("cc_l_out", 0, 0, 128))
            p.allgather(cout, cin, self.groups)
            p.dma("sp", self.lr_sumG, V(self.s["cc_l_out"].ap().rearrange("(r p) k -> p r k", p=128), cout.res))
            for d in range(2):
                hin = self.lr_hin[:, d, :]
                for r in (range(4) if d == 0 else reversed(range(4))):
                    A_r = self.lr_sumG[:, r, d * 8:d * 8 + 4]
                    B_r = self.lr_sumG[:, r, d * 8 + 4:d * 8 + 8]
                    p.tt("dve", self.lr_t4, A_r, hin, ALU.mult)
                    p.tt("dve", self.lr_t4, self.lr_t4, B_r, ALU.add)
                    p.tt("dve", self.lr_t4, self.lr_t4, hin, ALU.subtract)
                    p.stt("dve", hin, self.lr_t4, self.segw[:, 2 + d, r:r + 1], hin, ALU.mult, ALU.add)
            bufs = [[self.Fv(k) for k in (0, 1, 2, 3, 4)], [self.Fv(k) for k in (5, 6, 7, 8, 9)]]
            ys = [self.Hv(1), self.Hv(2)]
            p.barrier()
            for ti_, (c0, n, isctx) in enumerate(lat):
                hf, af, hb, ab, g = bufs[ti_ % 2]
                y = ys[ti_ % 2]
                for nm, t in (("HF", hf), ("AF", af), ("HB", hb), ("AB", ab), ("GB", g)):
                    p.dma("sp", t[:, :, :n], self.fm(nm, 0, 512, c0, n))
                for ch in range(4):
                    p.stt("dve", hf[:, ch, :n], af[:, ch, :n], self.lr_hin[:, 0, ch:ch + 1], hf[:, ch, :n], ALU.mult, ALU.add)
                    p.stt("dve", hb[:, ch, :n], ab[:, ch, :n], self.lr_hin[:, 1, ch:ch + 1], hb[:, ch, :n], ALU.mult, ALU.add)
                p.tt("pool", hf[:, :, :n], hf[:, :, :n], hb[:, :, :n], ALU.add)
                p.tt("pool", y[:, :, :n], hf[:, :, :n], g[:, :, :n], ALU.mult)
                p.dma("pool", self.fm("YT", 512, 512, c0, n), y[:, :, :n])


    def phase3(self, l):
        p = self.p
        self.pump(10 ** 6)
        p.barrier()
        p3y = [[self.Hv(0), self.Hv(1)], [self.Hv(2), self.Hv(3)]]
        p3x = [[self.Fv(0), self.Fv(1)], [self.Fv(2), self.Fv(3)]]
        cnt = 0
        for (c0, n, isctx) in self.tiles:
            if isctx and l == self.DEPTH - 1:
                continue
            w = 1 if isctx else 0
            y, x = p3y[cnt % 2], p3x[cnt % 2]
            for half in range(2):
                p.dma("sp", y[half][:, :, :n], self.fm("YT", half * 512, 512, c0, n))
                p.dma("sp", x[half][:, :, :n], self.fm("X", half * 512, 512, c0, n))
            for f in range(8):
                bk = self.bank()
                for k in range(8):
                    p.mm(bk[:, :n], self.Wo[:, k, f * 128:(f + 1) * 128], y[k // 4][:, k % 4, :n], start=(k == 0), stop=(k == 7))
                xf = x[f // 4][:, f % 4, :n]
                p.stt("dve", xf, bk[:, :n], self.GT[l][:, f, w:w + 1], xf, ALU.mult, ALU.add)
            for half in range(2):
                p.dma("pool", self.fm("X", half * 512, 512, c0, n), x[half][:, :, :n])
            cnt += 1

    def final_norm(self):
        p = self.p
        p.barrier()
        self.norm_alloc()
        for (c0, n, isctx) in self.tiles:
            if isctx:
                continue
            self.norm_tile(None, c0, n, False)
            o0 = c0 - CTX
            for half in range(2):
                ov = V(self.out[half * 512:(half + 1) * 512, o0:o0 + n].rearrange("(c p) t -> p c t", p=128), p.dres("out", half, o0, n))
                p.dma("pool", ov, self.xt[half][:, :, :n])

    def phase1_odd(self, l):
        p, i = self.p, self.i
        C = ODD_COLS
        p.barrier()
        self.norm_alloc()
        stg = [self.Fv(k) for k in (3, 4, 5, 6)]
        Ct = [self.Fv(7, 0), self.Fv(7, 1)]
        St = [self.Fv(7, 2), self.Fv(7, 3)]
        t1 = [self.Fv(8, 0), self.Fv(8, 1)]
        t2 = [self.Fv(8, 2), self.Fv(8, 3)]
        aq = self.Hv(4)
        ak = self.Hv(5, 0)
        k2 = self.Hv(5, 1)
        av = self.Hv(5, 2).re("p (b c) -> p b c", c=128)
        if not hasattr(self, "kmax"):
            self.kmax = p.sb([128, 2], F32, "kmax")
            self.kred = p.sb([128, 1], F32, "kred")
            self.kmx = p.sb([128, 2], F32, "kmx")
        p.memset("dve", self.kmax, 0.0)
        rr = [0]
        srr = 0
        ti = 0
        for (c0, n, isctx) in self.tiles:
            hT = self.norm_tile(l, c0, n, isctx)
            self.pump(2)
            ct, st_ = Ct[ti % 2], St[ti % 2]
            if not isctx:
                p.dma("sp", ct[:, :n], self.cst(i["ropec"][:, c0 - CTX:c0 - CTX + n]))
                p.dma("sp", st_[:, :n], self.cst(i["ropes"][:, c0 - CTX:c0 - CTX + n]))

            def rope_evac(out, col, colr, scale):
                bk = self.proj_fm(hT, n, col, 128)
                if isctx:
                    p.act(out, bk[:, :n], AF.Copy, scale=scale)
                else:
                    bk2 = self.proj_fm(hT, n, colr, 128)
                    a, b = t1[rr[0] % 2], t2[rr[0] % 2]
                    rr[0] += 1
                    p.stt("dve", a[:, :n], bk[:, :n], scale, ct[:, :n], ALU.mult, ALU.mult)
                    p.stt("dve", b[:, :n], bk2[:, :n], scale, st_[:, :n], ALU.mult, ALU.mult)
                    p.tt("pool", out, a[:, :n], b[:, :n], ALU.add)

            for c in range(4):
                rope_evac(aq[:, c, :n], C["q"][0] + c * 128, C["qr"][0] + c * 128, 0.125)
            p.dma("pool", self.fm("AQ", 0, 512, c0, n), aq[:, :, :n])
            rope_evac(ak[:, :n], C["k"][0], C["kr"][0], 1.0)
            for g in range(2):
                for dup in range(2):
                    p.dma("pool", V(self.s["AK"][g, dup * 64:(dup + 1) * 64, c0:c0 + n], p.dres("AK", g, c0, n)),
                          ak[g * 64:(g + 1) * 64, :n])
            p.tt("pool", k2[:, :n], ak[:, :n], ak[:, :n], ALU.mult)
            for g in range(2):
                bk = self.bank()
                p.mm(bk[:, :n], self.ones_bf[g * 64:(g + 1) * 64, :], k2[g * 64:(g + 1) * 64, :n])
                p.reduce(self.kred, bk[:, :n], ALU.max)
                p.tt("dve", self.kmax[:, g:g + 1], self.kmax[:, g:g + 1], self.kred, ALU.max)
            nb = n // 128
            for b in range(nb):
                bk = self.proj_tm(hT, b, C["v"][0], 128)
                p.copy("act" if b % 2 else "dve", av[:, b, :], bk[:, :128])
            p.dma("pool", self.tm("AV", c0, n, 0, 128), av[:, :nb, :])
            for (nm, dst) in (("gs", "GB"), ("gr", "GA")):
                st = stg[srr % 4]
                srr += 1
                for c in range(4):
                    bk = self.proj_fm(hT, n, C[nm][0] + c * 128, 128)
                    p.act(st[:, c, :n], bk[:, :n], AF.Silu)
                p.dma("pool", self.fm(dst, 0, 512, c0, n), st[:, :, :n])
            for (nm, nmr, dst, scale) in (("rq", "rqr", "QT", 1.0), ("rk", "rkr", "KT", 0.125)):
                st = stg[srr % 4]
                srr += 1
                for c in range(2):
                    rope_evac(st[:, c, :n], C[nm][0] + c * 128, C[nmr][0] + c * 128, scale)
                p.dma("pool", self.fm(dst, 0, 256, c0, n), st[:, 0:2, :n])
            vs = self.sq[0]
            for b in range(nb):
                bk = self.proj_tm(hT, b, C["rv"][0], 512)
                p.copy("act" if b % 2 else "dve", vs[:, b, :], bk[:, :])
            p.dma("pool", self.tm("VT", c0, n, 0, 512), vs[:, :nb, :])
            ti += 1
        p.act(self.kmx, self.kmax, AF.Sqrt)
        p.ts("dve", self.kmx, self.kmx, 1.01, None, ALU.mult)

    def swa(self, l):
        p, i = self.p, self.i
        j = l // 2
        T = self.TALL
        p.barrier()
        if not hasattr(self, "sink"):
            self.sink = p.sb([128, 8], F32, "sink")
        p.dma("sp", self.sink, self.cst(i["sink"][j]))
        Kc = [self.Hv(0, 0), self.Hv(0, 1)]
        Vc = self.Hv(0, 2).re("p (b c) -> p b c", c=128)
        q2 = self.Hv(0, 3)
        for g in range(2):
            p.dma("sp", Kc[g][:, :256], V(self.s["AK"][g, :, 0:256], p.dres("AK", g, 0, 256)))
        p.dma("sp", Vc[:, :2, :], self.tm("AV", 0, 256, 0, 128))
        NSEG = self.NSEG
        if NSEG > 1:
            if not hasattr(self, "sw_vlr"):
                self.sw_vlr = p.sb([128, 2], F32, "sw_vlr")
            fb4 = self.Fp[4].bitcast(BF16)
            self.sw_KL = [V(fb4[:, 0, g * 128:(g + 1) * 128], [Res()]) for g in range(2)]
            self.sw_KR = [V(fb4[:, 0, (2 + g) * 128:(3 + g) * 128], [Res()]) for g in range(2)]
            self.sw_VL = V(fb4[:, 0, 512:640], [Res()])
            self.sw_VR = V(fb4[:, 0, 640:768], [Res()])
            self.sw_mL = V(self.Fp[3][:, 0, 0:128], [Res()])
            self.sw_mR = V(self.Fp[3][:, 1, 0:128], [Res()])
            cres = p.dres("cc_kv_in", 0, 0, 128)
            for g in range(2):
                p.dma("sp", V(self.s["cc_kv_in"][g * 128:(g + 1) * 128, 0:128], cres),
                      V(self.s["AK"][g, :, CTX:CTX + 128], p.dres("AK", g, CTX, 128)))
                p.dma("sp", V(self.s["cc_kv_in"][g * 128:(g + 1) * 128, 128:256], cres),
                      V(self.s["AK"][g, :, T - 128:T], p.dres("AK", g, T - 128, 128)))
            p.dma("sp", V(self.s["cc_kv_in"][0:128, 256:384], cres), V(self.s["AV"][CTX:CTX + 128, :], p.dres("AV", "all", CTX, 128)))
            p.dma("sp", V(self.s["cc_kv_in"][128:256, 256:384], cres), V(self.s["AV"][T - 128:T, :], p.dres("AV", "all", T - 128, 128)))
            cout = V(self.s["cc_kv_out"].ap(), p.dres("cc_kv_out", 0, 0, 128))
            p.allgather(cout, V(self.s["cc_kv_in"].ap(), cres), self.groups)
            GK = V(self.Hp[3][:, 0:4, :].rearrange("p a b -> p (a b)").rearrange("p (r g c) -> p r g c", r=4, g=2), [Res()])
            GV = V(self.Hp[2][:, 0:2, :].rearrange("p a b -> p (a b)").rearrange("p (r f c) -> p r f c", r=4, f=2), [Res()])
            for r in range(4):
                p.dma("sp", GK[:, r, :, :], V(self.s["cc_kv_out"][r * 256:(r + 1) * 256, 0:256].rearrange("(g p) c -> p g c", p=128), cout.res))
                p.dma("sp", GV[:, r, :, :], V(self.s["cc_kv_out"][r * 256:(r + 1) * 256, 256:384].rearrange("(f p) c -> p f c", p=128), cout.res))
            for r in range(4):
                wl = self.segw[:, 0, r:r + 1]
                wr = self.segw[:, 1, r:r + 1]
                sel = [(self.sw_VL, GV[:, r, 1, :], wl), (self.sw_VR, GV[:, r, 0, :], wr)]
                for g in range(2):
                    sel.append((self.sw_KL[g], GK[:, r, g, 128:256], wl))
                    sel.append((self.sw_KR[g], GK[:, r, g, 0:128], wr))
                for (dst, src, w) in sel:
                    if r == 0:
                        p.ts("dve", dst, src, w, None, ALU.mult)
                    else:
                        p.stt("dve", dst, src, w, dst, ALU.mult, ALU.add)
            p.reduce(self.sw_vlr[:, 0:1], self.segw[:, 0, :], ALU.add)
            p.reduce(self.sw_vlr[:, 1:2], self.segw[:, 1, :], ALU.add)
            p.ts("dve", self.sw_mL, self.masks[:, 2, :], self.sw_vlr[:, 0:1], None, ALU.mult)
            p.ts("dve", self.sw_mR, self.masks[:, 3, :], self.sw_vlr[:, 1:2], None, ALU.mult)
            p.barrier()
        Kw = [[V(self.Hp[1 + x][:, 0:2, :].rearrange("p a b -> p (a b)"), [Res()]),
               V(self.Hp[1 + x][:, 2:4, :].rearrange("p a b -> p (a b)"), [Res()])] for x in range(2)]
        Vw = [V(self.Hp[3][:, 2 * x:2 * x + 2, :].rearrange("p a b -> p (a b)").rearrange("p (b c) -> p b c", c=128), [Res()])
              for x in range(2)]
        Q = [self.Hv(4), self.Hv(5)]
        Pt = []
        for k in (8, 9):
            fb = self.Fp[k].bitcast(BF16)
            for pl in range(4):
                for h in range(2):
                    Pt.append(V(fb[:, pl, h * 512:(h + 1) * 512], [Res()]))
        fb7 = self.Fp[7].bitcast(BF16)
        ybf = [V(fb7[0:64, 0, 0:512], [Res()]), V(fb7[0:64, 0, 512:1024], [Res()])]
        MB = [self.Fv(0, 0), self.Fv(0, 1)]
        tmp = [self.Fv(0, 2), self.Fv(0, 3), self.Fv(1, 0)]
        esk = self.Fv(1, 1, rows=64)
        den = self.Fv(1, 2, rows=64)
        o = self.Fv(1, 3, rows=64)
        gate = [self.Fv(2, 0, rows=64), self.Fv(2, 1, rows=64)]
        ti = 0
        it = 0
        pi = 0
        import os
        STG = int(os.environ.get("KSWA_STAGE", "9"))
        rot = [self.banks[k] for k in (2, 3, 4, 5, 6)]
        rri = [0]

        def rb():
            bnk = rot[rri[0] % 5]
            rri[0] += 1
            return bnk
        psO = self.banks[0]
        psD = self.banks[1]
        for (c0, n, isctx) in self.tiles:
            x = ti % 2
            Qx = Q[x]
            p.dma("sp", Qx[:, :, :n], self.fm("AQ", 0, 512, c0, n))
            if not isctx:
                lo = max(CTX, c0 - 128)
                hi = min(T, c0 + n + 128)
                off = lo - (c0 - 128)
                for g in range(2):
                    p.dma("sp", Kw[x][g][:, off:off + hi - lo], V(self.s["AK"][g, :, lo:hi], p.dres("AK", g, lo, hi - lo)))
                p.dma("sp", Vw[x][:, off // 128:off // 128 + (hi - lo) // 128, :], self.tm("AV", lo, hi - lo, 0, 128))
                if NSEG > 1 and c0 == CTX:
                    for g in range(2):
                        p.copy("pool", Kw[x][g][:, 0:128], self.sw_KL[g])
                    p.copy("pool", Vw[x][:, 0, :], self.sw_VL)
                if NSEG > 1 and c0 + n == T:
                    for g in range(2):
                        p.copy("pool", Kw[x][g][:, n + 128:n + 256], self.sw_KR[g])
                    p.copy("pool", Vw[x][:, (n + 128) // 128, :], self.sw_VR)
            for qb in range(n // 128):
                qs = slice(qb * 128, (qb + 1) * 128)
                tpos = c0 + qb * 128
                if STG < 0:
                    continue
                for c in range(4):
                    p.tt("pool", q2[:, c * 128:(c + 1) * 128], Qx[:, c, qs], Qx[:, c, qs], ALU.mult)
                for g in range(2):
                    hl = [(par, hs, 4 * g + 2 * hs + par) for par in range(2) for hs in range(2)]
                    bkq = [rb(), rb()]
                    for (par, hs, hq) in hl:
                        r = slice(par * 64, par * 64 + 64)
                        ch = hq // 2
                        p.mm(bkq[par][:, hs * 128:(hs + 1) * 128], self.ones_bf[r, :], q2[r, ch * 128:(ch + 1) * 128])
                    mb = MB[it % 2]
                    if STG < 1:
                        continue
                    for par in range(2):
                        p.act(mb[:, par * 256:(par + 1) * 256], bkq[par][:, :256], AF.Ln)
                    p.act(mb, mb, AF.Exp, scale=0.5)
                    p.ts("dve", mb, mb, self.kmx[:, g:g + 1], None, ALU.mult)
                    gt = gate[it % 2]
                    for par in range(2):
                        p.dma("sp", gt[:, par * 256:(par + 1) * 256].re("p (h t) -> p h t", h=2),
                              V(self.s["GB"][g * 256:(g + 1) * 256, tpos:tpos + 128].rearrange("(hs par d) t -> par d hs t", par=2, d=64)[par],
                                p.dres("GB", 2 * g, tpos, 128) + p.dres("GB", 2 * g + 1, tpos, 128)))
                    kb = []
                    for b in range(2):
                        kb.append((Kc[g][:, b * 128:(b + 1) * 128], Vc[:, b, g * 64:(g + 1) * 64], None))
                    if not isctx:
                        for rel in (0, -1, 1):
                            kpos = tpos + rel * 128
                            halo = kpos < CTX or kpos >= T
                            if halo and NSEG == 1:
                                continue
                            wc = kpos - (c0 - 128)
                            m = None if rel == 0 else (self.masks[:, 2, :] if rel == -1 else self.masks[:, 3, :])
                            if halo:
                                m = self.sw_mL if rel == -1 else self.sw_mR
                            kb.append((Kw[x][g][:, wc:wc + 128], Vw[x][:, wc // 128, g * 64:(g + 1) * 64], m))
                    if STG < 2:
                        continue
                    for bi, (Kb, Vb, m) in enumerate(kb):
                        psS = [rb(), rb()]
                        for (par, hs, hq) in hl:
                            r = slice(par * 64, par * 64 + 64)
                            p.mm(psS[par][:, hs * 128:(hs + 1) * 128], Kb[r, :], Qx[r, hq // 2, qs])
                        t = tmp[pi % 3]
                        P = Pt[pi % 16]
                        pi += 1
                        for par in range(2):
                            cs_ = slice(par * 256, (par + 1) * 256)
                            p.tt("dve", t[:, cs_], psS[par][:, :256], mb[:, cs_], ALU.subtract)
                        p.act(P, t, AF.Exp)
                        if m is not None:
                            P3 = P.re("p (h t) -> p h t", h=4)
                            p.tt("dve", P3, P3, V(m.ap.unsqueeze(1).to_broadcast([128, 4, 128]), m.res), ALU.mult)
                        last = bi == len(kb) - 1
                        if STG < 3:
                            continue
                        p.mm(psO[0:64, :], Vb, P, start=(bi == 0), stop=last)
                        p.mm(psD[0:64, :], self.ones_bf[:, 0:64], P, start=(bi == 0), stop=last)
                    if STG < 4:
                        continue
                    for cb, (par, hs, hq) in enumerate(hl):
                        p.act(esk[:, cb * 128:(cb + 1) * 128], mb[0:64, cb * 128:(cb + 1) * 128], AF.Exp, scale=-1.0,
                              bias=self.sink[0:64, hq:hq + 1])
                    p.tt("dve", den, psD[0:64, :], esk, ALU.add)
                    p.recip(den, den)
                    p.tt("dve", o, psO[0:64, :], den, ALU.mult)
                    if STG < 5:
                        continue
                    yb = ybf[it % 2]
                    p.tt("pool", yb, o, gt, ALU.mult)
                    for par in range(2):
                        p.dma("pool", V(self.s["YT"][g * 256:(g + 1) * 256, tpos:tpos + 128].rearrange("(hs par d) t -> par d hs t", par=2, d=64)[par],
                                        p.dres("YT", 2 * g, tpos, 128) + p.dres("YT", 2 * g + 1, tpos, 128)),
                              yb[:, par * 256:(par + 1) * 256].re("p (h t) -> p h t", h=2))
                    it += 1
            ti += 1


def _pc(v, c):
    return np.ascontiguousarray(np.asarray(v, np.float32).reshape(c, 128).T)


def make_consts(SEQ):
    s = np.arange(128)[:, None]
    t = np.arange(128)[None, :]
    same = (s // 64) == (t // 64)
    masks = np.stack([(s <= t) & same, (s >= t) & same, s >= t, s <= t]).astype(np.float32)
    col = np.arange(512)
    rmask = np.stack([np.broadcast_to((col % 64 != 0), (128, 512)),
                      np.broadcast_to((col % 64 != 63), (128, 512))]).astype(np.float32)
    tok = np.arange(SEQ)
    row, cl = tok // 64, tok % 64
    inv = (10000.0 ** (-np.arange(16, dtype=np.float32) / 16)).astype(np.float32)
    ang = np.concatenate([row[:, None].astype(np.float32) * inv[None], cl[:, None].astype(np.float32) * inv[None]], axis=-1)
    cos, sin = np.cos(ang).astype(np.float32), np.sin(ang).astype(np.float32)
    C64 = np.concatenate([cos, cos], axis=1).T
    S64 = np.concatenate([-sin, sin], axis=1).T
    ropec = np.ascontiguousarray(np.concatenate([C64, C64], axis=0))
    ropes = np.ascontiguousarray(np.concatenate([S64, S64], axis=0))
    ident = np.eye(128, dtype=np.float32).astype(ml_dtypes.bfloat16)
    return dict(masks=masks, rmask=rmask, ropec=ropec, ropes=ropes, ident=ident)


def rot_cols(w, nheads, dh=64):
    K = w.shape[0]
    w4 = w.reshape(K, nheads, 2, dh // 2)
    return w4[:, :, ::-1, :].reshape(K, nheads * dh)


def prep_inputs(inp, b, SEQ):
    f = lambda a: np.ascontiguousarray(np.asarray(a, np.float32))
    m = {}
    m["xin"] = np.ascontiguousarray(np.concatenate([f(inp["ctx"][b]).T, f(inp["x"][b])[:SEQ].T], axis=1))
    m["cvec"] = np.ascontiguousarray(np.stack([_pc(inp["c"][b], 8), _pc(inp["c_ctx"], 8)], axis=-1))
    m["ada_w"] = f(inp["ada_w"])
    m["ada_b"] = np.stack([_pc(inp["ada_b"][l], 24) for l in range(4)])
    m["norm_g"] = np.stack([_pc(inp["norm_g"][l], 8) for l in range(4)])
    m["final_g"] = _pc(inp["final_g"], 8)
    m["e_w_in"] = f(inp["e_w_in"])
    m["e_w_out"] = f(inp["e_w_out"])
    ow = f(inp["o_w_in"])
    C = ODD_COLS
    rot = [np.concatenate([rot_cols(ow[j][:, C["q"][0]:C["q"][1]], 8), rot_cols(ow[j][:, C["k"][0]:C["k"][1]], 2),
                           rot_cols(ow[j][:, C["rq"][0]:C["rq"][1]], 4), rot_cols(ow[j][:, C["rk"][0]:C["rk"][1]], 4)], axis=1)
           for j in range(2)]
    m["o_w_in"] = np.ascontiguousarray(np.concatenate([ow, np.stack(rot)], axis=2))
    m["o_w_out"] = f(inp["o_w_out"])
    m["gla_up"] = np.ascontiguousarray(np.stack([f(inp["gla_up_fw"]), f(inp["gla_up_bw"])], axis=1))
    m["gla_b"] = np.stack([np.stack([_pc(inp["gla_b_fw"][j], 2), _pc(inp["gla_b_bw"][j], 2)]) for j in range(2)])
    m["gla_ng"] = np.stack([_pc(inp["gla_norm_g"][j], 1) for j in range(2)])
    m["ret_ng"] = np.stack([_pc(inp["ret_norm_g"][j], 1) for j in range(2)])
    cw = f(inp["lru_conv_w"])
    m["conv_w"] = np.ascontiguousarray(cw.reshape(2, 4, 4, 128).transpose(0, 3, 2, 1))
    m["conv_b"] = np.stack([_pc(inp["lru_conv_b"][j], 4) for j in range(2)])
    m["lru_w"] = np.ascontiguousarray(np.stack([f(inp["lru_wa_fw"]), f(inp["lru_wx_fw"]), f(inp["lru_wa_bw"]), f(inp["lru_wx_bw"])], axis=1))
    m["lru_b"] = np.stack([np.stack([_pc(inp[k][j], 4) for k in ("lru_ba_fw", "lru_bx_fw", "lru_ba_bw", "lru_bx_bw")]) for j in range(2)])
    m["lru_lam"] = np.stack([np.stack([_pc(inp["lru_lam_fw"][j], 4), _pc(inp["lru_lam_bw"][j], 4)]) for j in range(2)])
    m["sink"] = np.ascontiguousarray(np.broadcast_to(f(inp["swa_sink"])[:, None, :], (2, 128, 8)))
    rd = np.stack([f(inp["ret_dec_fw"]), f(inp["ret_dec_bw"])], axis=1)
    m["ret_dec"] = np.ascontiguousarray(np.repeat(rd, 64, axis=2).reshape(2, 2, 2, 128).transpose(0, 1, 3, 2))
    m.update(make_consts(SEQ))
    return m


_CACHE = {}


def seg_weights(s, nseg):
    w = np.zeros((4, 4), np.float32)
    if nseg > 1:
        if s > 0:
            w[0, s - 1] = 1.0
        if s < nseg - 1:
            w[1, s + 1] = 1.0
        w[2, :s] = 1.0
        w[3, s + 1:] = 1.0
    return np.ascontiguousarray(np.broadcast_to(w.reshape(1, 16), (128, 16)))


def make_in_maps(inp, SEQ_TOTAL, NSEG):
    maps = []
    L = SEQ_TOTAL // NSEG
    base = [prep_inputs(inp, b, SEQ_TOTAL) for b in range(2)]
    for b in range(2):
        for s in range(NSEG):
            m = dict(base[b])
            if NSEG > 1:
                xin = base[b]["xin"]
                m["xin"] = np.ascontiguousarray(np.concatenate([xin[:, :CTX], xin[:, CTX + s * L:CTX + (s + 1) * L]], axis=1))
                m["ropec"] = np.ascontiguousarray(base[b]["ropec"][:, s * L:(s + 1) * L])
                m["ropes"] = np.ascontiguousarray(base[b]["ropes"][:, s * L:(s + 1) * L])
            m["segw"] = seg_weights(s, NSEG)
            maps.append(m)
    return maps


def run(inp, SEQ_TOTAL, DEPTH, dbg=(), NSEG=4):
    key = (SEQ_TOTAL, DEPTH, tuple(dbg), NSEG)
    if key not in _CACHE:
        _CACHE[key] = Builder(SEQ_TOTAL // NSEG, DEPTH, dbg, NSEG).build()
    nc = _CACHE[key]
    in_maps = make_in_maps(inp, SEQ_TOTAL, NSEG)
    res = run_bass_kernel_spmd(nc, in_maps, core_ids=list(range(2 * NSEG)), trace=(_os.environ.get("KTRACE", "0") == "1"))
    return res


def gather_out(res, NSEG):
    outs = []
    for b in range(2):
        outs.append(np.concatenate([res.results[b * NSEG + s]["out"].T for s in range(NSEG)], axis=0))
    return np.stack(outs).astype(np.float32)


def kernel(**inputs):
    SEQ = inputs["x"].shape[1]
    res = run(inputs, SEQ, 4, NSEG=4)
    return gather_out(res, 4)
```
